# Optimizing a Trainium2 kernel written in Bass

```python
import jax, jax.numpy as jnp
from jax import lax
import numpy as np

D_MODEL = 1024
BATCH = 8
SEQ = 4096
DEPTH = 2

CHUNK = 64
ROPE_THETA = 500000.0
DN_ALPHA = (2 * DEPTH) ** 0.25
DN_BETA = (8 * DEPTH) ** -0.25
LN_EPS = 1e-5
RMS_EPS = 1e-6

A_HEADS = 8
A_HEAD_DIM = 64
A_WIDTH = A_HEADS * A_HEAD_DIM
A_DECAY_RANK = 64
A_ICLR_RANK = 64
A_GN_EPS = 64e-5

B_HEADS = 8
B_HEAD_DIM = 64
B_WIDTH = B_HEADS * B_HEAD_DIM
B_KV_RANK = 128
IDX_HEADS = 8
IDX_DIM = 32
TOPK_MAX = 256
Q_BLOCK = 128

C_HEADS = 4
C_KEY_DIM = 128
C_VAL_DIM = 256
C_KEY_WIDTH = C_HEADS * C_KEY_DIM
C_VAL_WIDTH = C_HEADS * C_VAL_DIM
C_GATE_RANK = 16
C_GATE_NORM = 16.0

A_SHIFT_SIZES = [A_WIDTH, A_WIDTH, A_WIDTH, A_DECAY_RANK, A_ICLR_RANK]
EVEN_REST_SIZES = [A_WIDTH, B_WIDTH, B_KV_RANK, IDX_HEADS * IDX_DIM, IDX_DIM, IDX_HEADS, B_WIDTH]
EVEN_SHIFT = sum(A_SHIFT_SIZES)
EVEN_IN = EVEN_SHIFT + sum(EVEN_REST_SIZES)
EVEN_OUT_IN = A_WIDTH + B_WIDTH
ODD_SIZES = [C_KEY_WIDTH, C_KEY_WIDTH, C_VAL_WIDTH, C_GATE_RANK, C_VAL_WIDTH]
ODD_IN = sum(ODD_SIZES)
N_EVEN = (DEPTH + 1) // 2
N_ODD = DEPTH // 2

kernel_name = 'hybrid_rwkv7_dsa_gla_deepnorm'

F32 = jnp.float32


def split_cols(p, sizes):
    idx = [int(i) for i in np.cumsum(sizes)[:-1]]
    return jnp.split(p, idx, axis=-1)


def layer_norm(x, g, b):
    xf = x.astype(F32)
    mu = jnp.mean(xf, -1, keepdims=True)
    var = jnp.mean(jnp.square(xf - mu), -1, keepdims=True)
    return ((xf - mu) * lax.rsqrt(var + LN_EPS) * g + b).astype(x.dtype)


def rms_norm(x, g):
    xf = x.astype(F32)
    return xf * lax.rsqrt(jnp.mean(jnp.square(xf), -1, keepdims=True) + RMS_EPS) * g


def token_shift(p):
    return jnp.pad(p, ((0, 0), (1, 0), (0, 0)))[:, :-1, :]


def partial_rope(x, pos):
    hd = x.shape[-1]
    rd = hd // 4
    half = rd // 2
    inv = jnp.power(ROPE_THETA, -jnp.arange(half, dtype=F32) * 2.0 / rd)
    ang = pos.astype(F32)[..., None] * inv
    ang = ang.reshape(ang.shape[:2] + (1,) * (x.ndim - 3) + (half,))
    cos, sin = jnp.cos(ang), jnp.sin(ang)
    xf = x.astype(F32)
    x1, x2, xp = xf[..., :half], xf[..., half:rd], xf[..., rd:]
    return jnp.concatenate([x1 * cos - x2 * sin, x2 * cos + x1 * sin, xp], -1)


def rwkv7_mix(r, k, v, wd, ad, w0, w2, a0, a2, kk_scale, ka, rk, gn_g, gn_b):
    Bn, T, _ = r.shape
    H, Dh = A_HEADS, A_HEAD_DIM
    r, k, v, wd, ad = (t.astype(F32) for t in (r, k, v, wd, ad))
    w_log = -jax.nn.softplus(-(w0 + jnp.tanh(wd) @ w2)) - 0.5
    decay = jnp.exp(-jnp.exp(w_log))
    a = jax.nn.sigmoid(a0 + ad @ a2)
    kk = (k * kk_scale).reshape(Bn, T, H, Dh)
    kk = kk / jnp.maximum(jnp.sqrt(jnp.sum(jnp.square(kk), -1, keepdims=True)), 1e-12)
    k = k * (1.0 + (a - 1.0) * ka)
    hd = lambda t: t.reshape(Bn, T, H, Dh)
    r_h, k_h, v_h, w_h, a_h = hd(r), hd(k), hd(v), hd(decay), hd(a)
    b_h = kk * a_h

    def step(S, inp):
        r_t, w_t, k_t, v_t, kk_t, b_t = inp
        sa = jnp.einsum('bhvk,bhk->bhv', S, -kk_t)
        S = S * w_t[:, :, None, :] + sa[..., None] * b_t[:, :, None, :] + v_t[..., None] * k_t[:, :, None, :]
        return S, jnp.einsum('bhvk,bhk->bhv', S, r_t)

    xs = tuple(jnp.moveaxis(t, 1, 0) for t in (r_h, w_h, k_h, v_h, kk, b_h))
    _, y = lax.scan(step, jnp.zeros((Bn, H, Dh, Dh), F32), xs)
    y = jnp.moveaxis(y, 0, 1)
    mu = jnp.mean(y, -1, keepdims=True)
    var = jnp.mean(jnp.square(y - mu), -1, keepdims=True)
    y = (y - mu) * lax.rsqrt(var + A_GN_EPS) * gn_g.reshape(H, Dh) + gn_b.reshape(H, Dh)
    bonus = jnp.sum(r_h * k_h * rk, -1, keepdims=True) * v_h
    return (y + bonus).reshape(Bn, T, A_WIDTH)


def dsa_mix(q, ckv, qi, ki, wi, pos, kvn_g, wuk, wuv):
    Bn, T, _ = q.shape
    c = rms_norm(ckv, kvn_g)
    k = partial_rope(c @ wuk, pos)
    v = c @ wuv
    q = partial_rope(q.reshape(Bn, T, B_HEADS, B_HEAD_DIM), pos)
    qi = partial_rope(qi.reshape(Bn, T, IDX_HEADS, IDX_DIM), pos)
    ki = partial_rope(ki, pos)
    wi = wi.astype(F32) * (IDX_HEADS ** -0.5 * IDX_DIM ** -0.5)
    top_k = min(TOPK_MAX, T // 4)
    nblk = T // Q_BLOCK
    key_chunk = jnp.arange(T) // CHUNK

    def to_blocks(t):
        return jnp.moveaxis(t.reshape((Bn, nblk, Q_BLOCK) + t.shape[2:]), 1, 0)

    def block(args):
        qb, qib, wib, s0 = args
        q_chunk = (s0 + jnp.arange(Q_BLOCK)) // CHUNK
        adm = key_chunk[None, :] <= q_chunk[:, None]
        score = jnp.einsum('bqh,bqhs->bqs', wib, jax.nn.relu(jnp.einsum('bqhd,bsd->bqhs', qib, ki)))
        score = jnp.where(adm[None], score, -jnp.inf)
        vals, idx = lax.top_k(score, top_k)
        valid = vals > -jnp.inf
        k_sel = jax.vmap(lambda kb, ib: kb[ib])(k, idx)
        v_sel = jax.vmap(lambda vb, ib: vb[ib])(v, idx)
        s = jnp.einsum('bqhd,bqkd->bqhk', qb, k_sel) * (B_HEAD_DIM ** -0.5)
        s = jnp.where(valid[:, :, None, :], s, -1e30)
        p = jax.nn.softmax(s, axis=-1)
        return jnp.einsum('bqhk,bqkd->bqhd', p, v_sel)

    starts = jnp.arange(nblk, dtype=jnp.int32) * Q_BLOCK
    out = lax.map(block, (to_blocks(q), to_blocks(qi), to_blocks(wi), starts))
    return jnp.moveaxis(out, 0, 1).reshape(Bn, T, B_WIDTH)


def gla_mix(q, k, v, gd, g2, g_bias, norm_g):
    Bn, T, _ = q.shape
    H, Dk, Dv = C_HEADS, C_KEY_DIM, C_VAL_DIM
    n = T // CHUNK
    log_a = jax.nn.log_sigmoid((gd @ g2 + g_bias).astype(F32)) / C_GATE_NORM
    chunks = lambda t, d: t.astype(F32).reshape(Bn, n, CHUNK, H, d)
    qc = chunks(q, Dk) * (Dk ** -0.5)
    kc, vc, gc = chunks(k, Dk), chunks(v, Dv), chunks(log_a, Dk)
    b = jnp.cumsum(gc, axis=2)
    q_e = qc * jnp.exp(b)
    k_e = kc * jnp.exp(-b)
    causal = jnp.tril(jnp.ones((CHUNK, CHUNK), bool))
    att = jnp.where(causal, jnp.einsum('bnthd,bnshd->bnhts', q_e, k_e), 0.0)
    o_intra = jnp.einsum('bnhts,bnshd->bnthd', att, vc)
    b_last = b[:, :, -1]
    chunk_kv = jnp.einsum('bnshk,bnshv->bnhkv', kc * jnp.exp(b_last[:, :, None] - b), vc)

    def step(S, inp):
        dec, kv = inp
        return S * dec[..., None] + kv, S

    _, S_prev = lax.scan(step, jnp.zeros((Bn, H, Dk, Dv), F32),
                         (jnp.moveaxis(jnp.exp(b_last), 1, 0), jnp.moveaxis(chunk_kv, 1, 0)))
    S_prev = jnp.moveaxis(S_prev, 0, 1)
    o = o_intra + jnp.einsum('bnthk,bnhkv->bnthv', q_e, S_prev)
    return rms_norm(o, norm_g).reshape(Bn, T, C_VAL_WIDTH)


def setup_inputs(seed: int = 0) -> dict:
    key = jax.random.key(seed)
    ks = jax.random.split(key, 24)
    nrm = lambda k, s, sc: jax.random.normal(k, s, F32) * sc
    offs = jax.random.randint(ks[1], (BATCH, 1), 0, 64, dtype=jnp.int32) * CHUNK
    positions = (offs + jnp.arange(SEQ, dtype=jnp.int32)[None, :]).astype(jnp.int32)
    return {
        'x': jax.random.normal(ks[0], (BATCH, SEQ, D_MODEL), F32),
        'positions': positions,
        'e_w_in': nrm(ks[2], (N_EVEN, D_MODEL, EVEN_IN), D_MODEL ** -0.5),
        'a_mu': jax.random.uniform(ks[3], (N_EVEN, EVEN_SHIFT), F32),
        'a_w0': -1.0 + nrm(ks[4], (N_EVEN, A_WIDTH), 0.5),
        'a_w2': nrm(ks[5], (N_EVEN, A_DECAY_RANK, A_WIDTH), 0.5 * A_DECAY_RANK ** -0.5),
        'a_a0': nrm(ks[6], (N_EVEN, A_WIDTH), 0.1),
        'a_a2': nrm(ks[7], (N_EVEN, A_ICLR_RANK, A_WIDTH), 0.5 * A_ICLR_RANK ** -0.5),
        'a_kk_scale': 0.85 + nrm(ks[8], (N_EVEN, A_WIDTH), 0.02),
        'a_ka': 1.0 + nrm(ks[9], (N_EVEN, A_WIDTH), 0.02),
        'a_rk': nrm(ks[10], (N_EVEN, A_HEADS, A_HEAD_DIM), 0.1),
        'a_gn_g': 1.0 + nrm(ks[11], (N_EVEN, A_WIDTH), 0.02),
        'a_gn_b': nrm(ks[12], (N_EVEN, A_WIDTH), 0.02),
        'b_kv_norm_g': 1.0 + nrm(ks[13], (N_EVEN, B_KV_RANK), 0.02),
        'b_wuk': nrm(ks[14], (N_EVEN, B_KV_RANK, B_HEAD_DIM), B_KV_RANK ** -0.5),
        'b_wuv': nrm(ks[15], (N_EVEN, B_KV_RANK, B_HEAD_DIM), B_KV_RANK ** -0.5),
        'e_w_out': nrm(ks[16], (N_EVEN, EVEN_OUT_IN, D_MODEL), DN_BETA * EVEN_OUT_IN ** -0.5),
        'o_w_in': nrm(ks[17], (N_ODD, D_MODEL, ODD_IN), D_MODEL ** -0.5),
        'c_g2': nrm(ks[18], (N_ODD, C_GATE_RANK, C_KEY_WIDTH), C_GATE_RANK ** -0.5),
        'c_g_bias': 1.0 + nrm(ks[19], (N_ODD, C_KEY_WIDTH), 0.1),
        'c_norm_g': 1.0 + nrm(ks[20], (N_ODD, C_VAL_DIM), 0.02),
        'o_w_out': nrm(ks[21], (N_ODD, C_VAL_WIDTH, D_MODEL), DN_BETA * C_VAL_WIDTH ** -0.5),
        'ln_g': 1.0 + nrm(ks[22], (DEPTH, D_MODEL), 0.02),
        'ln_b': nrm(ks[23], (DEPTH, D_MODEL), 0.02),
    }


def reference(x, positions, e_w_in, a_mu, a_w0, a_w2, a_a0, a_a2, a_kk_scale, a_ka, a_rk, a_gn_g, a_gn_b,
              b_kv_norm_g, b_wuk, b_wuv, e_w_out, o_w_in, c_g2, c_g_bias, c_norm_g, o_w_out, ln_g, ln_b):
    h = x
    for i in range(DEPTH):
        j = i // 2
        if i % 2 == 0:
            p = h @ e_w_in[j]
            ps, pr = p[..., :EVEN_SHIFT], p[..., EVEN_SHIFT:]
            ps = ps + (token_shift(ps) - ps) * a_mu[j]
            r, k, v, wd, ad = split_cols(ps, A_SHIFT_SIZES)
            g_a, q_b, ckv, qi, ki, wi, g_b = split_cols(pr, EVEN_REST_SIZES)
            ya = rwkv7_mix(r, k, v, wd, ad, a_w0[j], a_w2[j], a_a0[j], a_a2[j], a_kk_scale[j], a_ka[j],
                           a_rk[j], a_gn_g[j], a_gn_b[j]) * jax.nn.silu(g_a.astype(F32))
            yb = dsa_mix(q_b, ckv, qi, ki, wi, positions, b_kv_norm_g[j], b_wuk[j], b_wuv[j]) * jax.nn.silu(g_b.astype(F32))
            y = jnp.concatenate([ya, yb], -1).astype(h.dtype) @ e_w_out[j]
        else:
            p = h @ o_w_in[j]
            q_c, k_c, v_c, gd, g_c = split_cols(p, ODD_SIZES)
            yc = gla_mix(q_c, k_c, v_c, gd, c_g2[j], c_g_bias[j], c_norm_g[j]) * jax.nn.silu(g_c.astype(F32))
            y = yc.astype(h.dtype) @ o_w_out[j]
        h = layer_norm(DN_ALPHA * h + y.astype(h.dtype), ln_g[i], ln_b[i])
    return h
```

```python
import numpy as np
import concourse.bass as bass
import concourse.mybir as mybir
from contextlib import ExitStack

F32 = mybir.dt.float32
BF16 = mybir.dt.bfloat16
I32 = mybir.dt.int32
AF = mybir.ActivationFunctionType
ALU = mybir.AluOpType
AX = mybir.AxisListType

SEM_WRAP = 30000


class Buf:
    def __init__(self, t, name):
        self.t = t
        self.name = name
        self.W = None
        self.R = []
        self.keys = {}

    def __getitem__(self, idx):
        return View(self, None, self.t[idx])

    def k(self, key):
        return _Keyed(self, key)


class _Keyed:
    def __init__(self, buf, key):
        self.buf = buf
        self.key = key

    def __getitem__(self, idx):
        return View(self.buf, self.key, self.buf.t[idx])


class View:
    def __init__(self, buf, key, ap):
        self.buf = buf
        self.key = key
        self.ap = ap

    def __getitem__(self, idx):
        return View(self.buf, self.key, self.ap[idx])

    def re(self, s, **kw):
        return View(self.buf, self.key, self.ap.rearrange(s, **kw))

    def bc(self, shape):
        return View(self.buf, self.key, self.ap.to_broadcast(shape))

    def bitcast(self, dt):
        return View(self.buf, self.key, self.ap.bitcast(dt))


class Op:
    __slots__ = ("eng", "fn", "deps", "idx", "sig", "count", "chan", "chcount", "waits", "tag")


class Chan:
    def __init__(self, sem):
        self.sem = sem
        self.n = 0
        self.last = None


class Prog:
    ENGS = ("pe", "act", "dve", "pool", "sp")

    def __init__(self, nc, stack, prefix=""):
        self.prefix = prefix
        self.nc = nc
        self.stack = stack
        self.ops = []
        self.cur = self.ops
        self.bufs = []
        self.nsb = 0
        self.chans = {}

    def sb(self, name, shape, dt):
        t = self.stack.enter_context(self.nc.sbuf_tensor(self.prefix + name, list(shape), dt))
        b = Buf(t, name)
        self.bufs.append(b)
        return b

    def ps(self, name, shape, dt):
        t = self.stack.enter_context(self.nc.psum_tensor(self.prefix + name, list(shape), dt))
        b = Buf(t, name)
        b.psum = True
        self.bufs.append(b)
        return b

    def alias(self, buf, t, name):
        raise NotImplementedError

    def chan(self, name):
        if name not in self.chans:
            sem = self.stack.enter_context(self.nc.semaphore(self.prefix + "c_" + name))
            self.chans[name] = Chan(sem)
        return self.chans[name]

    def _track(self, idx, reads, writes, st=None):
        if st is not None:
            return self._track_st(idx, reads, writes, st)
        deps = set()
        for v in reads:
            b, k = v.buf, v.key
            if b.W is not None:
                deps.add(b.W)
            if k is None:
                for kk, (w, r) in b.keys.items():
                    if w is not None:
                        deps.add(w)
            else:
                if k in b.keys and b.keys[k][0] is not None:
                    deps.add(b.keys[k][0])
        for v in writes:
            b, k = v.buf, v.key
            if b.W is not None:
                deps.add(b.W)
            deps.update(b.R)
            if k is None:
                for kk, (w, r) in b.keys.items():
                    if w is not None:
                        deps.add(w)
                    deps.update(r)
            else:
                if k in b.keys:
                    w, r = b.keys[k]
                    if w is not None:
                        deps.add(w)
                    deps.update(r)
        for v in reads:
            b, k = v.buf, v.key
            if k is None:
                b.R.append(idx)
            else:
                b.keys.setdefault(k, [None, []])[1].append(idx)
        for v in writes:
            b, k = v.buf, v.key
            if k is None:
                b.W = idx
                b.R = []
                b.keys = {}
            else:
                b.keys[k] = [idx, []]
        deps.discard(idx)
        return deps

    def _track_st(self, idx, reads, writes, st):
        class _R:
            __slots__ = ("W", "R", "keys")
        deps = set()

        def rec(b):
            r = st.get(id(b))
            if r is None:
                r = _R()
                r.W = None
                r.R = []
                r.keys = {}
                st[id(b)] = r
            return r
        for v in reads:
            b, k = rec(v.buf), v.key
            if b.W is not None:
                deps.add(b.W)
            if k is None:
                for kk, (w, r) in b.keys.items():
                    if w is not None:
                        deps.add(w)
            elif k in b.keys and b.keys[k][0] is not None:
                deps.add(b.keys[k][0])
        for v in writes:
            b, k = rec(v.buf), v.key
            if b.W is not None:
                deps.add(b.W)
            deps.update(b.R)
            if k is None:
                for kk, (w, r) in b.keys.items():
                    if w is not None:
                        deps.add(w)
                    deps.update(r)
            elif k in b.keys:
                w, r = b.keys[k]
                if w is not None:
                    deps.add(w)
                deps.update(r)
        for v in reads:
            b, k = rec(v.buf), v.key
            if k is None:
                b.R.append(idx)
            else:
                b.keys.setdefault(k, [None, []])[1].append(idx)
        for v in writes:
            b, k = rec(v.buf), v.key
            if k is None:
                b.W = idx
                b.R = []
                b.keys = {}
            else:
                b.keys[k] = [idx, []]
        deps.discard(idx)
        return deps

    @staticmethod
    def _cost(op):
        reads, writes, chan, extra = op.deps
        n = 1
        v = writes[0] if writes else (reads[0] if reads else None)
        if v is not None:
            try:
                shp = list(v.ap.shape)
                n = 1
                for d in shp[1:]:
                    n *= int(d)
            except Exception:
                n = 64
        if chan is not None:
            return 0.05, 2.5
        e = op.eng
        if e == "pe":
            f32 = False
            try:
                f32 = (reads[0].ap.dtype == F32)
            except Exception:
                pass
            d = 0.04 + n / 2400.0 * (4.0 if f32 else 1.0)
        elif e == "act":
            d = 0.2 + n / 1200.0
        elif e == "dve":
            d = 0.08 + n / 960.0
        elif e == "pool":
            d = 0.15 + n / 350.0
        else:
            d = 0.05
        return d, d

    prio_delta = 0.0
    hop_same = 0.15
    hop_x = 0.7

    def merge_sched(self, a, b):
        streams = [a, b]
        sts = [{}, {}]
        fin = [{}, {}]
        ptr = [0, 0]
        efree = getattr(self, "_efree", None)
        if efree is None:
            efree = {e: 0.0 for e in self.ENGS}
        base = max(efree.values()) if False else min(efree.values())
        cache = [None, None]

        def est(si):
            if cache[si] is not None:
                return cache[si]
            i = ptr[si]
            op = streams[si][i]
            reads, writes, chan, extra = op.deps
            deps = self._track_st(i, reads, writes, sts[si]) if False else None
            return None
        ldeps = []
        for si in range(2):
            dl = []
            stt_ = {}
            for i, op in enumerate(streams[si]):
                reads, writes, chan, extra = op.deps
                dl.append(self._track_st(i, reads, writes, stt_))
            ldeps.append(dl)
        start_of = [None, None]

        def earliest(si):
            i = ptr[si]
            op = streams[si][i]
            t = efree[op.eng]
            for d in ldeps[si][i]:
                pe = streams[si][d].eng
                hop = 0.0 if (pe == "pe" and op.eng == "pe") else (self.hop_same if pe == op.eng else self.hop_x)
                t = max(t, fin[si][d] + hop)
            return t
        while ptr[0] < len(a) or ptr[1] < len(b):
            cands = [si for si in range(2) if ptr[si] < len(streams[si])]
            if len(cands) == 2:
                e0, e1 = earliest(0), earliest(1)
                si = 1 if e1 <= e0 + self.prio_delta else 0
                t = e1 if si == 1 else e0
            else:
                si = cands[0]
                t = earliest(si)
            i = ptr[si]
            op = streams[si][i]
            occ, lat = self._cost(op)
            efree[op.eng] = t + occ
            fin[si][i] = t + lat
            self.cur.append(op)
            ptr[si] += 1
        self._efree = efree

    def add(self, eng, fn, reads=(), writes=(), chan=None, extra_deps=(), tag=None):
        op = Op()
        op.eng = eng
        op.fn = fn
        op.idx = None
        op.sig = False
        op.count = None
        op.chan = None
        op.chcount = None
        op.waits = []
        op.tag = tag
        reads = [v for v in reads if v is not None]
        writes = [v for v in writes if v is not None]
        writes = writes + [v for v in reads if getattr(v.buf, "psum", False)]
        reads = [v for v in reads if not getattr(v.buf, "psum", False)]
        op.deps = (reads, writes, chan, [d for d in extra_deps if d is not None])
        self.cur.append(op)
        return op

    def begin_stream(self):
        lst = []
        self._saved = getattr(self, "_saved", [])
        self._saved.append(self.cur)
        self.cur = lst
        return lst

    def end_stream(self):
        self.cur = self._saved.pop()

    def merge(self, a, b):
        na, nb = len(a), len(b)
        ia = ib = 0
        while ia < na or ib < nb:
            if ib >= nb or (ia < na and ia * nb <= ib * na):
                self.cur.append(a[ia])
                ia += 1
            else:
                self.cur.append(b[ib])
                ib += 1

    def resolve(self):
        for i, op in enumerate(self.ops):
            op.idx = i
        for op in self.ops:
            reads, writes, chan, extra = op.deps
            deps = self._track(op.idx, reads, writes)
            deps.update(d.idx for d in extra)
            if chan is not None:
                ch = self.chan(chan)
                op.chan = ch
                ch.n += 1
                op.chcount = ch.n
                if ch.last is not None:
                    deps.add(ch.last)
                ch.last = op.idx
            deps.discard(op.idx)
            assert all(d < op.idx for d in deps), "dependency on a later op"
            op.deps = deps

    def mm(self, out, lhsT, rhs, start=True, stop=True):
        o, l, r = out.ap, lhsT.ap, rhs.ap
        return self.add("pe", lambda e: e.matmul(o, l, r, start=start, stop=stop),
                        reads=[lhsT, rhs], writes=[out])

    def tr(self, out, in_, ident):
        o, i, d = out.ap, in_.ap, ident.ap
        return self.add("pe", lambda e: e.transpose(o, i, d), reads=[in_, ident], writes=[out])

    def act(self, out, in_, func, bias=None, scale=1.0, accum=None, eng="act"):
        o, i = out.ap, in_.ap
        reads = [in_]
        kw = {}
        if bias is not None:
            if isinstance(bias, View):
                reads.append(bias)
                kw["bias"] = bias.ap
            else:
                kw["bias"] = float(bias)
        if isinstance(scale, View):
            reads.append(scale)
            kw["scale"] = scale.ap
        else:
            kw["scale"] = float(scale)
        writes = [out]
        if accum is not None:
            writes.append(accum)
            kw["accum_out"] = accum.ap
        return self.add("act", lambda e: e.activation(o, i, func, **kw), reads=reads, writes=writes)

    def tt(self, out, a, b, op, eng="dve"):
        o, x, y = out.ap, a.ap, b.ap
        return self.add(eng, lambda e: e.tensor_tensor(o, x, y, op), reads=[a, b], writes=[out])

    def ts(self, out, a, s1, op0, s2=None, op1=None, accum=None, eng="dve"):
        o, x = out.ap, a.ap
        reads = [a]
        if isinstance(s1, View):
            reads.append(s1)
            s1 = s1.ap
        if isinstance(s2, View):
            reads.append(s2)
            s2 = s2.ap
        kw = {}
        writes = [out]
        if op1 is not None:
            kw["op1"] = op1
        if accum is not None:
            kw["accum_out"] = accum.ap
            writes.append(accum)
        return self.add(eng, lambda e: e.tensor_scalar(o, x, s1, s2, op0, **kw), reads=reads, writes=writes)

    def stt(self, out, a, s, b, op0, op1, eng="dve"):
        o, x, y = out.ap, a.ap, b.ap
        reads = [a, b]
        if isinstance(s, View):
            reads.append(s)
            s = s.ap
        return self.add("dve", lambda e: e.scalar_tensor_tensor(o, x, s, y, op0, op1), reads=reads, writes=[out])

    def copy(self, out, in_, eng="dve"):
        o, i = out.ap, in_.ap
        if eng == "act":
            return self.add("act", lambda e: e.copy(o, i), reads=[in_], writes=[out])
        return self.add(eng, lambda e: e.tensor_copy(o, i), reads=[in_], writes=[out])

    def memset(self, out, val, eng="dve"):
        o = out.ap
        return self.add(eng, lambda e: e.memset(o, val), writes=[out])

    def reduce(self, out, in_, op, axis=AX.X, eng="dve"):
        o, i = out.ap, in_.ap
        return self.add(eng, lambda e: e.tensor_reduce(o, i, axis, op), reads=[in_], writes=[out])

    def recip(self, out, in_):
        o, i = out.ap, in_.ap
        return self.add("dve", lambda e: e.reciprocal(o, i), reads=[in_], writes=[out])

    def dma(self, out, in_, chan, q="sp", reads=(), writes=()):
        o = out.ap if isinstance(out, View) else out
        i = in_.ap if isinstance(in_, View) else in_
        rd = list(reads) + ([in_] if isinstance(in_, View) else [])
        wr = list(writes) + ([out] if isinstance(out, View) else [])
        return self.add(q, lambda e: e.dma_start(out=o, in_=i), reads=rd, writes=wr, chan=chan)

    def emit(self, final_wait_ops):
        nc = self.nc
        self.resolve()
        ops = self.ops
        fin = Op()
        fin.eng = "sp"
        fin.fn = None
        fin.idx = len(ops)
        fin.sig = False
        fin.count = None
        fin.chan = None
        fin.chcount = None
        fin.waits = []
        fin.deps = set(o.idx for o in final_wait_ops)
        fin.tag = "final"
        ops.append(fin)
        for op in ops:
            for d in op.deps:
                p = ops[d]
                if p.chan is None:
                    if p.eng == "pe" and op.eng == "pe":
                        continue
                    p.sig = True
        cnt = {e: 0 for e in self.ENGS}
        for op in ops:
            if op.sig:
                cnt[op.eng] += 1
                op.count = cnt[op.eng]
        nsem = {e: (cnt[e] // SEM_WRAP) + 1 for e in self.ENGS}
        sems = {e: [self.stack.enter_context(nc.semaphore(self.prefix + "s_%s_%d" % (e, j))) for j in range(nsem[e])]
                for e in self.ENGS}
        waited = {e: {} for e in self.ENGS}
        for op in ops:
            need = {}
            for d in op.deps:
                p = ops[d]
                if p.chan is not None:
                    key = ("c", id(p.chan))
                    val = p.chcount
                    obj = p.chan
                else:
                    if p.eng == "pe" and op.eng == "pe":
                        continue
                    key = ("e", p.eng)
                    val = p.count
                    obj = p.eng
                if key not in need or need[key][0] < val:
                    need[key] = (val, obj)
            w = waited[op.eng]
            for key, (val, obj) in need.items():
                if w.get(key, 0) >= val:
                    continue
                w[key] = val
                if key[0] == "c":
                    op.waits.append((obj.sem, 16 * val))
                else:
                    j, v = divmod(val - 1, SEM_WRAP)
                    op.waits.append((sems[obj][j], v + 1))
        by_eng = {e: [op for op in ops if op.eng == e] for e in self.ENGS}
        self.stats = {e: len(by_eng[e]) for e in self.ENGS}
        self.stats["waits"] = sum(len(op.waits) for op in ops)

        def run(eng_obj, lst, ename):
            for op in lst:
                for (sem, val) in op.waits:
                    eng_obj.wait_ge(sem, val)
                if op.fn is None:
                    continue
                ins = op.fn(eng_obj)
                if op.chan is not None:
                    ins.then_inc(op.chan.sem, 16)
                elif op.sig:
                    j = (op.count - 1) // SEM_WRAP
                    ins.then_inc(sems[ename][j], 1)

        with nc.Block() as block:
            @block.tensor
            def _(e):
                run(e, by_eng["pe"], "pe")

            @block.scalar
            def _(e):
                run(e, by_eng["act"], "act")

            @block.vector
            def _(e):
                run(e, by_eng["dve"], "dve")

            @block.gpsimd
            def _(e):
                run(e, by_eng["pool"], "pool")

            @block.sync
            def _(e):
                run(e, by_eng["sp"], "sp")

from concourse.bass_utils import run_bass_kernel_spmd

D = 1024
EVEN_IN = 3624
ODD_IN = 3088
NEG_BIG = -1.0e30
NEG_SEL = -3.0e38
ALPHA = 4.0 ** 0.25
TWO_PI = 2.0 * np.pi

C_ID = 0
C_TRIU = 128
C_TRIL = 384
C_ONES = 512
C_INV = 640
C_POW = 656
NIT = 28
C_NH = 688
NCONST = 656 + 32 + 8

BC0 = {}
_o = 0
for _n, _l in (("mu", 1664), ("w0", 512), ("a0", 512), ("kks", 512), ("ka", 512), ("rk", 512),
               ("gng", 512), ("gnb", 512), ("kvg", 128)):
    BC0[_n] = (_o, _l)
    _o += _l
NBC0 = _o


def make_consts():
    c = np.zeros((128, NCONST), np.float32)
    c[:, C_ID:C_ID + 128] = np.eye(128, dtype=np.float32)
    s = np.arange(128)[:, None]
    t = np.arange(128)[None, :]
    c[:, C_TRIU:C_TRIU + 128] = (s < t)
    c[:, C_TRIU + 128:C_TRIU + 256] = (s <= t)
    c[:, C_TRIL:C_TRIL + 128] = (s > t)
    c[:, C_ONES:C_ONES + 128] = 1.0
    inv16 = np.power(500000.0, -np.arange(8, dtype=np.float32) * 2.0 / 16).astype(np.float32)
    inv8 = np.power(500000.0, -np.arange(4, dtype=np.float32) * 2.0 / 8).astype(np.float32)
    c[:, C_INV:C_INV + 8] = (inv16.astype(np.float64) / TWO_PI).astype(np.float32)[None, :]
    c[:, C_INV + 8:C_INV + 12] = (inv8.astype(np.float64) / TWO_PI).astype(np.float32)[None, :]
    c[:, C_NH:C_NH + 8] = -0.5
    c[:, C_POW:C_POW + NIT] = (2.0 ** -(np.arange(NIT) + 1.0)).astype(np.float32)[None, :]
    return c


import os as _os
_SKIP = set(_os.environ.get("KSKIP", "").split(","))


class K:
    def __init__(self, T, topk, dbg=(), phases=("a", "b"), stop=99):
        self.stop = stop
        self.T = T
        self.NT = T // 128
        self.topk = topk
        self.dbg = set(dbg)
        self.phases = phases
        self.dbg_outs = {}

    def dram_in(self, name, shape, dt=F32):
        return self.nc.dram_tensor(name, list(shape), dt, kind="ExternalInput").ap()

    def build(self):
        nc = bass.Bass("TRN2", target_bir_lowering=False)
        self.nc = nc
        T, NT = self.T, self.NT
        self.x = self.dram_in("x", [T, D])
        self.pos = self.dram_in("pos", [128, NT], I32)
        self.consts = self.dram_in("consts", [128, NCONST])
        self.w_in0 = self.dram_in("w_in0", [D, EVEN_IN])
        self.w_out0 = self.dram_in("w_out0", [D, D])
        self.bcv0 = self.dram_in("bcv0", [1, NBC0])
        self.w2a2 = self.dram_in("w2a2", [64, 1024])
        self.wukv = self.dram_in("wukv", [128, 128])
        self.w_in1 = self.dram_in("w_in1", [D, ODD_IN])
        self.w_out1 = self.dram_in("w_out1", [D, D])
        self.bcv1 = self.dram_in("bcv1", [1, NBC1])
        self.g2 = self.dram_in("g2", [16, 512])
        if "b" in self.phases:
            self.out = nc.dram_tensor("out", [T, D], F32, kind="ExternalOutput").ap()
        if self.phases == ("a", "b"):
            self.h1 = nc.dram_tensor("h1", [T, D], F32).ap()
        elif self.phases == ("a",):
            self.h1 = nc.dram_tensor("h1", [T, D], F32, kind="ExternalOutput").ap()
        else:
            self.h1 = self.dram_in("h1", [T, D])
        self.w1_d = None
        if self.phases == ("a", "b"):
            self.w1_d = (nc.dram_tensor("wbf1_d", [D, ODD_IN], BF16).ap(), nc.dram_tensor("wob1_d", [D, D], BF16).ap())
        first = True
        for ph in self.phases:
            if ("pa" in _SKIP and ph == "a"):
                continue
            if not first:
                nc.all_engine_barrier()
            first = False
            with ExitStack() as st:
                P = Prog(nc, st, prefix=ph + "_")
                self.P = P
                self.finals = []
                self.setup_common(ph)
                if ph == "a":
                    self.phase_a()
                else:
                    self.phase_b()
                P.emit(self.finals)
                self.stats = getattr(self, "stats", {})
                self.stats[ph] = P.stats
        return nc

    def dump(self, name, view, shape):
        if name not in self.dbg:
            return
        t = self.nc.dram_tensor("dbg_" + name, list(shape), view.ap.dtype if hasattr(view.ap, "dtype") else F32,
                                kind="ExternalOutput").ap()
        self.dbg_outs[name] = "dbg_" + name
        i = self.P.dma(t, view, chan="dbg_" + name, q="pool")
        self.finals.append(i)

    def setup_common(self, ph="a"):
        P = self.P
        self.cst = P.sb("cst", [128, NCONST], F32)
        P.dma(self.cst[:, :], self.consts, chan="cst")
        self.cstb = P.sb("cstb", [128, 640], BF16)
        P.copy(self.cstb[:, :], self.cst[:, 0:640])
        self.B0 = P.ps("B0", [128, 512], F32)
        self.B1 = P.ps("B1", [128, 512], F32)
        self.B2 = P.ps("B2", [128, 1024], BF16)
        if ph == "a":
            self.B3 = self.B1
            self.B2d = P.ps("B2d", [128, 1024], BF16)
        else:
            self.B3 = P.ps("B3", [128, 512], F32)
        if ph == "a":
            self.BI = P.ps("BI", [128, 512], F32)
            self.BS = P.ps("BS", [128, 512], F32)
            self.BA = P.ps("BA", [128, 1024], F32)
        else:
            self.B56 = P.ps("B56", [128, 1024], F32)
            self.B78 = P.ps("B78", [128, 1024], F32)
        self.rot = 0

    def rbank(self):
        self.rot ^= 1
        return self.B0 if self.rot else self.B1

    def ident(self, n=128, bf=False):
        return (self.cstb if bf else self.cst)[0:n, C_ID:C_ID + n]

    def layernorm_out(self, res_ps_a, res_ps_b, xt, lng, lnb, dst_dram_rows, tmp, chan):
        P = self.P
        r = tmp["r"]
        sq = tmp["sq"]
        if isinstance(xt, tuple):
            xa, xb_ = xt[0][:, :], xt[1][:, :]
        else:
            xa, xb_ = xt[:, 0:512], xt[:, 512:1024]
        P.stt(r[:, 0:512], xa, ALPHA, res_ps_a, ALU.mult, ALU.add)
        P.stt(r[:, 512:1024], xb_, ALPHA, res_ps_b, ALU.mult, ALU.add)
        st = tmp["st"]
        P.reduce(st[:, 0:1], r[:, :], ALU.add)
        P.act(sq[:, :], r[:, :], AF.Square)
        P.reduce(st[:, 1:2], sq[:, :], ALU.add)
        P.ts(st[:, 2:3], st[:, 0:1], 1.0 / 1024, ALU.mult)
        P.tt(st[:, 3:4], st[:, 2:3], st[:, 2:3], ALU.mult)
        P.stt(st[:, 4:5], st[:, 1:2], 1.0 / 1024, st[:, 3:4], ALU.mult, ALU.subtract)
        P.ts(st[:, 4:5], st[:, 4:5], 1e-5, ALU.add)
        P.tt(st[:, 6:7], st[:, 4:5], self.cst[:, C_NH:C_NH + 1], ALU.pow, eng="pool")
        P.ts(r[:, :], r[:, :], st[:, 2:3], ALU.subtract, st[:, 6:7], ALU.mult)
        if lng is not None:
            P.tt(r[:, :], r[:, :], lng, ALU.mult)
            P.tt(r[:, :], r[:, :], lnb, ALU.add, eng="pool")
        i = P.dma(dst_dram_rows, r[:, :], chan=chan, q="sp")
        return i

    def phase_a(self):
        P = self.P
        T, NT = self.T, self.NT
        cst, cstb = self.cst, self.cstb
        wbf_d = self.nc.dram_tensor("wbf0_d", [D, EVEN_IN], BF16).ap()
        wbf_cast = []
        for c in range(8):
            wbf_cast.append(P.add("pool", (lambda e, o=wbf_d[c * 128:(c + 1) * 128, :], i=self.w_in0[c * 128:(c + 1) * 128, :]:
                                           e.dma_start(out=o, in_=i)), chan="wl%d" % (c % 2)))
        wbf_dv = wbf_d.rearrange("(c p) n -> p c n", p=128)
        wg = [P.sb("wg%d" % j, [128, 8, 512], BF16) for j in range(2)]
        wob_d = self.nc.dram_tensor("wob0_d", [D, D], BF16).ap()
        wob_cast = []
        for c in range(8):
            wob_cast.append(P.add("pool", (lambda e, o=wob_d[c * 128:(c + 1) * 128, :], i=self.w_out0[c * 128:(c + 1) * 128, :]:
                                           e.dma_start(out=o, in_=i)), chan="wl%d" % (c % 2)))
        wob2 = [P.sb("wob0_%d" % j, [128, 8, 512], BF16) for j in range(2)]
        wob_dv = wob_d.rearrange("(c p) n -> p c n", p=128)
        bc = P.sb("bc0", [128, NBC0], F32)
        P.dma(bc[:, :], self.bcv0[0].partition_broadcast(128), chan="bc0")
        BCV = lambda n: bc[:, BC0[n][0]:BC0[n][0] + BC0[n][1]]
        w2a2 = P.sb("w2a2", [65, 1024], F32)
        P.dma(w2a2[0:64, :], self.w2a2, chan="w2a2")
        P.dma(w2a2[64:65, :], self.bcv0[0:1, BC0["w0"][0]:BC0["w0"][0] + 1024], chan="w2a2")
        wukv = P.sb("wukv", [128, 128], F32)
        P.dma(wukv[:, :], self.wukv, chan="wukv")
        posi = P.sb("posi", [128, NT], I32)
        P.dma(posi[:, :], self.pos, chan="pos")
        posf = P.sb("posf", [128, NT], F32)
        P.copy(posf[:, :], posi[:, :])
        sc = [(P.sb("scr%d" % j, [128, 512], F32) if j != 6 else None) for j in range(9)]
        sinT = P.sb("sinT", [128, NT, 12], F32)
        cosT = P.sb("cosT", [128, NT, 12], F32)
        tph = sc[0][:, 0:NT * 12].re("p (n j) -> p n j", j=12)
        tmpR = sc[1][:, 0:NT * 12].re("p (n j) -> p n j", j=12)
        tmpI = sc[2][:, 0:NT * 12].bitcast(I32).re("p (n j) -> p n j", j=12)
        P.tt(tph[:, :, :], posf[:, :, None].bc([128, NT, 12]),
             cst[:, None, C_INV:C_INV + 12].bc([128, NT, 12]), ALU.mult)

        def frac_sin(dst, shift):
            P.ts(tmpR[:, :, :], tph[:, :, :], shift, ALU.add)
            P.copy(tmpI[:, :, :], tmpR[:, :, :])
            P.copy(dst[:, :, :], tmpI[:, :, :])
            P.tt(tmpR[:, :, :], tmpR[:, :, :], dst[:, :, :], ALU.subtract)
            P.ts(dst[:, :, :], tmpR[:, :, :], 0.5, ALU.is_ge)
            P.tt(tmpR[:, :, :], tmpR[:, :, :], dst[:, :, :], ALU.subtract)
            P.ts(dst[:, :, :], tmpR[:, :, :], -0.5, ALU.is_lt)
            P.tt(tmpR[:, :, :], tmpR[:, :, :], dst[:, :, :], ALU.add)
            P.act(dst[:, :, :], tmpR[:, :, :], AF.Sin, scale=TWO_PI)
        frac_sin(sinT, 0.0)
        frac_sin(cosT, 0.25)
        self.dump("sinT", sinT[:, :, :], [128, NT, 12])
        self.dump("cosT", cosT[:, :, :], [128, NT, 12])
        if self.stop == 0:
            return

        kT = P.sb("kT", [64, T], BF16)
        kiT = P.sb("kiT", [32, T], BF16)
        Vaug = P.sb("Vaug", [128, NT, 65], BF16)
        P.memset(Vaug[:, :, 64:65], 1.0)
        score = P.sb("score", [128, T], F32)
        Hf = P.sb("Hf", [64, 8, 64], F32)
        Hb = P.sb("Hb", [64, 8, 64], BF16)
        P.memset(Hf[:, :, :], 0.0)
        P.memset(Hb[:, :, :], 0.0)
        xb = P.sb("xb", [128, D], BF16)
        xT = P.sb("xT", [128, 8, 128], BF16)
        p = P.sb("p", [128, 1960], F32)
        psh = P.sb("psh", [128, 1664], F32)
        xsT = P.sb("xsT", [128, 8, 128], BF16)
        xlast = P.sb("xlast", [128, 8], BF16)
        sm = P.sb("sm", [128, 64], F32)
        thw = P.sb("thw", [128, 128], F32)
        thwT = P.sb("thwT", [65, 256], F32)
        P.memset(thwT[64:65, :], 1.0)
        rt_b = P.sb("rt_b", [128, 512], BF16)
        at_b = P.sb("at_b", [128, 512], BF16)
        bt_b = P.sb("bt_b", [128, 512], BF16)
        kt_b = P.sb("kt_b", [128, 512], BF16)
        v_b = P.sb("v_b", [128, 512], BF16)
        arT = P.sb("arT", [64, 8, 256], BF16)
        bT = P.sb("bT", [64, 8, 128], BF16)
        kTt = P.sb("kTt", [64, 8, 128], BF16)
        gC = P.sb("gC", [64, 8], F32)
        Mt = [P.sb("Mt%d" % j, [128, 4, 128], F32) for j in range(2)]
        LRb = P.sb("LRb", [128, 4, 128], BF16)
        LRk = P.sb("LRk", [128, 4, 256], BF16)
        rhs_f = P.sb("rhs_f", [128, 256], F32)
        Ub = P.sb("Ub", [128, 8, 64], BF16)
        ycb2 = [P.sb("ycb%d" % j, [128, D], BF16) for j in range(2)]
        yT_d = P.sb("yT_d", [128, 8, 128], BF16)
        xl = P.sb("xl", [128, D], F32)
        lnt_d = {"r": xl[:, :], "st": P.sb("ln_st", [128, 8], F32), "sq": P.sb("ln_sq", [128, D], F32)[:, :]}
        cT = sc[3][:, 0:128]
        kv = sc[3][:, 128:256]
        qb16 = rt_b
        qT2 = [P.sb("qT%d" % j, [64, 8, 128], BF16) for j in range(2)]
        gbs2 = [P.sb("gbs%d" % j, [128, 512], BF16) for j in range(2)]
        qib = at_b
        qiT2 = [P.sb("qiT%d" % j, [32, 8, 128], BF16) for j in range(2)]
        kb16 = P.sb("kb16", [128, 96], BF16)
        wis2 = [P.sb("wis%d" % j, [128, 8], F32) for j in range(2)]
        relu2 = [P.sb("relu%d" % j, [128, 512], F32) for j in range(2)]
        bis = P.sb("bis", [128, 8], F32)
        hw = P.sb("hw", [128, NIT], F32)
        sacc = P.sb("sacc", [128, NIT], F32)
        cacc = P.sb("cacc", [128, NIT], F32)
        junk = P.sb("junk", [128, T], BF16)
        mk = [P.sb("mk%d" % j, [128, 128], BF16) for j in range(2)]
        mkT = [P.sb("mkT%d" % j, [128, 4, 128], BF16) for j in range(2)]
        Eh = [[P.sb("E%d_%d" % (g, j), [128, 4, 128], BF16) for j in range(2)] for g in range(2)]
        rope_t = P.sb("rope_t", [128, 8, 16], F32)
        den = P.sb("den", [128, 8], F32)

        B0, B1, B2, B3, BI, BS, BA, B2d = self.B0, self.B1, self.B2, self.B3, self.BI, self.BS, self.BA, self.B2d
        groups = [(0, 512), (512, 1024), (1024, 1536), (1536, 1664), (1664, 2176), (2176, 2688), (2688, 3112), (3112, 3624)]
        O = 0
        c_ga, c_q, c_ckv, c_qi, c_ki, c_wi, c_gb = O, O + 512, O + 1024, O + 1152, O + 1408, O + 1440, O + 1448

        def rope(dst, src, nh, hd, half, ti, foff):
            cs = cosT[:, ti, None, foff:foff + half].bc([128, nh, half])
            sn = sinT[:, ti, None, foff:foff + half].bc([128, nh, half])
            x1 = src[:, :, 0:half]
            x2 = src[:, :, half:2 * half]
            t = rope_t
            P.tt(t[:, 0:nh, 0:half], x1, cs, ALU.mult)
            P.tt(t[:, 0:nh, 8:8 + half], x2, sn, ALU.mult)
            P.tt(dst[:, :, 0:half], t[:, 0:nh, 0:half], t[:, 0:nh, 8:8 + half], ALU.subtract)
            P.tt(t[:, 0:nh, 0:half], x2, cs, ALU.mult)
            P.tt(t[:, 0:nh, 8:8 + half], x1, sn, ALU.mult)
            P.tt(dst[:, :, half:2 * half], t[:, 0:nh, 0:half], t[:, 0:nh, 8:8 + half], ALU.add)
            P.copy(dst[:, :, 2 * half:hd], src[:, :, 2 * half:hd], eng="pool")

        H8 = lambda v: v.re("p (h d) -> p h d", h=8)

        def emit_A(ti):
            S = 128 * (ti + 1)
            tok = slice(ti * 128, (ti + 1) * 128)
            P.dma(sc[0][:, :], self.x[tok, 0:512], chan="xin0")
            P.dma(sc[1][:, :], self.x[tok, 512:1024], chan="xin1")
            P.copy(xb[:, 0:512], sc[0][:, :], eng="act")
            P.copy(xb[:, 512:1024], sc[1][:, :], eng="act")
            for c in range(8):
                P.tr(B2[:, c * 128:(c + 1) * 128], xb[:, c * 128:(c + 1) * 128], self.ident(128, True))
            P.copy(xT[:, :, :], B2[:, :].re("p (c t) -> p c t", c=8))
            if ti == 0:
                P.memset(xsT[:, :, 0:1], 0.0)
            else:
                P.copy(xsT[:, :, 0:1], xlast[:, :, None], eng="pool")
            P.copy(xsT[:, :, 1:128], xT[:, :, 0:127], eng="dve")
            P.copy(xlast[:, :], xT[:, :, 127], eng="pool")

            def project(gl, off, shifted=False):
                for (c0, c1) in gl:
                    bk = self.rbank()
                    wgb = wg[self.wgi % 2]
                    self.wgi += 1
                    _w = 16 if "wgsmall" in _SKIP else c1 - c0
                    P.add("sp", (lambda e, o=wgb[:, :, 0:_w].ap, i=wbf_dv[:, :, c0:c0 + _w]: e.dma_start(out=o, in_=i)),
                          writes=[wgb[:, :, :]], chan="wg%d" % (self.wgi % 2), extra_deps=wbf_cast)
                    for c in range(8):
                        P.mm(bk[:, 0:c1 - c0], xT[:, c, :], wgb[:, c, 0:c1 - c0], start=(c == 0), stop=(c == 7))
                    P.copy(p[:, c0 - off:c1 - off], bk[:, 0:c1 - c0], eng=("dve" if S >= 2048 else "act"))
                    if shifted:
                        bk2 = self.rbank()
                        for c in range(8):
                            P.mm(bk2[:, 0:c1 - c0], xsT[:, c, :], wgb[:, c, 0:c1 - c0], start=(c == 0), stop=(c == 7))
                        P.tt(psh[:, c0:c1], bk2[:, 0:c1 - c0], p[:, c0:c1], ALU.subtract)
            project(groups[0:4], 0, shifted=True)
            if self.stop == 1:
                self.dump("p", p[:, :], [128, EVEN_IN])
                return
            P.tt(psh[:, :], psh[:, :], BCV("mu"), ALU.mult)
            P.tt(psh[:, :], psh[:, :], p[:, 0:1664], ALU.add)
            r_, k_, v_ = psh[:, 0:512], psh[:, 512:1024], psh[:, 1024:1536]
            project(groups[4:8], 1664)
            if ti == self.NT - 1 or self.stop == 2:
                self.dump("ps", psh[:, :], [128, 1664])
            if self.stop == 2:
                return
            P.act(thw[:, 0:64], psh[:, 1536:1600], AF.Tanh)
            P.copy(thw[:, 64:128], psh[:, 1600:1664], eng="pool")
            P.tr(B3[0:64, 0:128], thw[:, 0:64], self.ident(128))
            P.tr(B3[0:64, 128:256], thw[:, 64:128], self.ident(128))
            P.copy(thwT[0:64, :], B3[0:64, 0:256])
            bz = self.rbank()
            P.mm(bz[:, :], thwT[:, 0:128], w2a2[:, 0:512])
            lw = sc[1]
            P.act(lw[:, :], bz[:, :], AF.Tanh, scale=0.5)
            P.ts(lw[:, :], lw[:, :], -0.5 * float(np.exp(-0.5)), ALU.mult, -0.5 * float(np.exp(-0.5)), ALU.add)
            if self.stop == 30:
                self.dump("lw", lw[:, :], [128, 512])
                return
            ba = self.rbank()
            P.mm(ba[:, :], thwT[:, 128:256], w2a2[:, 512:1024])
            av = sc[2]
            P.act(av[:, :], ba[:, :], AF.Tanh, scale=0.5)
            P.ts(av[:, :], av[:, :], 0.5, ALU.mult, 0.5, ALU.add)
            bcs = self.rbank()
            P.mm(bcs[:, :], cst[:, C_TRIU + 128:C_TRIU + 256], lw[:, :])
            cum = sc[3]
            P.copy(cum[:, :], bcs[:, :])
            for h in range(8):
                P.mm(B3[0:64, 256 + h:257 + h], lw[:, h * 64:(h + 1) * 64], cst[:, C_ONES:C_ONES + 1])
            P.act(gC[:, :], B3[0:64, 256:264], AF.Exp)
            gam = sc[4]
            P.act(gam[:, :], cum[:, :], AF.Exp)
            ginv = sc[5]
            P.act(ginv[:, :], cum[:, :], AF.Exp, scale=-1.0)
            gprev = sc[0]
            P.tt(gprev[:, :], cum[:, :], lw[:, :], ALU.subtract)
            P.act(gprev[:, :], gprev[:, :], AF.Exp)
            if self.stop == 31:
                self.dump("cum", cum[:, :], [128, 512])
                self.dump("gC", gC[:, :], [64, 8])
                return
            kk = sc[7]
            P.tt(kk[:, :], k_, BCV("kks"), ALU.mult)
            t8 = sc[8]
            P.act(t8[:, :], kk[:, :], AF.Square)
            P.reduce(sm[:, 0:8], H8(t8[:, :]), ALU.add)
            P.ts(sm[:, 8:16], sm[:, 0:8], 1e-24, ALU.max)
            P.tt(sm[:, 16:24], sm[:, 8:16], cst[:, C_NH:C_NH + 8], ALU.pow, eng="pool")
            P.tt(H8(kk[:, :]), H8(kk[:, :]), sm[:, 16:24, None].bc([128, 8, 64]), ALU.mult)
            k2 = sc[8]
            P.ts(k2[:, :], av[:, :], -1.0, ALU.add, eng="pool")
            P.tt(k2[:, :], k2[:, :], BCV("ka"), ALU.mult, eng="pool")
            P.ts(k2[:, :], k2[:, :], 1.0, ALU.add, eng="pool")
            P.tt(k2[:, :], k2[:, :], k_, ALU.mult, eng="pool")
            P.tt(rt_b[:, :], r_, gam[:, :], ALU.mult)
            P.stt(at_b[:, :], kk[:, :], -1.0, gprev[:, :], ALU.mult, ALU.mult)
            P.tt(kk[:, :], kk[:, :], av[:, :], ALU.mult)
            P.tt(bt_b[:, :], kk[:, :], ginv[:, :], ALU.mult)
            P.tt(kt_b[:, :], k2[:, :], ginv[:, :], ALU.mult)
            P.copy(v_b[:, :], v_, eng="act")
            bon = sc[0]
            P.tt(bon[:, :], r_, k2[:, :], ALU.mult)
            P.tt(bon[:, :], bon[:, :], BCV("rk"), ALU.mult, eng="pool")
            P.reduce(sm[:, 24:32], H8(bon[:, :]), ALU.add)
            P.tt(H8(bon[:, :]), H8(v_), sm[:, 24:32, None].bc([128, 8, 64]), ALU.mult)
            for (src, dstf) in ((at_b, lambda h: arT[:, h, 0:128]), (rt_b, lambda h: arT[:, h, 128:256]),
                                (bt_b, lambda h: bT[:, h, :]), (kt_b, lambda h: kTt[:, h, :])):
                for h in range(8):
                    P.tr(B2[0:64, h * 128:(h + 1) * 128], src[:, h * 64:(h + 1) * 64], self.ident(128, True))
                if src is at_b:
                    P.copy(arT[:, :, 0:128], B2[0:64, :].re("p (h t) -> p h t", h=8))
                elif src is rt_b:
                    P.copy(arT[:, :, 128:256], B2[0:64, :].re("p (h t) -> p h t", h=8), eng="act")
                elif src is bt_b:
                    P.copy(bT[:, :, :], B2[0:64, :].re("p (h t) -> p h t", h=8))
                else:
                    P.copy(kTt[:, :, :], B2[0:64, :].re("p (h t) -> p h t", h=8), eng="act")
            if self.stop == 32:
                self.dump("arT", arT[:, :, :], [64, 8, 256])
                self.dump("bon", bon[:, :], [128, 512])
                return
            cv = sc[2]
            P.tt(cv[:, 0:128], p[:, c_ckv:c_ckv + 128], p[:, c_ckv:c_ckv + 128], ALU.mult, eng="pool")
            P.reduce(sm[:, 56:57], cv[:, 0:128], ALU.add)
            P.ts(sm[:, 56:57], sm[:, 56:57], 1.0 / 128, ALU.mult, 1e-6, ALU.add)
            P.tt(sm[:, 57:58], sm[:, 56:57], cst[:, C_NH:C_NH + 1], ALU.pow, eng="pool")
            P.ts(cv[:, 0:128], p[:, c_ckv:c_ckv + 128], sm[:, 57:58], ALU.mult)
            P.tt(cv[:, 0:128], cv[:, 0:128], BCV("kvg"), ALU.mult)
            P.tr(B3[:, 0:128], cv[:, 0:128], self.ident(128))
            P.copy(cT[:, :], B3[:, 0:128])
            bkv = self.rbank()
            P.mm(bkv[:, 0:128], cT[:, :], wukv[:, :])
            P.copy(kv[:, :], bkv[:, 0:128], eng="act")
            kr = sc[7]
            rope(kr[:, 0:64].re("p (h d) -> p h d", h=1), kv[:, 0:64].re("p (h d) -> p h d", h=1), 1, 64, 8, ti, 0)
            qr = sc[8]
            qir = sc[1]
            rope(H8(qr[:, :]), H8(p[:, c_q:c_q + 512]), 8, 64, 8, ti, 0)
            rope(qir[:, 0:256].re("p (h d) -> p h d", h=8), p[:, c_qi:c_qi + 256].re("p (h d) -> p h d", h=8), 8, 32, 4, ti, 8)
            kir = sc[4]
            rope(kir[:, 0:32].re("p (h d) -> p h d", h=1), p[:, c_ki:c_ki + 32].re("p (h d) -> p h d", h=1), 1, 32, 4, ti, 8)
            P.copy(kb16[:, 0:64], kr[:, 0:64])
            P.copy(kb16[:, 64:96], kir[:, 0:32])
            P.tr(B2[0:64, 0:128], kb16[:, 0:64], self.ident(128, True))
            P.tr(B2[0:32, 128:256], kb16[:, 64:96], self.ident(128, True))
            P.copy(kT.k(ti)[:, tok], B2[0:64, 0:128])
            P.copy(kiT.k(ti)[:, tok], B2[0:32, 128:256], eng="act")
            P.copy(Vaug.k(ti)[:, ti, 0:64], kv[:, 64:128], eng="act")
            P.ts(qb16[:, :], qr[:, :], 0.125, ALU.mult)
            for h in range(8):
                P.tr(B2[0:64, h * 128:(h + 1) * 128], qb16[:, h * 64:(h + 1) * 64], self.ident(128, True))
            P.copy(qT2[ti % 2][:, :, :], B2[0:64, :].re("p (h t) -> p h t", h=8))
            P.copy(qib[:, 0:256], qir[:, 0:256], eng="act")
            for h in range(8):
                P.tr(B2[0:32, h * 128:(h + 1) * 128], qib[:, h * 32:(h + 1) * 32], self.ident(128, True))
            P.copy(qiT2[ti % 2][:, :, :], B2[0:32, :].re("p (h t) -> p h t", h=8))
            P.ts(wis2[ti % 2][:, :], p[:, c_wi:c_wi + 8], 1.0 / 16.0, ALU.mult)
            P.act(gbs2[ti % 2][:, :], p[:, c_gb:c_gb + 512], AF.Tanh, scale=0.5)
            P.stt(gbs2[ti % 2][:, :], gbs2[ti % 2][:, :], 1.0, p[:, c_gb:c_gb + 512], ALU.add, ALU.mult)
            y = sc[3]
            G4 = lambda v: v.re("p (h t) -> p h t", h=4)
            triS = cst[:, None, C_TRIU:C_TRIU + 128].bc([128, 4, 128])
            triI = cst[:, None, C_TRIU + 128:C_TRIU + 256].bc([128, 4, 128])
            trilS = cst[:, None, C_TRIL:C_TRIL + 128].bc([128, 4, 128])
            idb = cst[:, None, C_ID:C_ID + 128].bc([128, 4, 128])
            for g2 in range(2):
                hg = [4 * g2 + i for i in range(4)]
                if "invf32" in _SKIP:
                    Pp = [G4(sc[2][:, :]), G4(sc[7][:, :])]
                    Qp = [G4(sc[8][:, :]), G4(sc[4][:, :])]
                    Mp = [Mt[0][:, :, :], Mt[1][:, :, :]]
                    rhs_v = rhs_f[:, :]
                    idm, idb_ = self.ident(128), idb
                else:
                    Pp = [G4(sc[2][:, 0:256].bitcast(BF16)), G4(sc[7][:, 0:256].bitcast(BF16))]
                    Qp = [G4(sc[8][:, 0:256].bitcast(BF16)), G4(sc[4][:, 0:256].bitcast(BF16))]
                    Mp = [G4(Mt[0][:, 0:2, :].re("p h t -> p (h t)").bitcast(BF16)), G4(Mt[1][:, 0:2, :].re("p h t -> p (h t)").bitcast(BF16))]
                    rhs_v = rhs_f[:, 0:128].bitcast(BF16)
                    idb_ = cstb[:, None, C_ID:C_ID + 128].bc([128, 4, 128])
                bk = self.rbank()
                for i, h in enumerate(hg):
                    P.mm(bk[:, i * 128:(i + 1) * 128], bT[:, h, :], arT[:, h, 0:128])
                P.tt(Qp[0], G4(bk[:, :]), triS, ALU.mult)
                bk = self.rbank()
                for i, h in enumerate(hg):
                    P.mm(bk[:, i * 128:(i + 1) * 128], arT[:, h, 0:128], bT[:, h, :])
                P.tt(Pp[0], G4(bk[:, :]), trilS, ALU.mult)
                P.tt(Mp[0], Qp[0], idb_, ALU.add, eng="pool")
                bk = B3
                for i, h in enumerate(hg):
                    P.mm(bk[:, i * 128:(i + 1) * 128], bT[:, h, :], arT[:, h, 128:256])
                P.tt(LRb[:, :, :], G4(bk[:, :]), triI, ALU.mult)
                bk = self.rbank()
                for i, h in enumerate(hg):
                    P.mm(bk[:, i * 128:(i + 1) * 128], kTt[:, h, :], arT[:, h, 0:128])
                P.tt(LRk[:, :, 0:128], G4(bk[:, :]), triS, ALU.mult)
                bk = self.rbank()
                for i, h in enumerate(hg):
                    P.mm(bk[:, i * 128:(i + 1) * 128], kTt[:, h, :], arT[:, h, 128:256])
                P.tt(LRk[:, :, 128:256], G4(bk[:, :]), triI, ALU.mult)
                cp, cq, cm = 0, 0, 0
                for kstep in range(1, 1 if "inv" in _SKIP else 7):
                    bp = self.rbank()
                    for i in range(4):
                        P.mm(bp[:, i * 128:(i + 1) * 128], Qp[cq][:, i, :], Pp[cp][:, i, :])
                    if kstep < 6:
                        bq = self.rbank()
                        for i in range(4):
                            P.mm(bq[:, i * 128:(i + 1) * 128], Pp[cp][:, i, :], Qp[cq][:, i, :])
                    P.copy(Pp[1 - cp], G4(bp[:, :]))
                    if kstep < 6:
                        P.copy(Qp[1 - cq], G4(bq[:, :]), eng="act")
                    cp, cq = 1 - cp, 1 - cq
                    for i in range(4):
                        P.mm(B3[:, i * 128:(i + 1) * 128], Pp[cp][:, i, :], Mp[cm][:, i, :])
                    P.tt(Mp[1 - cm], G4(B3[:, :]), Mp[cm], ALU.add)
                    cm = 1 - cm
                br = self.rbank()
                for i, h in enumerate(hg):
                    P.mm(br[:, i * 64:(i + 1) * 64], arT[:, h, 0:128], Hb[:, h, :], start=True, stop=False)
                    P.mm(br[:, i * 64:(i + 1) * 64], LRk[:, i, 0:128], v_b[:, h * 64:(h + 1) * 64], start=False, stop=True)
                P.copy(rhs_v, br[:, 0:256], eng="act")
                bu = self.rbank()
                for i, h in enumerate(hg):
                    P.mm(bu[:, i * 64:(i + 1) * 64], Mp[cm][:, i, :], rhs_v[:, i * 64:(i + 1) * 64])
                P.copy(Ub[:, 4 * g2:4 * g2 + 4, :], bu[:, 0:256].re("p (h v) -> p h v", h=4))
                by = self.rbank()
                for i, h in enumerate(hg):
                    hs = slice(h * 64, (h + 1) * 64)
                    P.mm(by[:, i * 64:(i + 1) * 64], arT[:, h, 128:256], Hb[:, h, :], start=True, stop=False)
                    P.mm(by[:, i * 64:(i + 1) * 64], LRb[:, i, :], Ub[:, h, :], start=False, stop=False)
                    P.mm(by[:, i * 64:(i + 1) * 64], LRk[:, i, 128:256], v_b[:, hs], start=False, stop=True)
                P.copy(y[:, g2 * 256:(g2 + 1) * 256], by[:, 0:256], eng="act")
                bh = self.rbank()
                for i, h in enumerate(hg):
                    hs = slice(h * 64, (h + 1) * 64)
                    P.mm(bh[0:64, i * 64:(i + 1) * 64], bt_b[:, hs], Ub[:, h, :], start=True, stop=False)
                    P.mm(bh[0:64, i * 64:(i + 1) * 64], kt_b[:, hs], v_b[:, hs], start=False, stop=True)
                Hg = Hf[:, 4 * g2:4 * g2 + 4, :]
                P.tt(Hg, Hg, bh[0:64, 0:256].re("p (h v) -> p h v", h=4), ALU.add)
                P.tt(Hg, Hg, gC[:, 4 * g2:4 * g2 + 4, None].bc([64, 4, 64]), ALU.mult)
                P.copy(Hb[:, 4 * g2:4 * g2 + 4, :], Hg, eng="act")
            y = sc[3]
            if ti == self.NT - 1 or self.stop == 3:
                self.dump("yraw", y[:, :], [128, 512])
            if self.stop == 3:
                return
            P.reduce(sm[:, 32:40], H8(y[:, :]), ALU.add)
            ysq = sc[1]
            P.act(ysq[:, :], y[:, :], AF.Square)
            P.reduce(sm[:, 40:48], H8(ysq[:, :]), ALU.add)
            P.ts(sm[:, 32:40], sm[:, 32:40], 1.0 / 64, ALU.mult)
            P.tt(sm[:, 48:56], sm[:, 32:40], sm[:, 32:40], ALU.mult)
            P.stt(sm[:, 40:48], sm[:, 40:48], 1.0 / 64, sm[:, 48:56], ALU.mult, ALU.subtract)
            P.ts(sm[:, 40:48], sm[:, 40:48], 64e-5, ALU.add)
            P.tt(sm[:, 48:56], sm[:, 40:48], cst[:, C_NH:C_NH + 8], ALU.pow, eng="pool")
            P.tt(H8(y[:, :]), H8(y[:, :]), sm[:, 32:40, None].bc([128, 8, 64]), ALU.subtract)
            P.tt(H8(y[:, :]), H8(y[:, :]), sm[:, 48:56, None].bc([128, 8, 64]), ALU.mult)
            P.tt(y[:, :], y[:, :], BCV("gng"), ALU.mult)
            P.tt(y[:, :], y[:, :], BCV("gnb"), ALU.add, eng="pool")
            P.tt(y[:, :], y[:, :], bon[:, :], ALU.add)
            ga = sc[1]
            P.act(ga[:, :], p[:, c_ga:c_ga + 512], AF.Tanh, scale=0.5)
            P.stt(ga[:, :], ga[:, :], 1.0, p[:, c_ga:c_ga + 512], ALU.add, ALU.mult)
            P.stt(ycb2[ti % 2][:, 0:512], y[:, :], 0.5, ga[:, :], ALU.mult, ALU.mult)


        def emit_D(ti):
            S = 128 * (ti + 1)
            tok = slice(ti * 128, (ti + 1) * 128)
            qT, qiT, wis = qT2[ti % 2], qiT2[ti % 2], wis2[ti % 2]
            for half in range(2):
                P.add("sp", (lambda e, o=wob2[half][:, :, :].ap, i=wob_dv[:, :, half * 512:(half + 1) * 512]: e.dma_start(out=o, in_=i)),
                      writes=[wob2[half][:, :, :]], chan="wobld%d" % half, extra_deps=wob_cast)
            P.dma(xl[:, :], self.x[tok, :], chan="xl")
            nch = (S + 511) // 512
            for n in range(1 if "idx" in _SKIP else nch):
                k0 = n * 512
                k1 = min(S, k0 + 512)
                w = k1 - k0
                kkeys = range(k0 // 128, k1 // 128)
                scn = score.k(n)
                for h in range(8):
                    bi = BI if h % 2 == 0 else BS
                    P.add("pe", (lambda e, o=bi[:, 0:w].ap, l=qiT[:, h, :].ap, r=kiT[:, k0:k1].ap: e.matmul(o, l, r, start=True, stop=True)),
                          reads=[qiT[:, h, :]] + [kiT.k(j)[:, j * 128:(j + 1) * 128] for j in kkeys], writes=[bi[:, 0:w]])
                    rl = relu2[h % 2]
                    P.act(rl[:, 0:w], bi[:, 0:w], AF.Relu)
                    if h == 0:
                        P.ts(scn[:, k0:k1], rl[:, 0:w], wis[:, 0:1], ALU.mult, eng="pool")
                    else:
                        P.stt(scn[:, k0:k1], rl[:, 0:w], wis[:, h:h + 1], scn[:, k0:k1], ALU.mult, ALU.add)
            use_topk = S > self.topk
            if use_topk:
                P.add("dve", (lambda e, o=bis[:, 0:1].ap, i=score[:, 0:S].ap: e.tensor_reduce(o, i, AX.X, ALU.max, apply_absolute_value=True)),
                      reads=[score[:, 0:S]], writes=[bis[:, 0:1]])
            P.memset(score[0:64, S - 64:S], NEG_BIG)
            if use_topk:
                P.ts(bis[:, 1:2], bis[:, 0:1], -1.0, ALU.mult, -1.0, ALU.add)
                P.ts(bis[:, 2:3], bis[:, 0:1], 2.0, ALU.mult, 1.0, ALU.add)
                P.tt(hw[:, :], cst[:, C_POW:C_POW + NIT], bis[:, 2:3].bc([128, NIT]), ALU.mult)
                P.memset(sacc[:, :], 0.0)
                P.tt(bis[:, 3:4], bis[:, 1:2], hw[:, 0:1], ALU.add)
                nit = 0 if "topk" in _SKIP else NIT
                S1 = (int(S * 0.42) // 128) * 128 if S >= 1536 else 0
                n2 = S - S1
                for k in range(nit):
                    P.act(junk.k("a")[:, S1:S], score[:, S1:S], AF.Sign, bias=bis[:, 3:4], scale=-1.0, accum=sacc[:, k:k + 1])
                    if S1 > 0:
                        P.ts(junk.k("d")[:, 0:S1], score[:, 0:S1], bis[:, 3:4], ALU.is_gt, 0.0, ALU.add, accum=cacc[:, k:k + 1])
                        P.stt(bis[:, 5:6], cacc[:, k:k + 1], -2.0, sacc[:, k:k + 1], ALU.mult, ALU.add)
                        tv = bis[:, 5:6]
                    else:
                        tv = sacc[:, k:k + 1]
                    P.ts(bis[:, 4:5], tv, float(n2 - 2 * self.topk), ALU.is_le, hw[:, k:k + 1], ALU.mult)
                    P.tt(bis[:, 1:2], bis[:, 1:2], bis[:, 4:5], ALU.add)
                    if k + 1 < nit:
                        P.tt(bis[:, 3:4], bis[:, 1:2], hw[:, k + 1:k + 2], ALU.add)
            if use_topk:
                P.ts(junk[:, 0:S], score[:, 0:S], bis[:, 1:2], ALU.is_le, -30000.0, ALU.mult)
            else:
                P.ts(junk[:, 0:S], score[:, 0:S], -1.0e29, ALU.is_le, -30000.0, ALU.mult)
            nj = ti + 1
            for j in range(1 if "attn" in _SKIP else nj):
                ks = slice(j * 128, (j + 1) * 128)
                P.tr(B2d[:, 0:128], junk[:, ks], self.ident(128, True))
                mT = mkT[j % 2]
                P.copy(mT[:, :, :], B2d[:, None, 0:128].bc([128, 4, 128]), eng="act")
                for g in range(2):
                    bs_ = BI if g == 0 else BS
                    P.mm(bs_[:, 0:512], kT.k(j)[:, ks], qT[:, 4 * g:4 * g + 4, :].re("p h t -> p (h t)"), start=True, stop=False)
                    P.mm(bs_[:, 0:512], self.ident(128, True), mT[:, :, :].re("p h t -> p (h t)"), start=False, stop=True)
                    e_ = Eh[g][j % 2]
                    P.act(e_[:, :, :].re("p h t -> p (h t)"), bs_[:, 0:512], AF.Exp)
                    for hh in range(4):
                        o0 = g * 512 + hh * 65
                        P.mm(BA[:, o0:o0 + 65], e_[:, hh, :], Vaug.k(j)[:, j, :], start=(j == 0 and hh == 0),
                             stop=((j == nj - 1 or "attn" in _SKIP) and hh == 3))
            accv = BA[:, :].re("p (g c) -> p g c", g=2)[:, :, 0:260].re("p g (h c) -> p g h c", h=4)
            for g in range(2):
                P.copy(den[:, g * 4:(g + 1) * 4], accv[:, g, :, 64])
            P.recip(den[:, :], den[:, :])
            ycb = ycb2[ti % 2]
            for g in range(2):
                P.tt(ycb[:, 512 + g * 256:512 + (g + 1) * 256].re("p (h d) -> p h d", h=4), accv[:, g, :, 0:64],
                     den[:, g * 4:(g + 1) * 4, None].bc([128, 4, 64]), ALU.mult)
            P.stt(ycb[:, 512:1024], ycb[:, 512:1024], 0.5, gbs2[ti % 2][:, :], ALU.mult, ALU.mult)
            for c in range(8):
                P.tr(B2d[:, c * 128:(c + 1) * 128], ycb[:, c * 128:(c + 1) * 128], self.ident(128, True))
            P.copy(yT_d[:, :, :], B2d[:, :].re("p (c t) -> p c t", c=8))
            for half in range(2):
                bo = BI if half == 0 else BS
                for c in range(8):
                    P.mm(bo[:, :], yT_d[:, c, :], wob2[half][:, c, :], start=(c == 0), stop=(c == 7))
            i = self.layernorm_out(BI[:, :], BS[:, :], xl, None, None, self.h1[tok, :], lnt_d, "h1out")
            self.finals.append(i)

        self.wgi = 0
        for ti in range(NT + 1):
            a_ops = P.begin_stream()
            if ti < NT:
                emit_A(ti)
            if self.w1_d is not None:
                jobs = [(0, c) for c in range(8)] + [(1, c) for c in range(8)]
                if NT >= 20:
                    todo = [jobs[ti - 2]] if 2 <= ti < 18 else []
                else:
                    todo = jobs if ti == 0 else []
                for (w, c) in todo:
                    src = (self.w_in1 if w == 0 else self.w_out1)[c * 128:(c + 1) * 128, :]
                    self.finals.append(P.add("pool", (lambda e, o=self.w1_d[w][c * 128:(c + 1) * 128, :], i=src:
                                                      e.dma_start(out=o, in_=i)), chan="w1c%d" % (c % 2)))
            P.end_stream()
            d_ops = P.begin_stream()
            if ti >= 1:
                emit_D(ti - 1)
            P.end_stream()
            if "nomerge" in _SKIP:
                P.cur.extend(a_ops)
                P.cur.extend(d_ops)
            elif "propmerge" in _SKIP:
                P.merge(a_ops, d_ops)
            else:
                P.prio_delta = float(_os.environ.get("KDELTA", "1.0"))
                P.hop_x = float(_os.environ.get("KHOPX", "0.7"))
                P.hop_same = float(_os.environ.get("KHOPS", "0.15"))
                P.merge_sched(a_ops, d_ops)

    def phase_b(self):
        P = self.P
        T, NT = self.T, self.NT
        cst, cstb = self.cst, self.cstb
        wbf = P.sb("wbf1", [128, 8, ODD_IN], BF16)
        wob = P.sb("wob1", [128, 8, D], BF16)
        if self.w1_d is not None:
            for c in range(8):
                P.dma(wbf[:, c, :], self.w1_d[0][c * 128:(c + 1) * 128, :], chan="wl%d" % (c % 4), q="sp")
            for c in range(8):
                P.dma(wob[:, c, :], self.w1_d[1][c * 128:(c + 1) * 128, :], chan="wl%d" % (c % 4), q="sp")
        else:
            for c in range(8):
                P.dma(wbf[:, c, :], self.w_in1[c * 128:(c + 1) * 128, :], chan="wl%d" % (c % 2), q="pool")
            for c in range(8):
                P.dma(wob[:, c, :], self.w_out1[c * 128:(c + 1) * 128, :], chan="wl%d" % (c % 2), q="pool")
        bc = P.sb("bc1", [128, NBC1], F32)
        P.dma(bc[:, :], self.bcv1[0].partition_broadcast(128), chan="bc1")
        BCV = lambda n: bc[:, BC1[n][0]:BC1[n][0] + BC1[n][1]]
        g2 = P.sb("g2", [16, 512], F32)
        P.dma(g2[:, :], self.g2, chan="g2")
        Sf = P.sb("Sf", [128, 4, 256], F32)
        Sb = P.sb("Sb", [128, 4, 256], BF16)
        P.memset(Sf[:, :, :], 0.0)
        P.memset(Sb[:, :, :], 0.0)
        xt2 = [P.sb("xt%d" % j, [128, D], F32) for j in range(2)]
        xb = P.sb("xb", [128, D], BF16)
        xT = P.sb("xT", [128, 8, 128], BF16)
        p = P.sb("p", [128, ODD_IN], F32)
        sc = [P.sb("scr%d" % j, [128, 512], F32) for j in range(4)]
        sm = P.sb("sm", [128, 32], F32)
        gdT = P.sb("gdT", [16, 128], F32)
        qe_b = P.sb("qe_b", [128, 512], BF16)
        ke_b = P.sb("ke_b", [128, 512], BF16)
        kd_b2 = [P.sb("kd_b%d" % j, [128, 512], BF16) for j in range(2)]
        v_b2 = [P.sb("v_b%d" % j, [128, 1024], BF16) for j in range(2)]
        qeT2 = [P.sb("qeT%d" % j, [128, 4, 128], BF16) for j in range(2)]
        keT2 = [P.sb("keT%d" % j, [128, 4, 128], BF16) for j in range(2)]
        gS2 = [P.sb("gS%d" % j, [128, 4], F32) for j in range(2)]
        gc2 = [P.sb("gc%d" % j, [128, D], BF16) for j in range(2)]
        attT = [P.sb("attT%d" % j, [128, 128], BF16) for j in range(2)]
        o = P.sb("o", [128, D], F32)
        osq = P.sb("osq", [128, D], F32)
        ycb = P.sb("ycb", [128, D], BF16)
        yT = P.sb("yT", [128, 8, 128], BF16)
        ln_st = P.sb("ln_st", [128, 8], F32)
        B0, B1, B2, B3, B56, B78 = self.B0, self.B1, self.B2, self.B3, self.B56, self.B78
        B78b = B78[:, 0:512].bitcast(BF16)
        groups = [(0, 512), (512, 1024), (1024, 1536), (1536, 2048), (2048, 2064), (2064, 2576), (2576, 3088)]
        H4 = lambda v, h=4: v.re("p (h d) -> p h d", h=h)

        def emit_B1(ti):
            tok = slice(ti * 128, (ti + 1) * 128)
            xt = xt2[ti % 2]
            kd_b, v_b, qeT, keT, gS, gc = kd_b2[ti % 2], v_b2[ti % 2], qeT2[ti % 2], keT2[ti % 2], gS2[ti % 2], gc2[ti % 2]
            P.dma(xt[:, :], self.h1[tok, :], chan="xin")
            P.tt(xt[:, :], xt[:, :], BCV("lng0"), ALU.mult)
            P.tt(xt[:, :], xt[:, :], BCV("lnb0"), ALU.add, eng="pool")
            P.copy(xb[:, :], xt[:, :], eng="act")
            for c in range(8):
                P.tr(B2[:, c * 128:(c + 1) * 128], xb[:, c * 128:(c + 1) * 128], self.ident(128, True))
            P.copy(xT[:, :, :], B2[:, :].re("p (c t) -> p c t", c=8))
            for (c0, c1) in groups:
                bk = self.rbank()
                for c in range(8):
                    P.mm(bk[:, 0:c1 - c0], xT[:, c, :], wbf[:, c, c0:c1], start=(c == 0), stop=(c == 7))
                P.copy(p[:, c0:c1], bk[:, 0:c1 - c0], eng="act")
            q_, k_, v_, gd_, gcc = p[:, 0:512], p[:, 512:1024], p[:, 1024:2048], p[:, 2048:2064], p[:, 2064:3088]
            bt_ = self.rbank()
            P.tr(bt_[0:16, 0:128], gd_, self.ident(128))
            P.copy(gdT[:, :], bt_[0:16, 0:128])
            bz = self.rbank()
            P.mm(bz[:, :], gdT[:, :], g2[:, :])
            la = sc[0]
            P.tt(la[:, :], bz[:, :], BCV("gbias"), ALU.add)
            P.act(la[:, :], la[:, :], AF.Tanh, scale=0.5)
            P.ts(la[:, :], la[:, :], 0.5, ALU.mult, 0.5, ALU.add)
            P.act(la[:, :], la[:, :], AF.Ln)
            P.ts(la[:, :], la[:, :], 1.0 / 16.0, ALU.mult)
            bcs = self.rbank()
            P.mm(bcs[:, :], cst[:, C_TRIU + 128:C_TRIU + 256], la[:, :])
            cum = sc[1]
            P.copy(cum[:, :], bcs[:, :])
            bl = self.rbank()
            P.mm(bl[:, :], cst[:, C_ONES:C_ONES + 128], la[:, :])
            dlt = sc[2]
            P.tt(dlt[:, :], bl[:, :], cum[:, :], ALU.subtract)
            P.act(dlt[:, :], dlt[:, :], AF.Exp)
            P.tt(kd_b[:, :], k_, dlt[:, :], ALU.mult)
            bg = self.rbank()
            for h in range(4):
                P.mm(bg[:, 256 + h:257 + h], la[:, h * 128:(h + 1) * 128], cst[:, C_ONES:C_ONES + 1])
            P.act(gS[:, :], bg[:, 256:260], AF.Exp)
            eb = sc[3]
            P.act(eb[:, :], cum[:, :], AF.Exp)
            P.stt(qe_b[:, :], q_, float(128.0 ** -0.5), eb[:, :], ALU.mult, ALU.mult)
            P.act(eb[:, :], cum[:, :], AF.Exp, scale=-1.0)
            P.tt(ke_b[:, :], k_, eb[:, :], ALU.mult)
            P.copy(v_b[:, :], v_, eng="act")
            P.act(gc[:, :], gcc, AF.Tanh, scale=0.5)
            P.stt(gc[:, :], gc[:, :], 1.0, gcc, ALU.add, ALU.mult)
            for h in range(4):
                P.tr(B2[:, h * 128:(h + 1) * 128], qe_b[:, h * 128:(h + 1) * 128], self.ident(128, True))
                P.tr(B2[:, 512 + h * 128:512 + (h + 1) * 128], ke_b[:, h * 128:(h + 1) * 128], self.ident(128, True))
            P.copy(qeT[:, :, :], B2[:, 0:512].re("p (h t) -> p h t", h=4))
            P.copy(keT[:, :, :], B2[:, 512:1024].re("p (h t) -> p h t", h=4), eng="act")

        def emit_B2(ti):
            tok = slice(ti * 128, (ti + 1) * 128)
            xt = xt2[ti % 2]
            kd_b, v_b, qeT, keT, gS, gc = kd_b2[ti % 2], v_b2[ti % 2], qeT2[ti % 2], keT2[ti % 2], gS2[ti % 2], gc2[ti % 2]
            for h in range(4):
                P.mm(B3[:, 0:128], keT[:, h, :], qeT[:, h, :])
                at = attT[h % 2]
                P.tt(at[:, :], B3[:, 0:128], cst[:, C_TRIU + 128:C_TRIU + 256], ALU.mult)
                vs = slice(h * 256, (h + 1) * 256)
                P.mm(B78[:, vs], at[:, :], v_b[:, vs], start=True, stop=False)
                P.mm(B78[:, vs], qeT[:, h, :], Sb[:, h, :], start=False, stop=True)
                P.mm(B56[:, vs], kd_b[:, h * 128:(h + 1) * 128], v_b[:, vs], start=True, stop=True)
            P.copy(o[:, :], B78[:, :], eng="act")
            P.tt(Sf[:, :, :], Sf[:, :, :], gS[:, :, None].bc([128, 4, 256]), ALU.mult)
            P.tt(Sf[:, :, :], Sf[:, :, :], B56[:, :].re("p (h v) -> p h v", h=4), ALU.add)
            P.copy(Sb[:, :, :], Sf[:, :, :], eng="act")
            if ti == self.NT - 1:
                self.dump("gla_o", o[:, :], [128, 1024])
            P.act(osq[:, :], o[:, :], AF.Square)
            P.reduce(sm[:, 0:4], H4(osq[:, :]), ALU.add)
            P.ts(sm[:, 0:4], sm[:, 0:4], 1.0 / 256, ALU.mult, 1e-6, ALU.add)
            P.tt(sm[:, 8:12], sm[:, 0:4], cst[:, C_NH:C_NH + 4], ALU.pow, eng="pool")
            P.tt(H4(o[:, :]), H4(o[:, :]), sm[:, 8:12, None].bc([128, 4, 256]), ALU.mult)
            P.tt(H4(o[:, :]), H4(o[:, :]), BCV("normg")[:, None, :].bc([128, 4, 256]), ALU.mult, eng="pool")
            P.stt(ycb[:, :], o[:, :], 0.5, gc[:, :], ALU.mult, ALU.mult)
            for c in range(8):
                P.tr(B78b[:, c * 128:(c + 1) * 128], ycb[:, c * 128:(c + 1) * 128], self.ident(128, True))
            P.copy(yT[:, :, :], B78b[:, :].re("p (c t) -> p c t", c=8))
            for half in range(2):
                for c in range(8):
                    P.mm(B56[:, half * 512:(half + 1) * 512], yT[:, c, :], wob[:, c, half * 512:(half + 1) * 512],
                         start=(c == 0), stop=(c == 7))
            lnt = {"r": xt[:, :], "st": ln_st, "sq": osq[:, :]}
            i = self.layernorm_out(B56[:, 0:512], B56[:, 512:1024], xt, BCV("lng"), BCV("lnb"), self.out[tok, :], lnt, "yout")
            self.finals.append(i)

        for ti in range(NT + 1):
            a_ops = P.begin_stream()
            if ti < NT:
                emit_B1(ti)
            P.end_stream()
            d_ops = P.begin_stream()
            if ti >= 1:
                emit_B2(ti - 1)
            P.end_stream()
            if "nomergeb" in _SKIP:
                P.cur.extend(a_ops)
                P.cur.extend(d_ops)
            else:
                P.prio_delta = 0.0
                P.merge_sched(a_ops, d_ops)


BC1 = {}
_o = 0
for _n, _l in (("gbias", 512), ("normg", 256), ("lng", 1024), ("lnb", 1024), ("lng0", 1024), ("lnb0", 1024)):
    BC1[_n] = (_o, _l)
    _o += _l
NBC1 = _o


def prep_inputs(inp, b, T):
    f = lambda a: np.ascontiguousarray(np.asarray(a, dtype=np.float32))
    bcv0 = np.concatenate([f(inp["a_mu"][0]), f(inp["a_w0"][0]), f(inp["a_a0"][0]), f(inp["a_kk_scale"][0]),
                           f(inp["a_ka"][0]), f(inp["a_rk"][0]).reshape(-1), f(inp["a_gn_g"][0]), f(inp["a_gn_b"][0]),
                           f(inp["b_kv_norm_g"][0])])[None, :]
    bcv1 = np.concatenate([f(inp["c_g_bias"][0]), f(inp["c_norm_g"][0]), f(inp["ln_g"][1]), f(inp["ln_b"][1]),
                           f(inp["ln_g"][0]), f(inp["ln_b"][0])])[None, :]
    pos = np.ascontiguousarray(np.asarray(inp["positions"][b], dtype=np.int32).reshape(T // 128, 128).T)
    return {
        "x": f(inp["x"][b]),
        "pos": pos,
        "consts": make_consts(),
        "w_in0": f(inp["e_w_in"][0]),
        "w_out0": f(inp["e_w_out"][0]),
        "bcv0": f(bcv0),
        "w2a2": f(np.concatenate([inp["a_w2"][0], inp["a_a2"][0]], axis=1)),
        "wukv": f(np.concatenate([inp["b_wuk"][0], inp["b_wuv"][0]], axis=1)),
        "w_in1": f(inp["o_w_in"][0]),
        "w_out1": f(inp["o_w_out"][0]),
        "bcv1": f(bcv1),
        "g2": f(inp["c_g2"][0]),
    }


_T = 4096
_NB = 8


def kernel(**inputs):
    T = _T
    kb = K(T, min(256, T // 4))
    nc = kb.build()
    in_maps = [prep_inputs(inputs, b, T) for b in range(_NB)]
    res = run_bass_kernel_spmd(nc, in_maps, core_ids=list(range(_NB)))
    out = np.stack([np.asarray(r["out"], dtype=np.float32) for r in res.results], axis=0)
    return out
```

```python
import numpy as np
import concourse.bass as bass
import concourse.mybir as mybir
from contextlib import ExitStack

F32 = mybir.dt.float32
BF16 = mybir.dt.bfloat16
I32 = mybir.dt.int32
AF = mybir.ActivationFunctionType
ALU = mybir.AluOpType
AX = mybir.AxisListType

SEM_WRAP = 30000


class Buf:
    def __init__(self, t, name):
        self.t = t
        self.name = name
        self.W = None
        self.R = []
        self.keys = {}

    def __getitem__(self, idx):
        return View(self, None, self.t[idx])

    def k(self, key):
        return _Keyed(self, key)


class _Keyed:
    def __init__(self, buf, key):
        self.buf = buf
        self.key = key

    def __getitem__(self, idx):
        return View(self.buf, self.key, self.buf.t[idx])


class View:
    def __init__(self, buf, key, ap):
        self.buf = buf
        self.key = key
        self.ap = ap

    def __getitem__(self, idx):
        return View(self.buf, self.key, self.ap[idx])

    def re(self, s, **kw):
        return View(self.buf, self.key, self.ap.rearrange(s, **kw))

    def bc(self, shape):
        return View(self.buf, self.key, self.ap.to_broadcast(shape))

    def bitcast(self, dt):
        return View(self.buf, self.key, self.ap.bitcast(dt))


class Op:
    __slots__ = ("eng", "fn", "deps", "idx", "sig", "count", "chan", "chcount", "waits", "tag")


class Chan:
    def __init__(self, sem):
        self.sem = sem
        self.n = 0
        self.last = None


class Prog:
    ENGS = ("pe", "act", "dve", "pool", "sp")

    def __init__(self, nc, stack, prefix=""):
        self.prefix = prefix
        self.nc = nc
        self.stack = stack
        self.ops = []
        self.cur = self.ops
        self.bufs = []
        self.nsb = 0
        self.chans = {}

    def sb(self, name, shape, dt):
        t = self.stack.enter_context(self.nc.sbuf_tensor(self.prefix + name, list(shape), dt))
        b = Buf(t, name)
        self.bufs.append(b)
        return b

    def ps(self, name, shape, dt):
        t = self.stack.enter_context(self.nc.psum_tensor(self.prefix + name, list(shape), dt))
        b = Buf(t, name)
        b.psum = True
        self.bufs.append(b)
        return b

    def alias(self, buf, t, name):
        raise NotImplementedError

    def chan(self, name):
        if name not in self.chans:
            sem = self.stack.enter_context(self.nc.semaphore(self.prefix + "c_" + name))
            self.chans[name] = Chan(sem)
        return self.chans[name]

    def _track(self, idx, reads, writes, st=None):
        if st is not None:
            return self._track_st(idx, reads, writes, st)
        deps = set()
        for v in reads:
            b, k = v.buf, v.key
            if b.W is not None:
                deps.add(b.W)
            if k is None:
                for kk, (w, r) in b.keys.items():
                    if w is not None:
                        deps.add(w)
            else:
                if k in b.keys and b.keys[k][0] is not None:
                    deps.add(b.keys[k][0])
        for v in writes:
            b, k = v.buf, v.key
            if b.W is not None:
                deps.add(b.W)
            deps.update(b.R)
            if k is None:
                for kk, (w, r) in b.keys.items():
                    if w is not None:
                        deps.add(w)
                    deps.update(r)
            else:
                if k in b.keys:
                    w, r = b.keys[k]
                    if w is not None:
                        deps.add(w)
                    deps.update(r)
        for v in reads:
            b, k = v.buf, v.key
            if k is None:
                b.R.append(idx)
            else:
                b.keys.setdefault(k, [None, []])[1].append(idx)
        for v in writes:
            b, k = v.buf, v.key
            if k is None:
                b.W = idx
                b.R = []
                b.keys = {}
            else:
                b.keys[k] = [idx, []]
        deps.discard(idx)
        return deps

    def _track_st(self, idx, reads, writes, st):
        class _R:
            __slots__ = ("W", "R", "keys")
        deps = set()

        def rec(b):
            r = st.get(id(b))
            if r is None:
                r = _R()
                r.W = None
                r.R = []
                r.keys = {}
                st[id(b)] = r
            return r
        for v in reads:
            b, k = rec(v.buf), v.key
            if b.W is not None:
                deps.add(b.W)
            if k is None:
                for kk, (w, r) in b.keys.items():
                    if w is not None:
                        deps.add(w)
            elif k in b.keys and b.keys[k][0] is not None:
                deps.add(b.keys[k][0])
        for v in writes:
            b, k = rec(v.buf), v.key
            if b.W is not None:
                deps.add(b.W)
            deps.update(b.R)
            if k is None:
                for kk, (w, r) in b.keys.items():
                    if w is not None:
                        deps.add(w)
                    deps.update(r)
            elif k in b.keys:
                w, r = b.keys[k]
                if w is not None:
                    deps.add(w)
                deps.update(r)
        for v in reads:
            b, k = rec(v.buf), v.key
            if k is None:
                b.R.append(idx)
            else:
                b.keys.setdefault(k, [None, []])[1].append(idx)
        for v in writes:
            b, k = rec(v.buf), v.key
            if k is None:
                b.W = idx
                b.R = []
                b.keys = {}
            else:
                b.keys[k] = [idx, []]
        deps.discard(idx)
        return deps

    @staticmethod
    def _cost(op):
        reads, writes, chan, extra = op.deps
        n = 1
        v = writes[0] if writes else (reads[0] if reads else None)
        if v is not None:
            try:
                shp = list(v.ap.shape)
                n = 1
                for d in shp[1:]:
                    n *= int(d)
            except Exception:
                n = 64
        if chan is not None:
            return 0.05, 2.5
        e = op.eng
        if e == "pe":
            f32 = False
            try:
                f32 = (reads[0].ap.dtype == F32)
            except Exception:
                pass
            d = 0.04 + n / 2400.0 * (4.0 if f32 else 1.0)
        elif e == "act":
            d = 0.2 + n / 1200.0
        elif e == "dve":
            d = 0.08 + n / 960.0
        elif e == "pool":
            d = 0.15 + n / 350.0
        else:
            d = 0.05
        return d, d

    prio_delta = 0.0
    hop_same = 0.15
    hop_x = 0.7

    def merge_sched(self, a, b):
        streams = [a, b]
        sts = [{}, {}]
        fin = [{}, {}]
        ptr = [0, 0]
        efree = getattr(self, "_efree", None)
        if efree is None:
            efree = {e: 0.0 for e in self.ENGS}
        base = max(efree.values()) if False else min(efree.values())
        cache = [None, None]

        def est(si):
            if cache[si] is not None:
                return cache[si]
            i = ptr[si]
            op = streams[si][i]
            reads, writes, chan, extra = op.deps
            deps = self._track_st(i, reads, writes, sts[si]) if False else None
            return None
        ldeps = []
        for si in range(2):
            dl = []
            stt_ = {}
            for i, op in enumerate(streams[si]):
                reads, writes, chan, extra = op.deps
                dl.append(self._track_st(i, reads, writes, stt_))
            ldeps.append(dl)
        start_of = [None, None]

        def earliest(si):
            i = ptr[si]
            op = streams[si][i]
            t = efree[op.eng]
            for d in ldeps[si][i]:
                pe = streams[si][d].eng
                hop = 0.0 if (pe == "pe" and op.eng == "pe") else (self.hop_same if pe == op.eng else self.hop_x)
                t = max(t, fin[si][d] + hop)
            return t
        while ptr[0] < len(a) or ptr[1] < len(b):
            cands = [si for si in range(2) if ptr[si] < len(streams[si])]
            if len(cands) == 2:
                e0, e1 = earliest(0), earliest(1)
                si = 1 if e1 <= e0 + self.prio_delta else 0
                t = e1 if si == 1 else e0
            else:
                si = cands[0]
                t = earliest(si)
            i = ptr[si]
            op = streams[si][i]
            occ, lat = self._cost(op)
            efree[op.eng] = t + occ
            fin[si][i] = t + lat
            self.cur.append(op)
            ptr[si] += 1
        self._efree = efree

    def add(self, eng, fn, reads=(), writes=(), chan=None, extra_deps=(), tag=None):
        op = Op()
        op.eng = eng
        op.fn = fn
        op.idx = None
        op.sig = False
        op.count = None
        op.chan = None
        op.chcount = None
        op.waits = []
        op.tag = tag
        reads = [v for v in reads if v is not None]
        writes = [v for v in writes if v is not None]
        writes = writes + [v for v in reads if getattr(v.buf, "psum", False)]
        reads = [v for v in reads if not getattr(v.buf, "psum", False)]
        op.deps = (reads, writes, chan, [d for d in extra_deps if d is not None])
        self.cur.append(op)
        return op

    def begin_stream(self):
        lst = []
        self._saved = getattr(self, "_saved", [])
        self._saved.append(self.cur)
        self.cur = lst
        return lst

    def end_stream(self):
        self.cur = self._saved.pop()

    def merge(self, a, b):
        na, nb = len(a), len(b)
        ia = ib = 0
        while ia < na or ib < nb:
            if ib >= nb or (ia < na and ia * nb <= ib * na):
                self.cur.append(a[ia])
                ia += 1
            else:
                self.cur.append(b[ib])
                ib += 1

    def resolve(self):
        for i, op in enumerate(self.ops):
            op.idx = i
        for op in self.ops:
            reads, writes, chan, extra = op.deps
            deps = self._track(op.idx, reads, writes)
            deps.update(d.idx for d in extra)
            if chan is not None:
                ch = self.chan(chan)
                op.chan = ch
                ch.n += 1
                op.chcount = ch.n
                if ch.last is not None:
                    deps.add(ch.last)
                ch.last = op.idx
            deps.discard(op.idx)
            assert all(d < op.idx for d in deps), "dependency on a later op"
            op.deps = deps

    def mm(self, out, lhsT, rhs, start=True, stop=True):
        o, l, r = out.ap, lhsT.ap, rhs.ap
        return self.add("pe", lambda e: e.matmul(o, l, r, start=start, stop=stop),
                        reads=[lhsT, rhs], writes=[out])

    def tr(self, out, in_, ident):
        o, i, d = out.ap, in_.ap, ident.ap
        return self.add("pe", lambda e: e.transpose(o, i, d), reads=[in_, ident], writes=[out])

    def act(self, out, in_, func, bias=None, scale=1.0, accum=None, eng="act"):
        o, i = out.ap, in_.ap
        reads = [in_]
        kw = {}
        if bias is not None:
            if isinstance(bias, View):
                reads.append(bias)
                kw["bias"] = bias.ap
            else:
                kw["bias"] = float(bias)
        if isinstance(scale, View):
            reads.append(scale)
            kw["scale"] = scale.ap
        else:
            kw["scale"] = float(scale)
        writes = [out]
        if accum is not None:
            writes.append(accum)
            kw["accum_out"] = accum.ap
        return self.add("act", lambda e: e.activation(o, i, func, **kw), reads=reads, writes=writes)

    def tt(self, out, a, b, op, eng="dve"):
        o, x, y = out.ap, a.ap, b.ap
        return self.add(eng, lambda e: e.tensor_tensor(o, x, y, op), reads=[a, b], writes=[out])

    def ts(self, out, a, s1, op0, s2=None, op1=None, accum=None, eng="dve"):
        o, x = out.ap, a.ap
        reads = [a]
        if isinstance(s1, View):
            reads.append(s1)
            s1 = s1.ap
        if isinstance(s2, View):
            reads.append(s2)
            s2 = s2.ap
        kw = {}
        writes = [out]
        if op1 is not None:
            kw["op1"] = op1
        if accum is not None:
            kw["accum_out"] = accum.ap
            writes.append(accum)
        return self.add(eng, lambda e: e.tensor_scalar(o, x, s1, s2, op0, **kw), reads=reads, writes=writes)

    def stt(self, out, a, s, b, op0, op1, eng="dve"):
        o, x, y = out.ap, a.ap, b.ap
        reads = [a, b]
        if isinstance(s, View):
            reads.append(s)
            s = s.ap
        return self.add("dve", lambda e: e.scalar_tensor_tensor(o, x, s, y, op0, op1), reads=reads, writes=[out])

    def copy(self, out, in_, eng="dve"):
        o, i = out.ap, in_.ap
        if eng == "act":
            return self.add("act", lambda e: e.copy(o, i), reads=[in_], writes=[out])
        return self.add(eng, lambda e: e.tensor_copy(o, i), reads=[in_], writes=[out])

    def memset(self, out, val, eng="dve"):
        o = out.ap
        return self.add(eng, lambda e: e.memset(o, val), writes=[out])

    def reduce(self, out, in_, op, axis=AX.X, eng="dve"):
        o, i = out.ap, in_.ap
        return self.add(eng, lambda e: e.tensor_reduce(o, i, axis, op), reads=[in_], writes=[out])

    def recip(self, out, in_):
        o, i = out.ap, in_.ap
        return self.add("dve", lambda e: e.reciprocal(o, i), reads=[in_], writes=[out])

    def dma(self, out, in_, chan, q="sp", reads=(), writes=()):
        o = out.ap if isinstance(out, View) else out
        i = in_.ap if isinstance(in_, View) else in_
        rd = list(reads) + ([in_] if isinstance(in_, View) else [])
        wr = list(writes) + ([out] if isinstance(out, View) else [])
        return self.add(q, lambda e: e.dma_start(out=o, in_=i), reads=rd, writes=wr, chan=chan)

    def emit(self, final_wait_ops):
        nc = self.nc
        self.resolve()
        ops = self.ops
        fin = Op()
        fin.eng = "sp"
        fin.fn = None
        fin.idx = len(ops)
        fin.sig = False
        fin.count = None
        fin.chan = None
        fin.chcount = None
        fin.waits = []
        fin.deps = set(o.idx for o in final_wait_ops)
        fin.tag = "final"
        ops.append(fin)
        for op in ops:
            for d in op.deps:
                p = ops[d]
                if p.chan is None:
                    if p.eng == "pe" and op.eng == "pe":
                        continue
                    p.sig = True
        cnt = {e: 0 for e in self.ENGS}
        for op in ops:
            if op.sig:
                cnt[op.eng] += 1
                op.count = cnt[op.eng]
        nsem = {e: (cnt[e] // SEM_WRAP) + 1 for e in self.ENGS}
        sems = {e: [self.stack.enter_context(nc.semaphore(self.prefix + "s_%s_%d" % (e, j))) for j in range(nsem[e])]
                for e in self.ENGS}
        waited = {e: {} for e in self.ENGS}
        for op in ops:
            need = {}
            for d in op.deps:
                p = ops[d]
                if p.chan is not None:
                    key = ("c", id(p.chan))
                    val = p.chcount
                    obj = p.chan
                else:
                    if p.eng == "pe" and op.eng == "pe":
                        continue
                    key = ("e", p.eng)
                    val = p.count
                    obj = p.eng
                if key not in need or need[key][0] < val:
                    need[key] = (val, obj)
            w = waited[op.eng]
            for key, (val, obj) in need.items():
                if w.get(key, 0) >= val:
                    continue
                w[key] = val
                if key[0] == "c":
                    op.waits.append((obj.sem, 16 * val))
                else:
                    j, v = divmod(val - 1, SEM_WRAP)
                    op.waits.append((sems[obj][j], v + 1))
        by_eng = {e: [op for op in ops if op.eng == e] for e in self.ENGS}
        self.stats = {e: len(by_eng[e]) for e in self.ENGS}
        self.stats["waits"] = sum(len(op.waits) for op in ops)

        def run(eng_obj, lst, ename):
            for op in lst:
                for (sem, val) in op.waits:
                    eng_obj.wait_ge(sem, val)
                if op.fn is None:
                    continue
                ins = op.fn(eng_obj)
                if op.chan is not None:
                    ins.then_inc(op.chan.sem, 16)
                elif op.sig:
                    j = (op.count - 1) // SEM_WRAP
                    ins.then_inc(sems[ename][j], 1)

        with nc.Block() as block:
            @block.tensor
            def _(e):
                run(e, by_eng["pe"], "pe")

            @block.scalar
            def _(e):
                run(e, by_eng["act"], "act")

            @block.vector
            def _(e):
                run(e, by_eng["dve"], "dve")

            @block.gpsimd
            def _(e):
                run(e, by_eng["pool"], "pool")

            @block.sync
            def _(e):
                run(e, by_eng["sp"], "sp")

from concourse.bass_utils import run_bass_kernel_spmd

D = 1024
EVEN_IN = 3624
ODD_IN = 3088
NEG_BIG = -1.0e30
NEG_SEL = -3.0e38
ALPHA = 4.0 ** 0.25
TWO_PI = 2.0 * np.pi

C_ID = 0
C_TRIU = 128
C_TRIL = 384
C_ONES = 512
C_INV = 640
C_POW = 656
NIT = 28
C_NH = 688
NCONST = 656 + 32 + 8

BC0 = {}
_o = 0
for _n, _l in (("mu", 1664), ("w0", 512), ("a0", 512), ("kks", 512), ("ka", 512), ("rk", 512),
               ("gng", 512), ("gnb", 512), ("kvg", 128)):
    BC0[_n] = (_o, _l)
    _o += _l
NBC0 = _o


def make_consts():
    c = np.zeros((128, NCONST), np.float32)
    c[:, C_ID:C_ID + 128] = np.eye(128, dtype=np.float32)
    s = np.arange(128)[:, None]
    t = np.arange(128)[None, :]
    c[:, C_TRIU:C_TRIU + 128] = (s < t)
    c[:, C_TRIU + 128:C_TRIU + 256] = (s <= t)
    c[:, C_TRIL:C_TRIL + 128] = (s > t)
    c[:, C_ONES:C_ONES + 128] = 1.0
    inv16 = np.power(500000.0, -np.arange(8, dtype=np.float32) * 2.0 / 16).astype(np.float32)
    inv8 = np.power(500000.0, -np.arange(4, dtype=np.float32) * 2.0 / 8).astype(np.float32)
    c[:, C_INV:C_INV + 8] = (inv16.astype(np.float64) / TWO_PI).astype(np.float32)[None, :]
    c[:, C_INV + 8:C_INV + 12] = (inv8.astype(np.float64) / TWO_PI).astype(np.float32)[None, :]
    c[:, C_NH:C_NH + 8] = -0.5
    c[:, C_POW:C_POW + NIT] = (2.0 ** -(np.arange(NIT) + 1.0)).astype(np.float32)[None, :]
    return c


import os as _os
_SKIP = set(_os.environ.get("KSKIP", "").split(","))


class K:
    def __init__(self, T, topk, dbg=(), phases=("a", "b"), stop=99):
        self.stop = stop
        self.T = T
        self.NT = T // 128
        self.topk = topk
        self.dbg = set(dbg)
        self.phases = phases
        self.dbg_outs = {}

    def dram_in(self, name, shape, dt=F32):
        return self.nc.dram_tensor(name, list(shape), dt, kind="ExternalInput").ap()

    def build(self):
        nc = bass.Bass("TRN2", target_bir_lowering=False)
        self.nc = nc
        T, NT = self.T, self.NT
        self.x = self.dram_in("x", [T, D])
        self.pos = self.dram_in("pos", [128, NT], I32)
        self.consts = self.dram_in("consts", [128, NCONST])
        self.w_in0 = self.dram_in("w_in0", [D, EVEN_IN])
        self.w_out0 = self.dram_in("w_out0", [D, D])
        self.bcv0 = self.dram_in("bcv0", [1, NBC0])
        self.w2a2 = self.dram_in("w2a2", [64, 1024])
        self.wukv = self.dram_in("wukv", [128, 128])
        self.w_in1 = self.dram_in("w_in1", [D, ODD_IN])
        self.w_out1 = self.dram_in("w_out1", [D, D])
        self.bcv1 = self.dram_in("bcv1", [1, NBC1])
        self.g2 = self.dram_in("g2", [16, 512])
        if "b" in self.phases:
            self.out = nc.dram_tensor("out", [T, D], F32, kind="ExternalOutput").ap()
        if self.phases == ("a", "b"):
            self.h1 = nc.dram_tensor("h1", [T, D], F32).ap()
        elif self.phases == ("a",):
            self.h1 = nc.dram_tensor("h1", [T, D], F32, kind="ExternalOutput").ap()
        else:
            self.h1 = self.dram_in("h1", [T, D])
        self.w1_d = None
        if self.phases == ("a", "b"):
            self.w1_d = (nc.dram_tensor("wbf1_d", [D, ODD_IN], BF16).ap(), nc.dram_tensor("wob1_d", [D, D], BF16).ap())
        first = True
        for ph in self.phases:
            if ("pa" in _SKIP and ph == "a"):
                continue
            if not first:
                nc.all_engine_barrier()
            first = False
            with ExitStack() as st:
                P = Prog(nc, st, prefix=ph + "_")
                self.P = P
                self.finals = []
                self.setup_common(ph)
                if ph == "a":
                    self.phase_a()
                else:
                    self.phase_b()
                P.emit(self.finals)
                self.stats = getattr(self, "stats", {})
                self.stats[ph] = P.stats
        return nc

    def dump(self, name, view, shape):
        if name not in self.dbg:
            return
        t = self.nc.dram_tensor("dbg_" + name, list(shape), view.ap.dtype if hasattr(view.ap, "dtype") else F32,
                                kind="ExternalOutput").ap()
        self.dbg_outs[name] = "dbg_" + name
        i = self.P.dma(t, view, chan="dbg_" + name, q="pool")
        self.finals.append(i)

    def setup_common(self, ph="a"):
        P = self.P
        self.cst = P.sb("cst", [128, NCONST], F32)
        P.dma(self.cst[:, :], self.consts, chan="cst")
        self.cstb = P.sb("cstb", [128, 640], BF16)
        P.copy(self.cstb[:, :], self.cst[:, 0:640])
        self.B0 = P.ps("B0", [128, 512], F32)
        self.B1 = P.ps("B1", [128, 512], F32)
        self.B2 = P.ps("B2", [128, 1024], BF16)
        if ph == "a":
            self.B3 = self.B1
            self.B2d = P.ps("B2d", [128, 1024], BF16)
        else:
            self.B3 = P.ps("B3", [128, 512], F32)
        if ph == "a":
            self.BI = P.ps("BI", [128, 512], F32)
            self.BS = P.ps("BS", [128, 512], F32)
            self.BA = P.ps("BA", [128, 1024], F32)
        else:
            self.B56 = P.ps("B56", [128, 1024], F32)
            self.B78 = P.ps("B78", [128, 1024], F32)
        self.rot = 0

    def rbank(self):
        self.rot ^= 1
        return self.B0 if self.rot else self.B1

    def ident(self, n=128, bf=False):
        return (self.cstb if bf else self.cst)[0:n, C_ID:C_ID + n]

    def layernorm_out(self, res_ps_a, res_ps_b, xt, lng, lnb, dst_dram_rows, tmp, chan):
        P = self.P
        r = tmp["r"]
        sq = tmp["sq"]
        if isinstance(xt, tuple):
            xa, xb_ = xt[0][:, :], xt[1][:, :]
        else:
            xa, xb_ = xt[:, 0:512], xt[:, 512:1024]
        P.stt(r[:, 0:512], xa, ALPHA, res_ps_a, ALU.mult, ALU.add)
        P.stt(r[:, 512:1024], xb_, ALPHA, res_ps_b, ALU.mult, ALU.add)
        st = tmp["st"]
        P.reduce(st[:, 0:1], r[:, :], ALU.add)
        P.act(sq[:, :], r[:, :], AF.Square)
        P.reduce(st[:, 1:2], sq[:, :], ALU.add)
        P.ts(st[:, 2:3], st[:, 0:1], 1.0 / 1024, ALU.mult)
        P.tt(st[:, 3:4], st[:, 2:3], st[:, 2:3], ALU.mult)
        P.stt(st[:, 4:5], st[:, 1:2], 1.0 / 1024, st[:, 3:4], ALU.mult, ALU.subtract)
        P.ts(st[:, 4:5], st[:, 4:5], 1e-5, ALU.add)
        P.tt(st[:, 6:7], st[:, 4:5], self.cst[:, C_NH:C_NH + 1], ALU.pow, eng="pool")
        P.ts(r[:, :], r[:, :], st[:, 2:3], ALU.subtract, st[:, 6:7], ALU.mult)
        if lng is not None:
            P.tt(r[:, :], r[:, :], lng, ALU.mult)
            P.tt(r[:, :], r[:, :], lnb, ALU.add, eng="pool")
        i = P.dma(dst_dram_rows, r[:, :], chan=chan, q="sp")
        return i

    def phase_a(self):
        P = self.P
        T, NT = self.T, self.NT
        cst, cstb = self.cst, self.cstb
        wbf_d = self.nc.dram_tensor("wbf0_d", [D, EVEN_IN], BF16).ap()
        wbf_cast = []
        for c in range(8):
            wbf_cast.append(P.add("pool", (lambda e, o=wbf_d[c * 128:(c + 1) * 128, :], i=self.w_in0[c * 128:(c + 1) * 128, :]:
                                           e.dma_start(out=o, in_=i)), chan="wl%d" % (c % 2)))
        wbf_dv = wbf_d.rearrange("(c p) n -> p c n", p=128)
        wg = [P.sb("wg%d" % j, [128, 8, 512], BF16) for j in range(2)]
        wob_d = self.nc.dram_tensor("wob0_d", [D, D], BF16).ap()
        wob_cast = []
        for c in range(8):
            wob_cast.append(P.add("pool", (lambda e, o=wob_d[c * 128:(c + 1) * 128, :], i=self.w_out0[c * 128:(c + 1) * 128, :]:
                                           e.dma_start(out=o, in_=i)), chan="wl%d" % (c % 2)))
        wob2 = [P.sb("wob0_%d" % j, [128, 8, 512], BF16) for j in range(2)]
        wob_dv = wob_d.rearrange("(c p) n -> p c n", p=128)
        bc = P.sb("bc0", [128, NBC0], F32)
        P.dma(bc[:, :], self.bcv0[0].partition_broadcast(128), chan="bc0")
        BCV = lambda n: bc[:, BC0[n][0]:BC0[n][0] + BC0[n][1]]
        w2a2 = P.sb("w2a2", [65, 1024], F32)
        P.dma(w2a2[0:64, :], self.w2a2, chan="w2a2")
        P.dma(w2a2[64:65, :], self.bcv0[0:1, BC0["w0"][0]:BC0["w0"][0] + 1024], chan="w2a2")
        wukv = P.sb("wukv", [128, 128], F32)
        P.dma(wukv[:, :], self.wukv, chan="wukv")
        posi = P.sb("posi", [128, NT], I32)
        P.dma(posi[:, :], self.pos, chan="pos")
        posf = P.sb("posf", [128, NT], F32)
        P.copy(posf[:, :], posi[:, :])
        sc = [(P.sb("scr%d" % j, [128, 512], F32) if j != 6 else None) for j in range(9)]
        sinT = P.sb("sinT", [128, NT, 12], F32)
        cosT = P.sb("cosT", [128, NT, 12], F32)
        tph = sc[0][:, 0:NT * 12].re("p (n j) -> p n j", j=12)
        tmpR = sc[1][:, 0:NT * 12].re("p (n j) -> p n j", j=12)
        tmpI = sc[2][:, 0:NT * 12].bitcast(I32).re("p (n j) -> p n j", j=12)
        P.tt(tph[:, :, :], posf[:, :, None].bc([128, NT, 12]),
             cst[:, None, C_INV:C_INV + 12].bc([128, NT, 12]), ALU.mult)

        def frac_sin(dst, shift):
            P.ts(tmpR[:, :, :], tph[:, :, :], shift, ALU.add)
            P.copy(tmpI[:, :, :], tmpR[:, :, :])
            P.copy(dst[:, :, :], tmpI[:, :, :])
            P.tt(tmpR[:, :, :], tmpR[:, :, :], dst[:, :, :], ALU.subtract)
            P.ts(dst[:, :, :], tmpR[:, :, :], 0.5, ALU.is_ge)
            P.tt(tmpR[:, :, :], tmpR[:, :, :], dst[:, :, :], ALU.subtract)
            P.ts(dst[:, :, :], tmpR[:, :, :], -0.5, ALU.is_lt)
            P.tt(tmpR[:, :, :], tmpR[:, :, :], dst[:, :, :], ALU.add)
            P.act(dst[:, :, :], tmpR[:, :, :], AF.Sin, scale=TWO_PI)
        frac_sin(sinT, 0.0)
        frac_sin(cosT, 0.25)
        self.dump("sinT", sinT[:, :, :], [128, NT, 12])
        self.dump("cosT", cosT[:, :, :], [128, NT, 12])
        if self.stop == 0:
            return

        kT = P.sb("kT", [64, T], BF16)
        kiT = P.sb("kiT", [32, T], BF16)
        Vaug = P.sb("Vaug", [128, NT, 65], BF16)
        P.memset(Vaug[:, :, 64:65], 1.0)
        score = P.sb("score", [128, T], F32)
        Hf = P.sb("Hf", [64, 8, 64], F32)
        Hb = P.sb("Hb", [64, 8, 64], BF16)
        P.memset(Hf[:, :, :], 0.0)
        P.memset(Hb[:, :, :], 0.0)
        xb = P.sb("xb", [128, D], BF16)
        xT = P.sb("xT", [128, 8, 128], BF16)
        p = P.sb("p", [128, 1960], F32)
        psh = P.sb("psh", [128, 1664], F32)
        xsT = P.sb("xsT", [128, 8, 128], BF16)
        xlast = P.sb("xlast", [128, 8], BF16)
        sm = P.sb("sm", [128, 64], F32)
        thw = P.sb("thw", [128, 128], F32)
        thwT = P.sb("thwT", [65, 256], F32)
        P.memset(thwT[64:65, :], 1.0)
        rt_b = P.sb("rt_b", [128, 512], BF16)
        at_b = P.sb("at_b", [128, 512], BF16)
        bt_b = P.sb("bt_b", [128, 512], BF16)
        kt_b = P.sb("kt_b", [128, 512], BF16)
        v_b = P.sb("v_b", [128, 512], BF16)
        arT = P.sb("arT", [64, 8, 256], BF16)
        bT = P.sb("bT", [64, 8, 128], BF16)
        kTt = P.sb("kTt", [64, 8, 128], BF16)
        gC = P.sb("gC", [64, 8], F32)
        Mt = [P.sb("Mt%d" % j, [128, 4, 128], F32) for j in range(2)]
        LRb = P.sb("LRb", [128, 4, 128], BF16)
        LRk = P.sb("LRk", [128, 4, 256], BF16)
        rhs_f = P.sb("rhs_f", [128, 256], F32)
        Ub = P.sb("Ub", [128, 8, 64], BF16)
        ycb2 = [P.sb("ycb%d" % j, [128, D], BF16) for j in range(2)]
        yT_d = P.sb("yT_d", [128, 8, 128], BF16)
        xl = P.sb("xl", [128, D], F32)
        lnt_d = {"r": xl[:, :], "st": P.sb("ln_st", [128, 8], F32), "sq": P.sb("ln_sq", [128, D], F32)[:, :]}
        cT = sc[3][:, 0:128]
        kv = sc[3][:, 128:256]
        qb16 = rt_b
        qT2 = [P.sb("qT%d" % j, [64, 8, 128], BF16) for j in range(2)]
        gbs2 = [P.sb("gbs%d" % j, [128, 512], BF16) for j in range(2)]
        qib = at_b
        qiT2 = [P.sb("qiT%d" % j, [32, 8, 128], BF16) for j in range(2)]
        kb16 = P.sb("kb16", [128, 96], BF16)
        wis2 = [P.sb("wis%d" % j, [128, 8], F32) for j in range(2)]
        relu2 = [P.sb("relu%d" % j, [128, 512], F32) for j in range(2)]
        bis = P.sb("bis", [128, 8], F32)
        hw = P.sb("hw", [128, NIT], F32)
        sacc = P.sb("sacc", [128, NIT], F32)
        cacc = P.sb("cacc", [128, NIT], F32)
        junk = P.sb("junk", [128, T], BF16)
        mk = [P.sb("mk%d" % j, [128, 128], BF16) for j in range(2)]
        mkT = [P.sb("mkT%d" % j, [128, 4, 128], BF16) for j in range(2)]
        Eh = [[P.sb("E%d_%d" % (g, j), [128, 4, 128], BF16) for j in range(2)] for g in range(2)]
        rope_t = P.sb("rope_t", [128, 8, 16], F32)
        den = P.sb("den", [128, 8], F32)

        B0, B1, B2, B3, BI, BS, BA, B2d = self.B0, self.B1, self.B2, self.B3, self.BI, self.BS, self.BA, self.B2d
        groups = [(0, 512), (512, 1024), (1024, 1536), (1536, 1664), (1664, 2176), (2176, 2688), (2688, 3112), (3112, 3624)]
        O = 0
        c_ga, c_q, c_ckv, c_qi, c_ki, c_wi, c_gb = O, O + 512, O + 1024, O + 1152, O + 1408, O + 1440, O + 1448

        def rope(dst, src, nh, hd, half, ti, foff):
            cs = cosT[:, ti, None, foff:foff + half].bc([128, nh, half])
            sn = sinT[:, ti, None, foff:foff + half].bc([128, nh, half])
            x1 = src[:, :, 0:half]
            x2 = src[:, :, half:2 * half]
            t = rope_t
            P.tt(t[:, 0:nh, 0:half], x1, cs, ALU.mult)
            P.tt(t[:, 0:nh, 8:8 + half], x2, sn, ALU.mult)
            P.tt(dst[:, :, 0:half], t[:, 0:nh, 0:half], t[:, 0:nh, 8:8 + half], ALU.subtract)
            P.tt(t[:, 0:nh, 0:half], x2, cs, ALU.mult)
            P.tt(t[:, 0:nh, 8:8 + half], x1, sn, ALU.mult)
            P.tt(dst[:, :, half:2 * half], t[:, 0:nh, 0:half], t[:, 0:nh, 8:8 + half], ALU.add)
            P.copy(dst[:, :, 2 * half:hd], src[:, :, 2 * half:hd], eng="pool")

        H8 = lambda v: v.re("p (h d) -> p h d", h=8)

        def emit_A(ti):
            S = 128 * (ti + 1)
            tok = slice(ti * 128, (ti + 1) * 128)
            P.dma(sc[0][:, :], self.x[tok, 0:512], chan="xin0")
            P.dma(sc[1][:, :], self.x[tok, 512:1024], chan="xin1")
            P.copy(xb[:, 0:512], sc[0][:, :], eng="act")
            P.copy(xb[:, 512:1024], sc[1][:, :], eng="act")
            for c in range(8):
                P.tr(B2[:, c * 128:(c + 1) * 128], xb[:, c * 128:(c + 1) * 128], self.ident(128, True))
            P.copy(xT[:, :, :], B2[:, :].re("p (c t) -> p c t", c=8))
            if ti == 0:
                P.memset(xsT[:, :, 0:1], 0.0)
            else:
                P.copy(xsT[:, :, 0:1], xlast[:, :, None], eng="pool")
            P.copy(xsT[:, :, 1:128], xT[:, :, 0:127], eng="dve")
            P.copy(xlast[:, :], xT[:, :, 127], eng="pool")

            def project(gl, off, shifted=False):
                for (c0, c1) in gl:
                    bk = self.rbank()
                    wgb = wg[self.wgi % 2]
                    self.wgi += 1
                    _w = 16 if "wgsmall" in _SKIP else c1 - c0
                    P.add("sp", (lambda e, o=wgb[:, :, 0:_w].ap, i=wbf_dv[:, :, c0:c0 + _w]: e.dma_start(out=o, in_=i)),
                          writes=[wgb[:, :, :]], chan="wg%d" % (self.wgi % 2), extra_deps=wbf_cast)
                    for c in range(8):
                        P.mm(bk[:, 0:c1 - c0], xT[:, c, :], wgb[:, c, 0:c1 - c0], start=(c == 0), stop=(c == 7))
                    P.copy(p[:, c0 - off:c1 - off], bk[:, 0:c1 - c0], eng=("act" if ("evact" in _SKIP and S < 2048) else "dve"))
                    if shifted:
                        bk2 = self.rbank()
                        for c in range(8):
                            P.mm(bk2[:, 0:c1 - c0], xsT[:, c, :], wgb[:, c, 0:c1 - c0], start=(c == 0), stop=(c == 7))
                        P.tt(psh[:, c0:c1], bk2[:, 0:c1 - c0], p[:, c0:c1], ALU.subtract)
            project(groups[0:4], 0, shifted=True)
            if self.stop == 1:
                self.dump("p", p[:, :], [128, EVEN_IN])
                return
            P.tt(psh[:, :], psh[:, :], BCV("mu"), ALU.mult)
            P.tt(psh[:, :], psh[:, :], p[:, 0:1664], ALU.add)
            r_, k_, v_ = psh[:, 0:512], psh[:, 512:1024], psh[:, 1024:1536]
            project(groups[4:8], 1664)
            if ti == self.NT - 1 or self.stop == 2:
                self.dump("ps", psh[:, :], [128, 1664])
            if self.stop == 2:
                return
            P.act(thw[:, 0:64], psh[:, 1536:1600], AF.Tanh)
            P.copy(thw[:, 64:128], psh[:, 1600:1664], eng="pool")
            P.tr(B3[0:64, 0:128], thw[:, 0:64], self.ident(128))
            P.tr(B3[0:64, 128:256], thw[:, 64:128], self.ident(128))
            P.copy(thwT[0:64, :], B3[0:64, 0:256])
            bz = self.rbank()
            P.mm(bz[:, :], thwT[:, 0:128], w2a2[:, 0:512])
            lw = sc[1]
            P.act(lw[:, :], bz[:, :], AF.Tanh, scale=0.5)
            P.ts(lw[:, :], lw[:, :], -0.5 * float(np.exp(-0.5)), ALU.mult, -0.5 * float(np.exp(-0.5)), ALU.add)
            if self.stop == 30:
                self.dump("lw", lw[:, :], [128, 512])
                return
            ba = self.rbank()
            P.mm(ba[:, :], thwT[:, 128:256], w2a2[:, 512:1024])
            av = sc[2]
            P.act(av[:, :], ba[:, :], AF.Tanh, scale=0.5)
            P.ts(av[:, :], av[:, :], 0.5, ALU.mult, 0.5, ALU.add)
            bcs = self.rbank()
            P.mm(bcs[:, :], cst[:, C_TRIU + 128:C_TRIU + 256], lw[:, :])
            cum = sc[3]
            P.copy(cum[:, :], bcs[:, :])
            for h in range(8):
                P.mm(B3[0:64, 256 + h:257 + h], lw[:, h * 64:(h + 1) * 64], cst[:, C_ONES:C_ONES + 1])
            P.act(gC[:, :], B3[0:64, 256:264], AF.Exp)
            gam = sc[4]
            P.act(gam[:, :], cum[:, :], AF.Exp)
            ginv = sc[5]
            P.act(ginv[:, :], cum[:, :], AF.Exp, scale=-1.0)
            gprev = sc[0]
            P.tt(gprev[:, :], cum[:, :], lw[:, :], ALU.subtract)
            P.act(gprev[:, :], gprev[:, :], AF.Exp)
            if self.stop == 31:
                self.dump("cum", cum[:, :], [128, 512])
                self.dump("gC", gC[:, :], [64, 8])
                return
            kk = sc[7]
            P.tt(kk[:, :], k_, BCV("kks"), ALU.mult)
            t8 = sc[8]
            P.act(t8[:, :], kk[:, :], AF.Square)
            P.reduce(sm[:, 0:8], H8(t8[:, :]), ALU.add)
            P.ts(sm[:, 8:16], sm[:, 0:8], 1e-24, ALU.max)
            P.tt(sm[:, 16:24], sm[:, 8:16], cst[:, C_NH:C_NH + 8], ALU.pow, eng="pool")
            P.tt(H8(kk[:, :]), H8(kk[:, :]), sm[:, 16:24, None].bc([128, 8, 64]), ALU.mult)
            k2 = sc[8]
            _ke = "pool" if "k2pool" in _SKIP else "dve"
            P.ts(k2[:, :], av[:, :], -1.0, ALU.add, eng=_ke)
            P.tt(k2[:, :], k2[:, :], BCV("ka"), ALU.mult, eng=_ke)
            P.ts(k2[:, :], k2[:, :], 1.0, ALU.add, eng=_ke)
            P.tt(k2[:, :], k2[:, :], k_, ALU.mult, eng=_ke)
            P.tt(rt_b[:, :], r_, gam[:, :], ALU.mult)
            P.stt(at_b[:, :], kk[:, :], -1.0, gprev[:, :], ALU.mult, ALU.mult)
            P.tt(kk[:, :], kk[:, :], av[:, :], ALU.mult)
            P.tt(bt_b[:, :], kk[:, :], ginv[:, :], ALU.mult)
            P.tt(kt_b[:, :], k2[:, :], ginv[:, :], ALU.mult)
            P.copy(v_b[:, :], v_, eng="act")
            bon = sc[0]
            P.tt(bon[:, :], r_, k2[:, :], ALU.mult)
            P.tt(bon[:, :], bon[:, :], BCV("rk"), ALU.mult, eng="pool")
            P.reduce(sm[:, 24:32], H8(bon[:, :]), ALU.add)
            P.tt(H8(bon[:, :]), H8(v_), sm[:, 24:32, None].bc([128, 8, 64]), ALU.mult)
            for (src, dstf) in ((at_b, lambda h: arT[:, h, 0:128]), (rt_b, lambda h: arT[:, h, 128:256]),
                                (bt_b, lambda h: bT[:, h, :]), (kt_b, lambda h: kTt[:, h, :])):
                for h in range(8):
                    P.tr(B2[0:64, h * 128:(h + 1) * 128], src[:, h * 64:(h + 1) * 64], self.ident(128, True))
                if src is at_b:
                    P.copy(arT[:, :, 0:128], B2[0:64, :].re("p (h t) -> p h t", h=8))
                elif src is rt_b:
                    P.copy(arT[:, :, 128:256], B2[0:64, :].re("p (h t) -> p h t", h=8), eng="act")
                elif src is bt_b:
                    P.copy(bT[:, :, :], B2[0:64, :].re("p (h t) -> p h t", h=8))
                else:
                    P.copy(kTt[:, :, :], B2[0:64, :].re("p (h t) -> p h t", h=8), eng="act")
            if self.stop == 32:
                self.dump("arT", arT[:, :, :], [64, 8, 256])
                self.dump("bon", bon[:, :], [128, 512])
                return
            cv = sc[2]
            P.tt(cv[:, 0:128], p[:, c_ckv:c_ckv + 128], p[:, c_ckv:c_ckv + 128], ALU.mult, eng="pool")
            P.reduce(sm[:, 56:57], cv[:, 0:128], ALU.add)
            P.ts(sm[:, 56:57], sm[:, 56:57], 1.0 / 128, ALU.mult, 1e-6, ALU.add)
            P.tt(sm[:, 57:58], sm[:, 56:57], cst[:, C_NH:C_NH + 1], ALU.pow, eng="pool")
            P.ts(cv[:, 0:128], p[:, c_ckv:c_ckv + 128], sm[:, 57:58], ALU.mult)
            P.tt(cv[:, 0:128], cv[:, 0:128], BCV("kvg"), ALU.mult)
            P.tr(B3[:, 0:128], cv[:, 0:128], self.ident(128))
            P.copy(cT[:, :], B3[:, 0:128])
            bkv = self.rbank()
            P.mm(bkv[:, 0:128], cT[:, :], wukv[:, :])
            P.copy(kv[:, :], bkv[:, 0:128], eng="act")
            kr = sc[7]
            rope(kr[:, 0:64].re("p (h d) -> p h d", h=1), kv[:, 0:64].re("p (h d) -> p h d", h=1), 1, 64, 8, ti, 0)
            qr = sc[8]
            qir = sc[1]
            rope(H8(qr[:, :]), H8(p[:, c_q:c_q + 512]), 8, 64, 8, ti, 0)
            rope(qir[:, 0:256].re("p (h d) -> p h d", h=8), p[:, c_qi:c_qi + 256].re("p (h d) -> p h d", h=8), 8, 32, 4, ti, 8)
            kir = sc[4]
            rope(kir[:, 0:32].re("p (h d) -> p h d", h=1), p[:, c_ki:c_ki + 32].re("p (h d) -> p h d", h=1), 1, 32, 4, ti, 8)
            P.copy(kb16[:, 0:64], kr[:, 0:64])
            P.copy(kb16[:, 64:96], kir[:, 0:32])
            P.tr(B2[0:64, 0:128], kb16[:, 0:64], self.ident(128, True))
            P.tr(B2[0:32, 128:256], kb16[:, 64:96], self.ident(128, True))
            P.copy(kT.k(ti)[:, tok], B2[0:64, 0:128])
            P.copy(kiT.k(ti)[:, tok], B2[0:32, 128:256], eng="act")
            P.copy(Vaug.k(ti)[:, ti, 0:64], kv[:, 64:128], eng="act")
            P.ts(qb16[:, :], qr[:, :], 0.125, ALU.mult)
            for h in range(8):
                P.tr(B2[0:64, h * 128:(h + 1) * 128], qb16[:, h * 64:(h + 1) * 64], self.ident(128, True))
            P.copy(qT2[ti % 2][:, :, :], B2[0:64, :].re("p (h t) -> p h t", h=8))
            P.copy(qib[:, 0:256], qir[:, 0:256], eng="act")
            for h in range(8):
                P.tr(B2[0:32, h * 128:(h + 1) * 128], qib[:, h * 32:(h + 1) * 32], self.ident(128, True))
            P.copy(qiT2[ti % 2][:, :, :], B2[0:32, :].re("p (h t) -> p h t", h=8))
            P.ts(wis2[ti % 2][:, :], p[:, c_wi:c_wi + 8], 1.0 / 16.0, ALU.mult)
            P.act(gbs2[ti % 2][:, :], p[:, c_gb:c_gb + 512], AF.Tanh, scale=0.5)
            P.stt(gbs2[ti % 2][:, :], gbs2[ti % 2][:, :], 1.0, p[:, c_gb:c_gb + 512], ALU.add, ALU.mult)
            y = sc[3]
            G4 = lambda v: v.re("p (h t) -> p h t", h=4)
            triS = cst[:, None, C_TRIU:C_TRIU + 128].bc([128, 4, 128])
            triI = cst[:, None, C_TRIU + 128:C_TRIU + 256].bc([128, 4, 128])
            trilS = cst[:, None, C_TRIL:C_TRIL + 128].bc([128, 4, 128])
            idb = cst[:, None, C_ID:C_ID + 128].bc([128, 4, 128])
            for g2 in range(2):
                hg = [4 * g2 + i for i in range(4)]
                if "invf32" in _SKIP:
                    Pp = [G4(sc[2][:, :]), G4(sc[7][:, :])]
                    Qp = [G4(sc[8][:, :]), G4(sc[4][:, :])]
                    Mp = [Mt[0][:, :, :], Mt[1][:, :, :]]
                    rhs_v = rhs_f[:, :]
                    idm, idb_ = self.ident(128), idb
                else:
                    Pp = [G4(sc[2][:, 0:256].bitcast(BF16)), G4(sc[7][:, 0:256].bitcast(BF16))]
                    Qp = [G4(sc[8][:, 0:256].bitcast(BF16)), G4(sc[4][:, 0:256].bitcast(BF16))]
                    Mp = [G4(Mt[0][:, 0:2, :].re("p h t -> p (h t)").bitcast(BF16)), G4(Mt[1][:, 0:2, :].re("p h t -> p (h t)").bitcast(BF16))]
                    rhs_v = rhs_f[:, 0:128].bitcast(BF16)
                    idb_ = cstb[:, None, C_ID:C_ID + 128].bc([128, 4, 128])
                bk = self.rbank()
                for i, h in enumerate(hg):
                    P.mm(bk[:, i * 128:(i + 1) * 128], bT[:, h, :], arT[:, h, 0:128])
                P.tt(Qp[0], G4(bk[:, :]), triS, ALU.mult)
                bk = self.rbank()
                for i, h in enumerate(hg):
                    P.mm(bk[:, i * 128:(i + 1) * 128], arT[:, h, 0:128], bT[:, h, :])
                P.tt(Pp[0], G4(bk[:, :]), trilS, ALU.mult)
                P.tt(Mp[0], Qp[0], idb_, ALU.add, eng=("dve" if "mp0dve" in _SKIP else "pool"))
                bk = B3
                for i, h in enumerate(hg):
                    P.mm(bk[:, i * 128:(i + 1) * 128], bT[:, h, :], arT[:, h, 128:256])
                P.tt(LRb[:, :, :], G4(bk[:, :]), triI, ALU.mult)
                bk = self.rbank()
                for i, h in enumerate(hg):
                    P.mm(bk[:, i * 128:(i + 1) * 128], kTt[:, h, :], arT[:, h, 0:128])
                P.tt(LRk[:, :, 0:128], G4(bk[:, :]), triS, ALU.mult)
                bk = self.rbank()
                for i, h in enumerate(hg):
                    P.mm(bk[:, i * 128:(i + 1) * 128], kTt[:, h, :], arT[:, h, 128:256])
                P.tt(LRk[:, :, 128:256], G4(bk[:, :]), triI, ALU.mult)
                cp, cq, cm = 0, 0, 0
                for kstep in range(1, 1 if "inv" in _SKIP else 7):
                    bp = self.rbank()
                    for i in range(4):
                        P.mm(bp[:, i * 128:(i + 1) * 128], Qp[cq][:, i, :], Pp[cp][:, i, :])
                    if kstep < 6:
                        bq = self.rbank()
                        for i in range(4):
                            P.mm(bq[:, i * 128:(i + 1) * 128], Pp[cp][:, i, :], Qp[cq][:, i, :])
                    P.copy(Pp[1 - cp], G4(bp[:, :]))
                    if kstep < 6:
                        P.copy(Qp[1 - cq], G4(bq[:, :]), eng="act")
                    cp, cq = 1 - cp, 1 - cq
                    for i in range(4):
                        P.mm(B3[:, i * 128:(i + 1) * 128], Pp[cp][:, i, :], Mp[cm][:, i, :])
                    P.tt(Mp[1 - cm], G4(B3[:, :]), Mp[cm], ALU.add)
                    cm = 1 - cm
                br = self.rbank()
                for i, h in enumerate(hg):
                    P.mm(br[:, i * 64:(i + 1) * 64], arT[:, h, 0:128], Hb[:, h, :], start=True, stop=False)
                    P.mm(br[:, i * 64:(i + 1) * 64], LRk[:, i, 0:128], v_b[:, h * 64:(h + 1) * 64], start=False, stop=True)
                P.copy(rhs_v, br[:, 0:256], eng="act")
                bu = self.rbank()
                for i, h in enumerate(hg):
                    P.mm(bu[:, i * 64:(i + 1) * 64], Mp[cm][:, i, :], rhs_v[:, i * 64:(i + 1) * 64])
                P.copy(Ub[:, 4 * g2:4 * g2 + 4, :], bu[:, 0:256].re("p (h v) -> p h v", h=4))
                by = self.rbank()
                for i, h in enumerate(hg):
                    hs = slice(h * 64, (h + 1) * 64)
                    P.mm(by[:, i * 64:(i + 1) * 64], arT[:, h, 128:256], Hb[:, h, :], start=True, stop=False)
                    P.mm(by[:, i * 64:(i + 1) * 64], LRb[:, i, :], Ub[:, h, :], start=False, stop=False)
                    P.mm(by[:, i * 64:(i + 1) * 64], LRk[:, i, 128:256], v_b[:, hs], start=False, stop=True)
                P.copy(y[:, g2 * 256:(g2 + 1) * 256], by[:, 0:256], eng="act")
                bh = self.rbank()
                for i, h in enumerate(hg):
                    hs = slice(h * 64, (h + 1) * 64)
                    P.mm(bh[0:64, i * 64:(i + 1) * 64], bt_b[:, hs], Ub[:, h, :], start=True, stop=False)
                    P.mm(bh[0:64, i * 64:(i + 1) * 64], kt_b[:, hs], v_b[:, hs], start=False, stop=True)
                Hg = Hf[:, 4 * g2:4 * g2 + 4, :]
                P.tt(Hg, Hg, bh[0:64, 0:256].re("p (h v) -> p h v", h=4), ALU.add)
                P.tt(Hg, Hg, gC[:, 4 * g2:4 * g2 + 4, None].bc([64, 4, 64]), ALU.mult)
                P.copy(Hb[:, 4 * g2:4 * g2 + 4, :], Hg, eng="act")
            y = sc[3]
            if ti == self.NT - 1 or self.stop == 3:
                self.dump("yraw", y[:, :], [128, 512])
            if self.stop == 3:
                return
            P.reduce(sm[:, 32:40], H8(y[:, :]), ALU.add)
            ysq = sc[1]
            P.act(ysq[:, :], y[:, :], AF.Square)
            P.reduce(sm[:, 40:48], H8(ysq[:, :]), ALU.add)
            P.ts(sm[:, 32:40], sm[:, 32:40], 1.0 / 64, ALU.mult)
            P.tt(sm[:, 48:56], sm[:, 32:40], sm[:, 32:40], ALU.mult)
            P.stt(sm[:, 40:48], sm[:, 40:48], 1.0 / 64, sm[:, 48:56], ALU.mult, ALU.subtract)
            P.ts(sm[:, 40:48], sm[:, 40:48], 64e-5, ALU.add)
            P.tt(sm[:, 48:56], sm[:, 40:48], cst[:, C_NH:C_NH + 8], ALU.pow, eng="pool")
            P.tt(H8(y[:, :]), H8(y[:, :]), sm[:, 32:40, None].bc([128, 8, 64]), ALU.subtract)
            P.tt(H8(y[:, :]), H8(y[:, :]), sm[:, 48:56, None].bc([128, 8, 64]), ALU.mult)
            P.tt(y[:, :], y[:, :], BCV("gng"), ALU.mult)
            P.tt(y[:, :], y[:, :], BCV("gnb"), ALU.add, eng="pool")
            P.tt(y[:, :], y[:, :], bon[:, :], ALU.add)
            ga = sc[1]
            P.act(ga[:, :], p[:, c_ga:c_ga + 512], AF.Tanh, scale=0.5)
            P.stt(ga[:, :], ga[:, :], 1.0, p[:, c_ga:c_ga + 512], ALU.add, ALU.mult)
            P.stt(ycb2[ti % 2][:, 0:512], y[:, :], 0.5, ga[:, :], ALU.mult, ALU.mult)


        def emit_D(ti):
            S = 128 * (ti + 1)
            tok = slice(ti * 128, (ti + 1) * 128)
            qT, qiT, wis = qT2[ti % 2], qiT2[ti % 2], wis2[ti % 2]
            for half in range(2):
                P.add("sp", (lambda e, o=wob2[half][:, :, :].ap, i=wob_dv[:, :, half * 512:(half + 1) * 512]: e.dma_start(out=o, in_=i)),
                      writes=[wob2[half][:, :, :]], chan="wobld%d" % half, extra_deps=wob_cast)
            P.dma(xl[:, :], self.x[tok, :], chan="xl")
            nch = (S + 511) // 512
            for n in range(1 if "idx" in _SKIP else nch):
                k0 = n * 512
                k1 = min(S, k0 + 512)
                w = k1 - k0
                kkeys = range(k0 // 128, k1 // 128)
                scn = score.k(n)
                for h in range(8):
                    bi = BI if h % 2 == 0 else BS
                    P.add("pe", (lambda e, o=bi[:, 0:w].ap, l=qiT[:, h, :].ap, r=kiT[:, k0:k1].ap: e.matmul(o, l, r, start=True, stop=True)),
                          reads=[qiT[:, h, :]] + [kiT.k(j)[:, j * 128:(j + 1) * 128] for j in kkeys], writes=[bi[:, 0:w]])
                    rl = relu2[h % 2]
                    P.act(rl[:, 0:w], bi[:, 0:w], AF.Relu)
                    if h == 0:
                        P.ts(scn[:, k0:k1], rl[:, 0:w], wis[:, 0:1], ALU.mult, eng="pool")
                    else:
                        P.stt(scn[:, k0:k1], rl[:, 0:w], wis[:, h:h + 1], scn[:, k0:k1], ALU.mult, ALU.add)
            use_topk = S > self.topk
            if use_topk:
                P.add("dve", (lambda e, o=bis[:, 0:1].ap, i=score[:, 0:S].ap: e.tensor_reduce(o, i, AX.X, ALU.max, apply_absolute_value=True)),
                      reads=[score[:, 0:S]], writes=[bis[:, 0:1]])
            P.memset(score[0:64, S - 64:S], NEG_BIG)
            if use_topk:
                P.ts(bis[:, 1:2], bis[:, 0:1], -1.0, ALU.mult, -1.0, ALU.add)
                P.ts(bis[:, 2:3], bis[:, 0:1], 2.0, ALU.mult, 1.0, ALU.add)
                P.tt(hw[:, :], cst[:, C_POW:C_POW + NIT], bis[:, 2:3].bc([128, NIT]), ALU.mult)
                P.memset(sacc[:, :], 0.0)
                P.tt(bis[:, 3:4], bis[:, 1:2], hw[:, 0:1], ALU.add)
                nit = 0 if "topk" in _SKIP else NIT
                S1 = (int(S * float(_os.environ.get("KS1", "0.42"))) // 128) * 128
                if "s1thr" in _SKIP and S < 1536:
                    S1 = 0
                n2 = S - S1
                for k in range(nit):
                    P.act(junk.k("a")[:, S1:S], score[:, S1:S], AF.Sign, bias=bis[:, 3:4], scale=-1.0, accum=sacc[:, k:k + 1])
                    if S1 > 0:
                        P.ts(junk.k("d")[:, 0:S1], score[:, 0:S1], bis[:, 3:4], ALU.is_gt, 0.0, ALU.add, accum=cacc[:, k:k + 1])
                        P.stt(bis[:, 5:6], cacc[:, k:k + 1], -2.0, sacc[:, k:k + 1], ALU.mult, ALU.add)
                        tv = bis[:, 5:6]
                    else:
                        tv = sacc[:, k:k + 1]
                    P.ts(bis[:, 4:5], tv, float(n2 - 2 * self.topk), ALU.is_le, hw[:, k:k + 1], ALU.mult)
                    P.tt(bis[:, 1:2], bis[:, 1:2], bis[:, 4:5], ALU.add)
                    if k + 1 < nit:
                        P.tt(bis[:, 3:4], bis[:, 1:2], hw[:, k + 1:k + 2], ALU.add)
            if use_topk:
                P.ts(junk[:, 0:S], score[:, 0:S], bis[:, 1:2], ALU.is_le, -30000.0, ALU.mult)
            else:
                P.ts(junk[:, 0:S], score[:, 0:S], -1.0e29, ALU.is_le, -30000.0, ALU.mult)
            nj = ti + 1
            for j in range(1 if "attn" in _SKIP else nj):
                ks = slice(j * 128, (j + 1) * 128)
                P.tr(B2d[:, 0:128], junk[:, ks], self.ident(128, True))
                mT = mkT[j % 2]
                P.copy(mT[:, :, :], B2d[:, None, 0:128].bc([128, 4, 128]), eng="act")
                for g in range(2):
                    bs_ = BI if g == 0 else BS
                    P.mm(bs_[:, 0:512], kT.k(j)[:, ks], qT[:, 4 * g:4 * g + 4, :].re("p h t -> p (h t)"), start=True, stop=False)
                    P.mm(bs_[:, 0:512], self.ident(128, True), mT[:, :, :].re("p h t -> p (h t)"), start=False, stop=True)
                    e_ = Eh[g][j % 2]
                    P.act(e_[:, :, :].re("p h t -> p (h t)"), bs_[:, 0:512], AF.Exp)
                    for hh in range(4):
                        o0 = g * 512 + hh * 65
                        P.mm(BA[:, o0:o0 + 65], e_[:, hh, :], Vaug.k(j)[:, j, :], start=(j == 0 and hh == 0),
                             stop=((j == nj - 1 or "attn" in _SKIP) and hh == 3))
            accv = BA[:, :].re("p (g c) -> p g c", g=2)[:, :, 0:260].re("p g (h c) -> p g h c", h=4)
            for g in range(2):
                P.copy(den[:, g * 4:(g + 1) * 4], accv[:, g, :, 64])
            P.recip(den[:, :], den[:, :])
            ycb = ycb2[ti % 2]
            for g in range(2):
                P.tt(ycb[:, 512 + g * 256:512 + (g + 1) * 256].re("p (h d) -> p h d", h=4), accv[:, g, :, 0:64],
                     den[:, g * 4:(g + 1) * 4, None].bc([128, 4, 64]), ALU.mult)
            P.stt(ycb[:, 512:1024], ycb[:, 512:1024], 0.5, gbs2[ti % 2][:, :], ALU.mult, ALU.mult)
            for c in range(8):
                P.tr(B2d[:, c * 128:(c + 1) * 128], ycb[:, c * 128:(c + 1) * 128], self.ident(128, True))
            P.copy(yT_d[:, :, :], B2d[:, :].re("p (c t) -> p c t", c=8))
            for half in range(2):
                bo = BI if half == 0 else BS
                for c in range(8):
                    P.mm(bo[:, :], yT_d[:, c, :], wob2[half][:, c, :], start=(c == 0), stop=(c == 7))
            i = self.layernorm_out(BI[:, :], BS[:, :], xl, None, None, self.h1[tok, :], lnt_d, "h1out")
            self.finals.append(i)

        self.wgi = 0
        for ti in range(NT + 1):
            a_ops = P.begin_stream()
            if ti < NT:
                emit_A(ti)
            if self.w1_d is not None:
                jobs = [(0, c) for c in range(8)] + [(1, c) for c in range(8)]
                if NT >= 20:
                    todo = [jobs[ti - 2]] if 2 <= ti < 18 else []
                else:
                    todo = jobs if ti == 0 else []
                for (w, c) in todo:
                    src = (self.w_in1 if w == 0 else self.w_out1)[c * 128:(c + 1) * 128, :]
                    self.finals.append(P.add("pool", (lambda e, o=self.w1_d[w][c * 128:(c + 1) * 128, :], i=src:
                                                      e.dma_start(out=o, in_=i)), chan="w1c%d" % (c % 2)))
            P.end_stream()
            d_ops = P.begin_stream()
            if ti >= 1:
                emit_D(ti - 1)
            P.end_stream()
            if "nomerge" in _SKIP:
                P.cur.extend(a_ops)
                P.cur.extend(d_ops)
            elif "propmerge" in _SKIP:
                P.merge(a_ops, d_ops)
            else:
                P.prio_delta = float(_os.environ.get("KDELTA", "1.0"))
                P.hop_x = float(_os.environ.get("KHOPX", "0.7"))
                P.hop_same = float(_os.environ.get("KHOPS", "0.15"))
                P.merge_sched(a_ops, d_ops)

    def phase_b(self):
        P = self.P
        T, NT = self.T, self.NT
        cst, cstb = self.cst, self.cstb
        wbf = P.sb("wbf1", [128, 8, ODD_IN], BF16)
        wob = P.sb("wob1", [128, 8, D], BF16)
        if self.w1_d is not None:
            for c in range(8):
                P.dma(wbf[:, c, :], self.w1_d[0][c * 128:(c + 1) * 128, :], chan="wl%d" % (c % 4), q="sp")
            for c in range(8):
                P.dma(wob[:, c, :], self.w1_d[1][c * 128:(c + 1) * 128, :], chan="wl%d" % (c % 4), q="sp")
        else:
            for c in range(8):
                P.dma(wbf[:, c, :], self.w_in1[c * 128:(c + 1) * 128, :], chan="wl%d" % (c % 2), q="pool")
            for c in range(8):
                P.dma(wob[:, c, :], self.w_out1[c * 128:(c + 1) * 128, :], chan="wl%d" % (c % 2), q="pool")
        bc = P.sb("bc1", [128, NBC1], F32)
        P.dma(bc[:, :], self.bcv1[0].partition_broadcast(128), chan="bc1")
        BCV = lambda n: bc[:, BC1[n][0]:BC1[n][0] + BC1[n][1]]
        g2 = P.sb("g2", [16, 512], F32)
        P.dma(g2[:, :], self.g2, chan="g2")
        Sf = P.sb("Sf", [128, 4, 256], F32)
        Sb = P.sb("Sb", [128, 4, 256], BF16)
        P.memset(Sf[:, :, :], 0.0)
        P.memset(Sb[:, :, :], 0.0)
        xt2 = [P.sb("xt%d" % j, [128, D], F32) for j in range(2)]
        xb = P.sb("xb", [128, D], BF16)
        xT = P.sb("xT", [128, 8, 128], BF16)
        p = P.sb("p", [128, ODD_IN], F32)
        sc = [P.sb("scr%d" % j, [128, 512], F32) for j in range(4)]
        sm = P.sb("sm", [128, 32], F32)
        gdT = P.sb("gdT", [16, 128], F32)
        qe_b = P.sb("qe_b", [128, 512], BF16)
        ke_b = P.sb("ke_b", [128, 512], BF16)
        kd_b2 = [P.sb("kd_b%d" % j, [128, 512], BF16) for j in range(2)]
        v_b2 = [P.sb("v_b%d" % j, [128, 1024], BF16) for j in range(2)]
        qeT2 = [P.sb("qeT%d" % j, [128, 4, 128], BF16) for j in range(2)]
        keT2 = [P.sb("keT%d" % j, [128, 4, 128], BF16) for j in range(2)]
        gS2 = [P.sb("gS%d" % j, [128, 4], F32) for j in range(2)]
        gc2 = [P.sb("gc%d" % j, [128, D], BF16) for j in range(2)]
        attT = [P.sb("attT%d" % j, [128, 128], BF16) for j in range(2)]
        o = P.sb("o", [128, D], F32)
        osq = P.sb("osq", [128, D], F32)
        ycb = P.sb("ycb", [128, D], BF16)
        yT = P.sb("yT", [128, 8, 128], BF16)
        ln_st = P.sb("ln_st", [128, 8], F32)
        B0, B1, B2, B3, B56, B78 = self.B0, self.B1, self.B2, self.B3, self.B56, self.B78
        B78b = B78[:, 0:512].bitcast(BF16)
        groups = [(0, 512), (512, 1024), (1024, 1536), (1536, 2048), (2048, 2064), (2064, 2576), (2576, 3088)]
        H4 = lambda v, h=4: v.re("p (h d) -> p h d", h=h)

        def emit_B1(ti):
            tok = slice(ti * 128, (ti + 1) * 128)
            xt = xt2[ti % 2]
            kd_b, v_b, qeT, keT, gS, gc = kd_b2[ti % 2], v_b2[ti % 2], qeT2[ti % 2], keT2[ti % 2], gS2[ti % 2], gc2[ti % 2]
            P.dma(xt[:, :], self.h1[tok, :], chan="xin")
            P.tt(xt[:, :], xt[:, :], BCV("lng0"), ALU.mult)
            P.tt(xt[:, :], xt[:, :], BCV("lnb0"), ALU.add, eng="pool")
            P.copy(xb[:, :], xt[:, :], eng="act")
            for c in range(8):
                P.tr(B2[:, c * 128:(c + 1) * 128], xb[:, c * 128:(c + 1) * 128], self.ident(128, True))
            P.copy(xT[:, :, :], B2[:, :].re("p (c t) -> p c t", c=8))
            for (c0, c1) in groups:
                bk = self.rbank()
                for c in range(8):
                    P.mm(bk[:, 0:c1 - c0], xT[:, c, :], wbf[:, c, c0:c1], start=(c == 0), stop=(c == 7))
                P.copy(p[:, c0:c1], bk[:, 0:c1 - c0], eng="act")
            q_, k_, v_, gd_, gcc = p[:, 0:512], p[:, 512:1024], p[:, 1024:2048], p[:, 2048:2064], p[:, 2064:3088]
            bt_ = self.rbank()
            P.tr(bt_[0:16, 0:128], gd_, self.ident(128))
            P.copy(gdT[:, :], bt_[0:16, 0:128])
            bz = self.rbank()
            P.mm(bz[:, :], gdT[:, :], g2[:, :])
            la = sc[0]
            P.tt(la[:, :], bz[:, :], BCV("gbias"), ALU.add)
            P.act(la[:, :], la[:, :], AF.Tanh, scale=0.5)
            P.ts(la[:, :], la[:, :], 0.5, ALU.mult, 0.5, ALU.add)
            P.act(la[:, :], la[:, :], AF.Ln)
            P.ts(la[:, :], la[:, :], 1.0 / 16.0, ALU.mult)
            bcs = self.rbank()
            P.mm(bcs[:, :], cst[:, C_TRIU + 128:C_TRIU + 256], la[:, :])
            cum = sc[1]
            P.copy(cum[:, :], bcs[:, :])
            bl = self.rbank()
            P.mm(bl[:, :], cst[:, C_ONES:C_ONES + 128], la[:, :])
            dlt = sc[2]
            P.tt(dlt[:, :], bl[:, :], cum[:, :], ALU.subtract)
            P.act(dlt[:, :], dlt[:, :], AF.Exp)
            P.tt(kd_b[:, :], k_, dlt[:, :], ALU.mult)
            bg = self.rbank()
            for h in range(4):
                P.mm(bg[:, 256 + h:257 + h], la[:, h * 128:(h + 1) * 128], cst[:, C_ONES:C_ONES + 1])
            P.act(gS[:, :], bg[:, 256:260], AF.Exp)
            eb = sc[3]
            P.act(eb[:, :], cum[:, :], AF.Exp)
            P.stt(qe_b[:, :], q_, float(128.0 ** -0.5), eb[:, :], ALU.mult, ALU.mult)
            P.act(eb[:, :], cum[:, :], AF.Exp, scale=-1.0)
            P.tt(ke_b[:, :], k_, eb[:, :], ALU.mult)
            P.copy(v_b[:, :], v_, eng="act")
            P.act(gc[:, :], gcc, AF.Tanh, scale=0.5)
            P.stt(gc[:, :], gc[:, :], 1.0, gcc, ALU.add, ALU.mult)
            for h in range(4):
                P.tr(B2[:, h * 128:(h + 1) * 128], qe_b[:, h * 128:(h + 1) * 128], self.ident(128, True))
                P.tr(B2[:, 512 + h * 128:512 + (h + 1) * 128], ke_b[:, h * 128:(h + 1) * 128], self.ident(128, True))
            P.copy(qeT[:, :, :], B2[:, 0:512].re("p (h t) -> p h t", h=4))
            P.copy(keT[:, :, :], B2[:, 512:1024].re("p (h t) -> p h t", h=4), eng="act")

        def emit_B2(ti):
            tok = slice(ti * 128, (ti + 1) * 128)
            xt = xt2[ti % 2]
            kd_b, v_b, qeT, keT, gS, gc = kd_b2[ti % 2], v_b2[ti % 2], qeT2[ti % 2], keT2[ti % 2], gS2[ti % 2], gc2[ti % 2]
            for h in range(4):
                P.mm(B3[:, 0:128], keT[:, h, :], qeT[:, h, :])
                at = attT[h % 2]
                P.tt(at[:, :], B3[:, 0:128], cst[:, C_TRIU + 128:C_TRIU + 256], ALU.mult)
                vs = slice(h * 256, (h + 1) * 256)
                P.mm(B78[:, vs], at[:, :], v_b[:, vs], start=True, stop=False)
                P.mm(B78[:, vs], qeT[:, h, :], Sb[:, h, :], start=False, stop=True)
                P.mm(B56[:, vs], kd_b[:, h * 128:(h + 1) * 128], v_b[:, vs], start=True, stop=True)
            P.copy(o[:, :], B78[:, :], eng="act")
            P.tt(Sf[:, :, :], Sf[:, :, :], gS[:, :, None].bc([128, 4, 256]), ALU.mult)
            P.tt(Sf[:, :, :], Sf[:, :, :], B56[:, :].re("p (h v) -> p h v", h=4), ALU.add)
            P.copy(Sb[:, :, :], Sf[:, :, :], eng="act")
            if ti == self.NT - 1:
                self.dump("gla_o", o[:, :], [128, 1024])
            P.act(osq[:, :], o[:, :], AF.Square)
            P.reduce(sm[:, 0:4], H4(osq[:, :]), ALU.add)
            P.ts(sm[:, 0:4], sm[:, 0:4], 1.0 / 256, ALU.mult, 1e-6, ALU.add)
            P.tt(sm[:, 8:12], sm[:, 0:4], cst[:, C_NH:C_NH + 4], ALU.pow, eng="pool")
            P.tt(H4(o[:, :]), H4(o[:, :]), sm[:, 8:12, None].bc([128, 4, 256]), ALU.mult)
            P.tt(H4(o[:, :]), H4(o[:, :]), BCV("normg")[:, None, :].bc([128, 4, 256]), ALU.mult, eng="pool")
            P.stt(ycb[:, :], o[:, :], 0.5, gc[:, :], ALU.mult, ALU.mult)
            for c in range(8):
                P.tr(B78b[:, c * 128:(c + 1) * 128], ycb[:, c * 128:(c + 1) * 128], self.ident(128, True))
            P.copy(yT[:, :, :], B78b[:, :].re("p (c t) -> p c t", c=8))
            for half in range(2):
                for c in range(8):
                    P.mm(B56[:, half * 512:(half + 1) * 512], yT[:, c, :], wob[:, c, half * 512:(half + 1) * 512],
                         start=(c == 0), stop=(c == 7))
            lnt = {"r": xt[:, :], "st": ln_st, "sq": osq[:, :]}
            i = self.layernorm_out(B56[:, 0:512], B56[:, 512:1024], xt, BCV("lng"), BCV("lnb"), self.out[tok, :], lnt, "yout")
            self.finals.append(i)

        for ti in range(NT + 1):
            a_ops = P.begin_stream()
            if ti < NT:
                emit_B1(ti)
            P.end_stream()
            d_ops = P.begin_stream()
            if ti >= 1:
                emit_B2(ti - 1)
            P.end_stream()
            if "nomergeb" in _SKIP:
                P.cur.extend(a_ops)
                P.cur.extend(d_ops)
            else:
                P.prio_delta = float(_os.environ.get("KDELTAB", "-1.0"))
                P.merge_sched(a_ops, d_ops)


BC1 = {}
_o = 0
for _n, _l in (("gbias", 512), ("normg", 256), ("lng", 1024), ("lnb", 1024), ("lng0", 1024), ("lnb0", 1024)):
    BC1[_n] = (_o, _l)
    _o += _l
NBC1 = _o


def prep_inputs(inp, b, T):
    f = lambda a: np.ascontiguousarray(np.asarray(a, dtype=np.float32))
    bcv0 = np.concatenate([f(inp["a_mu"][0]), f(inp["a_w0"][0]), f(inp["a_a0"][0]), f(inp["a_kk_scale"][0]),
                           f(inp["a_ka"][0]), f(inp["a_rk"][0]).reshape(-1), f(inp["a_gn_g"][0]), f(inp["a_gn_b"][0]),
                           f(inp["b_kv_norm_g"][0])])[None, :]
    bcv1 = np.concatenate([f(inp["c_g_bias"][0]), f(inp["c_norm_g"][0]), f(inp["ln_g"][1]), f(inp["ln_b"][1]),
                           f(inp["ln_g"][0]), f(inp["ln_b"][0])])[None, :]
    pos = np.ascontiguousarray(np.asarray(inp["positions"][b], dtype=np.int32).reshape(T // 128, 128).T)
    return {
        "x": f(inp["x"][b]),
        "pos": pos,
        "consts": make_consts(),
        "w_in0": f(inp["e_w_in"][0]),
        "w_out0": f(inp["e_w_out"][0]),
        "bcv0": f(bcv0),
        "w2a2": f(np.concatenate([inp["a_w2"][0], inp["a_a2"][0]], axis=1)),
        "wukv": f(np.concatenate([inp["b_wuk"][0], inp["b_wuv"][0]], axis=1)),
        "w_in1": f(inp["o_w_in"][0]),
        "w_out1": f(inp["o_w_out"][0]),
        "bcv1": f(bcv1),
        "g2": f(inp["c_g2"][0]),
    }


_T = 4096
_NB = 8


def kernel(**inputs):
    T = _T
    kb = K(T, min(256, T // 4))
    nc = kb.build()
    in_maps = [prep_inputs(inputs, b, T) for b in range(_NB)]
    res = run_bass_kernel_spmd(nc, in_maps, core_ids=list(range(_NB)))
    out = np.stack([np.asarray(r["out"], dtype=np.float32) for r in res.results], axis=0)
    return out
```

```python
import numpy as np
import concourse.bass as bass
import concourse.mybir as mybir
from contextlib import ExitStack

F32 = mybir.dt.float32
BF16 = mybir.dt.bfloat16
I32 = mybir.dt.int32
AF = mybir.ActivationFunctionType
ALU = mybir.AluOpType
AX = mybir.AxisListType

SEM_WRAP = 30000


class Buf:
    def __init__(self, t, name):
        self.t = t
        self.name = name
        self.W = None
        self.R = []
        self.keys = {}

    def __getitem__(self, idx):
        return View(self, None, self.t[idx])

    def k(self, key):
        return _Keyed(self, key)


class _Keyed:
    def __init__(self, buf, key):
        self.buf = buf
        self.key = key

    def __getitem__(self, idx):
        return View(self.buf, self.key, self.buf.t[idx])


class View:
    def __init__(self, buf, key, ap):
        self.buf = buf
        self.key = key
        self.ap = ap

    def __getitem__(self, idx):
        return View(self.buf, self.key, self.ap[idx])

    def re(self, s, **kw):
        return View(self.buf, self.key, self.ap.rearrange(s, **kw))

    def bc(self, shape):
        return View(self.buf, self.key, self.ap.to_broadcast(shape))

    def bitcast(self, dt):
        return View(self.buf, self.key, self.ap.bitcast(dt))


class Op:
    __slots__ = ("eng", "fn", "deps", "idx", "sig", "count", "chan", "chcount", "waits", "tag")


class Chan:
    def __init__(self, sem):
        self.sem = sem
        self.n = 0
        self.last = None


class Prog:
    ENGS = ("pe", "act", "dve", "pool", "sp")

    def __init__(self, nc, stack, prefix=""):
        self.prefix = prefix
        self.nc = nc
        self.stack = stack
        self.ops = []
        self.cur = self.ops
        self.bufs = []
        self.nsb = 0
        self.chans = {}

    def sb(self, name, shape, dt):
        t = self.stack.enter_context(self.nc.sbuf_tensor(self.prefix + name, list(shape), dt))
        b = Buf(t, name)
        self.bufs.append(b)
        return b

    def ps(self, name, shape, dt):
        t = self.stack.enter_context(self.nc.psum_tensor(self.prefix + name, list(shape), dt))
        b = Buf(t, name)
        b.psum = True
        self.bufs.append(b)
        return b

    def alias(self, buf, t, name):
        raise NotImplementedError

    def chan(self, name):
        if name not in self.chans:
            sem = self.stack.enter_context(self.nc.semaphore(self.prefix + "c_" + name))
            self.chans[name] = Chan(sem)
        return self.chans[name]

    def _track(self, idx, reads, writes, st=None):
        if st is not None:
            return self._track_st(idx, reads, writes, st)
        deps = set()
        for v in reads:
            b, k = v.buf, v.key
            if b.W is not None:
                deps.add(b.W)
            if k is None:
                for kk, (w, r) in b.keys.items():
                    if w is not None:
                        deps.add(w)
            else:
                if k in b.keys and b.keys[k][0] is not None:
                    deps.add(b.keys[k][0])
        for v in writes:
            b, k = v.buf, v.key
            if b.W is not None:
                deps.add(b.W)
            deps.update(b.R)
            if k is None:
                for kk, (w, r) in b.keys.items():
                    if w is not None:
                        deps.add(w)
                    deps.update(r)
            else:
                if k in b.keys:
                    w, r = b.keys[k]
                    if w is not None:
                        deps.add(w)
                    deps.update(r)
        for v in reads:
            b, k = v.buf, v.key
            if k is None:
                b.R.append(idx)
            else:
                b.keys.setdefault(k, [None, []])[1].append(idx)
        for v in writes:
            b, k = v.buf, v.key
            if k is None:
                b.W = idx
                b.R = []
                b.keys = {}
            else:
                b.keys[k] = [idx, []]
        deps.discard(idx)
        return deps

    def _track_st(self, idx, reads, writes, st):
        class _R:
            __slots__ = ("W", "R", "keys")
        deps = set()

        def rec(b):
            r = st.get(id(b))
            if r is None:
                r = _R()
                r.W = None
                r.R = []
                r.keys = {}
                st[id(b)] = r
            return r
        for v in reads:
            b, k = rec(v.buf), v.key
            if b.W is not None:
                deps.add(b.W)
            if k is None:
                for kk, (w, r) in b.keys.items():
                    if w is not None:
                        deps.add(w)
            elif k in b.keys and b.keys[k][0] is not None:
                deps.add(b.keys[k][0])
        for v in writes:
            b, k = rec(v.buf), v.key
            if b.W is not None:
                deps.add(b.W)
            deps.update(b.R)
            if k is None:
                for kk, (w, r) in b.keys.items():
                    if w is not None:
                        deps.add(w)
                    deps.update(r)
            elif k in b.keys:
                w, r = b.keys[k]
                if w is not None:
                    deps.add(w)
                deps.update(r)
        for v in reads:
            b, k = rec(v.buf), v.key
            if k is None:
                b.R.append(idx)
            else:
                b.keys.setdefault(k, [None, []])[1].append(idx)
        for v in writes:
            b, k = rec(v.buf), v.key
            if k is None:
                b.W = idx
                b.R = []
                b.keys = {}
            else:
                b.keys[k] = [idx, []]
        deps.discard(idx)
        return deps

    @staticmethod
    def _cost(op):
        reads, writes, chan, extra = op.deps
        n = 1
        v = writes[0] if writes else (reads[0] if reads else None)
        if v is not None:
            try:
                shp = list(v.ap.shape)
                n = 1
                for d in shp[1:]:
                    n *= int(d)
            except Exception:
                n = 64
        if chan is not None:
            return 0.05, 2.5
        e = op.eng
        if e == "pe":
            f32 = False
            try:
                f32 = (reads[0].ap.dtype == F32)
            except Exception:
                pass
            d = 0.04 + n / 2400.0 * (4.0 if f32 else 1.0)
        elif e == "act":
            d = 0.2 + n / 1200.0
        elif e == "dve":
            d = 0.08 + n / 960.0
        elif e == "pool":
            d = 0.15 + n / 350.0
        else:
            d = 0.05
        return d, d

    prio_delta = 0.0
    hop_same = 0.15
    hop_x = 0.7

    def merge_sched(self, a, b):
        streams = [a, b]
        sts = [{}, {}]
        fin = [{}, {}]
        ptr = [0, 0]
        efree = getattr(self, "_efree", None)
        if efree is None:
            efree = {e: 0.0 for e in self.ENGS}
        base = max(efree.values()) if False else min(efree.values())
        cache = [None, None]

        def est(si):
            if cache[si] is not None:
                return cache[si]
            i = ptr[si]
            op = streams[si][i]
            reads, writes, chan, extra = op.deps
            deps = self._track_st(i, reads, writes, sts[si]) if False else None
            return None
        ldeps = []
        for si in range(2):
            dl = []
            stt_ = {}
            for i, op in enumerate(streams[si]):
                reads, writes, chan, extra = op.deps
                dl.append(self._track_st(i, reads, writes, stt_))
            ldeps.append(dl)
        start_of = [None, None]

        def earliest(si):
            i = ptr[si]
            op = streams[si][i]
            t = efree[op.eng]
            for d in ldeps[si][i]:
                pe = streams[si][d].eng
                hop = 0.0 if (pe == "pe" and op.eng == "pe") else (self.hop_same if pe == op.eng else self.hop_x)
                t = max(t, fin[si][d] + hop)
            return t
        while ptr[0] < len(a) or ptr[1] < len(b):
            cands = [si for si in range(2) if ptr[si] < len(streams[si])]
            if len(cands) == 2:
                e0, e1 = earliest(0), earliest(1)
                si = 1 if e1 <= e0 + self.prio_delta else 0
                t = e1 if si == 1 else e0
            else:
                si = cands[0]
                t = earliest(si)
            i = ptr[si]
            op = streams[si][i]
            occ, lat = self._cost(op)
            efree[op.eng] = t + occ
            fin[si][i] = t + lat
            self.cur.append(op)
            ptr[si] += 1
        self._efree = efree

    def add(self, eng, fn, reads=(), writes=(), chan=None, extra_deps=(), tag=None):
        op = Op()
        op.eng = eng
        op.fn = fn
        op.idx = None
        op.sig = False
        op.count = None
        op.chan = None
        op.chcount = None
        op.waits = []
        op.tag = tag
        reads = [v for v in reads if v is not None]
        writes = [v for v in writes if v is not None]
        writes = writes + [v for v in reads if getattr(v.buf, "psum", False)]
        reads = [v for v in reads if not getattr(v.buf, "psum", False)]
        op.deps = (reads, writes, chan, [d for d in extra_deps if d is not None])
        self.cur.append(op)
        return op

    def begin_stream(self):
        lst = []
        self._saved = getattr(self, "_saved", [])
        self._saved.append(self.cur)
        self.cur = lst
        return lst

    def end_stream(self):
        self.cur = self._saved.pop()

    def merge(self, a, b):
        na, nb = len(a), len(b)
        ia = ib = 0
        while ia < na or ib < nb:
            if ib >= nb or (ia < na and ia * nb <= ib * na):
                self.cur.append(a[ia])
                ia += 1
            else:
                self.cur.append(b[ib])
                ib += 1

    def resolve(self):
        for i, op in enumerate(self.ops):
            op.idx = i
        for op in self.ops:
            reads, writes, chan, extra = op.deps
            deps = self._track(op.idx, reads, writes)
            deps.update(d.idx for d in extra)
            if chan is not None:
                ch = self.chan(chan)
                op.chan = ch
                ch.n += 1
                op.chcount = ch.n
                if ch.last is not None:
                    deps.add(ch.last)
                ch.last = op.idx
            deps.discard(op.idx)
            assert all(d < op.idx for d in deps), "dependency on a later op"
            op.deps = deps

    def mm(self, out, lhsT, rhs, start=True, stop=True):
        o, l, r = out.ap, lhsT.ap, rhs.ap
        return self.add("pe", lambda e: e.matmul(o, l, r, start=start, stop=stop),
                        reads=[lhsT, rhs], writes=[out])

    def tr(self, out, in_, ident):
        o, i, d = out.ap, in_.ap, ident.ap
        return self.add("pe", lambda e: e.transpose(o, i, d), reads=[in_, ident], writes=[out])

    def act(self, out, in_, func, bias=None, scale=1.0, accum=None, eng="act"):
        o, i = out.ap, in_.ap
        reads = [in_]
        kw = {}
        if bias is not None:
            if isinstance(bias, View):
                reads.append(bias)
                kw["bias"] = bias.ap
            else:
                kw["bias"] = float(bias)
        if isinstance(scale, View):
            reads.append(scale)
            kw["scale"] = scale.ap
        else:
            kw["scale"] = float(scale)
        writes = [out]
        if accum is not None:
            writes.append(accum)
            kw["accum_out"] = accum.ap
        return self.add("act", lambda e: e.activation(o, i, func, **kw), reads=reads, writes=writes)

    def tt(self, out, a, b, op, eng="dve"):
        o, x, y = out.ap, a.ap, b.ap
        return self.add(eng, lambda e: e.tensor_tensor(o, x, y, op), reads=[a, b], writes=[out])

    def ts(self, out, a, s1, op0, s2=None, op1=None, accum=None, eng="dve"):
        o, x = out.ap, a.ap
        reads = [a]
        if isinstance(s1, View):
            reads.append(s1)
            s1 = s1.ap
        if isinstance(s2, View):
            reads.append(s2)
            s2 = s2.ap
        kw = {}
        writes = [out]
        if op1 is not None:
            kw["op1"] = op1
        if accum is not None:
            kw["accum_out"] = accum.ap
            writes.append(accum)
        return self.add(eng, lambda e: e.tensor_scalar(o, x, s1, s2, op0, **kw), reads=reads, writes=writes)

    def stt(self, out, a, s, b, op0, op1, eng="dve"):
        o, x, y = out.ap, a.ap, b.ap
        reads = [a, b]
        if isinstance(s, View):
            reads.append(s)
            s = s.ap
        return self.add("dve", lambda e: e.scalar_tensor_tensor(o, x, s, y, op0, op1), reads=reads, writes=[out])

    def copy(self, out, in_, eng="dve"):
        o, i = out.ap, in_.ap
        if eng == "act":
            return self.add("act", lambda e: e.copy(o, i), reads=[in_], writes=[out])
        return self.add(eng, lambda e: e.tensor_copy(o, i), reads=[in_], writes=[out])

    def memset(self, out, val, eng="dve"):
        o = out.ap
        return self.add(eng, lambda e: e.memset(o, val), writes=[out])

    def reduce(self, out, in_, op, axis=AX.X, eng="dve"):
        o, i = out.ap, in_.ap
        return self.add(eng, lambda e: e.tensor_reduce(o, i, axis, op), reads=[in_], writes=[out])

    def recip(self, out, in_):
        o, i = out.ap, in_.ap
        return self.add("dve", lambda e: e.reciprocal(o, i), reads=[in_], writes=[out])

    def dma(self, out, in_, chan, q="sp", reads=(), writes=()):
        o = out.ap if isinstance(out, View) else out
        i = in_.ap if isinstance(in_, View) else in_
        rd = list(reads) + ([in_] if isinstance(in_, View) else [])
        wr = list(writes) + ([out] if isinstance(out, View) else [])
        return self.add(q, lambda e: e.dma_start(out=o, in_=i), reads=rd, writes=wr, chan=chan)

    def emit(self, final_wait_ops):
        nc = self.nc
        self.resolve()
        ops = self.ops
        fin = Op()
        fin.eng = "sp"
        fin.fn = None
        fin.idx = len(ops)
        fin.sig = False
        fin.count = None
        fin.chan = None
        fin.chcount = None
        fin.waits = []
        fin.deps = set(o.idx for o in final_wait_ops)
        fin.tag = "final"
        ops.append(fin)
        for op in ops:
            for d in op.deps:
                p = ops[d]
                if p.chan is None:
                    if p.eng == "pe" and op.eng == "pe":
                        continue
                    p.sig = True
        cnt = {e: 0 for e in self.ENGS}
        for op in ops:
            if op.sig:
                cnt[op.eng] += 1
                op.count = cnt[op.eng]
        nsem = {e: (cnt[e] // SEM_WRAP) + 1 for e in self.ENGS}
        sems = {e: [self.stack.enter_context(nc.semaphore(self.prefix + "s_%s_%d" % (e, j))) for j in range(nsem[e])]
                for e in self.ENGS}
        waited = {e: {} for e in self.ENGS}
        for op in ops:
            need = {}
            for d in op.deps:
                p = ops[d]
                if p.chan is not None:
                    key = ("c", id(p.chan))
                    val = p.chcount
                    obj = p.chan
                else:
                    if p.eng == "pe" and op.eng == "pe":
                        continue
                    key = ("e", p.eng)
                    val = p.count
                    obj = p.eng
                if key not in need or need[key][0] < val:
                    need[key] = (val, obj)
            w = waited[op.eng]
            for key, (val, obj) in need.items():
                if w.get(key, 0) >= val:
                    continue
                w[key] = val
                if key[0] == "c":
                    op.waits.append((obj.sem, 16 * val))
                else:
                    j, v = divmod(val - 1, SEM_WRAP)
                    op.waits.append((sems[obj][j], v + 1))
        by_eng = {e: [op for op in ops if op.eng == e] for e in self.ENGS}
        self.stats = {e: len(by_eng[e]) for e in self.ENGS}
        self.stats["waits"] = sum(len(op.waits) for op in ops)

        def run(eng_obj, lst, ename):
            for op in lst:
                for (sem, val) in op.waits:
                    eng_obj.wait_ge(sem, val)
                if op.fn is None:
                    continue
                ins = op.fn(eng_obj)
                if op.chan is not None:
                    ins.then_inc(op.chan.sem, 16)
                elif op.sig:
                    j = (op.count - 1) // SEM_WRAP
                    ins.then_inc(sems[ename][j], 1)

        with nc.Block() as block:
            @block.tensor
            def _(e):
                run(e, by_eng["pe"], "pe")

            @block.scalar
            def _(e):
                run(e, by_eng["act"], "act")

            @block.vector
            def _(e):
                run(e, by_eng["dve"], "dve")

            @block.gpsimd
            def _(e):
                run(e, by_eng["pool"], "pool")

            @block.sync
            def _(e):
                run(e, by_eng["sp"], "sp")

from concourse.bass_utils import run_bass_kernel_spmd

D = 1024
EVEN_IN = 3624
ODD_IN = 3088
NEG_BIG = -1.0e30
NEG_SEL = -3.0e38
ALPHA = 4.0 ** 0.25
TWO_PI = 2.0 * np.pi

C_ID = 0
C_TRIU = 128
C_TRIL = 384
C_ONES = 512
C_INV = 640
C_POW = 656
NIT = 28
C_NH = 688
NCONST = 656 + 32 + 8

BC0 = {}
_o = 0
for _n, _l in (("mu", 1664), ("w0", 512), ("a0", 512), ("kks", 512), ("ka", 512), ("rk", 512),
               ("gng", 512), ("gnb", 512), ("kvg", 128)):
    BC0[_n] = (_o, _l)
    _o += _l
NBC0 = _o


def make_consts():
    c = np.zeros((128, NCONST), np.float32)
    c[:, C_ID:C_ID + 128] = np.eye(128, dtype=np.float32)
    s = np.arange(128)[:, None]
    t = np.arange(128)[None, :]
    c[:, C_TRIU:C_TRIU + 128] = (s < t)
    c[:, C_TRIU + 128:C_TRIU + 256] = (s <= t)
    c[:, C_TRIL:C_TRIL + 128] = (s > t)
    c[:, C_ONES:C_ONES + 128] = 1.0
    inv16 = np.power(500000.0, -np.arange(8, dtype=np.float32) * 2.0 / 16).astype(np.float32)
    inv8 = np.power(500000.0, -np.arange(4, dtype=np.float32) * 2.0 / 8).astype(np.float32)
    c[:, C_INV:C_INV + 8] = (inv16.astype(np.float64) / TWO_PI).astype(np.float32)[None, :]
    c[:, C_INV + 8:C_INV + 12] = (inv8.astype(np.float64) / TWO_PI).astype(np.float32)[None, :]
    c[:, C_NH:C_NH + 8] = -0.5
    c[:, C_POW:C_POW + NIT] = (2.0 ** -(np.arange(NIT) + 1.0)).astype(np.float32)[None, :]
    return c


import os as _os
_SKIP = set(_os.environ.get("KSKIP", "").split(","))


class K:
    def __init__(self, T, topk, dbg=(), phases=("a", "b"), stop=99):
        self.stop = stop
        self.T = T
        self.NT = T // 128
        self.topk = topk
        self.dbg = set(dbg)
        self.phases = phases
        self.dbg_outs = {}

    def dram_in(self, name, shape, dt=F32):
        return self.nc.dram_tensor(name, list(shape), dt, kind="ExternalInput").ap()

    def build(self):
        nc = bass.Bass("TRN2", target_bir_lowering=False)
        self.nc = nc
        T, NT = self.T, self.NT
        self.x = self.dram_in("x", [T, D])
        self.pos = self.dram_in("pos", [128, NT], I32)
        self.consts = self.dram_in("consts", [128, NCONST])
        self.w_in0 = self.dram_in("w_in0", [D, EVEN_IN])
        self.w_out0 = self.dram_in("w_out0", [D, D])
        self.bcv0 = self.dram_in("bcv0", [1, NBC0])
        self.w2a2 = self.dram_in("w2a2", [64, 1024])
        self.wukv = self.dram_in("wukv", [128, 128])
        self.w_in1 = self.dram_in("w_in1", [D, ODD_IN])
        self.w_out1 = self.dram_in("w_out1", [D, D])
        self.bcv1 = self.dram_in("bcv1", [1, NBC1])
        self.g2 = self.dram_in("g2", [16, 512])
        if "b" in self.phases:
            self.out = nc.dram_tensor("out", [T, D], F32, kind="ExternalOutput").ap()
        if self.phases == ("a", "b"):
            self.h1 = nc.dram_tensor("h1", [T, D], F32).ap()
        elif self.phases == ("a",):
            self.h1 = nc.dram_tensor("h1", [T, D], F32, kind="ExternalOutput").ap()
        else:
            self.h1 = self.dram_in("h1", [T, D])
        self.w1_d = None
        if self.phases == ("a", "b"):
            self.w1_d = (nc.dram_tensor("wbf1_d", [D, ODD_IN], BF16).ap(), nc.dram_tensor("wob1_d", [D, D], BF16).ap())
        first = True
        for ph in self.phases:
            if ("pa" in _SKIP and ph == "a"):
                continue
            if not first:
                nc.all_engine_barrier()
            first = False
            with ExitStack() as st:
                P = Prog(nc, st, prefix=ph + "_")
                self.P = P
                self.finals = []
                self.setup_common(ph)
                if ph == "a":
                    self.phase_a()
                else:
                    self.phase_b()
                P.emit(self.finals)
                self.stats = getattr(self, "stats", {})
                self.stats[ph] = P.stats
        return nc

    def dump(self, name, view, shape):
        if name not in self.dbg:
            return
        t = self.nc.dram_tensor("dbg_" + name, list(shape), view.ap.dtype if hasattr(view.ap, "dtype") else F32,
                                kind="ExternalOutput").ap()
        self.dbg_outs[name] = "dbg_" + name
        i = self.P.dma(t, view, chan="dbg_" + name, q="pool")
        self.finals.append(i)

    def setup_common(self, ph="a"):
        P = self.P
        self.cst = P.sb("cst", [128, NCONST], F32)
        P.dma(self.cst[:, :], self.consts, chan="cst")
        self.cstb = P.sb("cstb", [128, 640], BF16)
        P.copy(self.cstb[:, :], self.cst[:, 0:640])
        self.B0 = P.ps("B0", [128, 512], F32)
        self.B1 = P.ps("B1", [128, 512], F32)
        self.B2 = P.ps("B2", [128, 1024], BF16)
        if ph == "a":
            self.B3 = self.B1
            self.B2d = P.ps("B2d", [128, 1024], BF16)
        else:
            self.B3 = P.ps("B3", [128, 512], F32)
        if ph == "a":
            self.BI = P.ps("BI", [128, 512], F32)
            self.BS = P.ps("BS", [128, 512], F32)
            self.BA = P.ps("BA", [128, 1024], F32)
        else:
            self.B56 = P.ps("B56", [128, 1024], F32)
            self.B78 = P.ps("B78", [128, 1024], F32)
        self.rot = 0

    def rbank(self):
        self.rot ^= 1
        return self.B0 if self.rot else self.B1

    def ident(self, n=128, bf=False):
        return (self.cstb if bf else self.cst)[0:n, C_ID:C_ID + n]

    def layernorm_out(self, res_ps_a, res_ps_b, xt, lng, lnb, dst_dram_rows, tmp, chan):
        P = self.P
        r = tmp["r"]
        sq = tmp["sq"]
        if isinstance(xt, tuple):
            xa, xb_ = xt[0][:, :], xt[1][:, :]
        else:
            xa, xb_ = xt[:, 0:512], xt[:, 512:1024]
        P.stt(r[:, 0:512], xa, ALPHA, res_ps_a, ALU.mult, ALU.add)
        P.stt(r[:, 512:1024], xb_, ALPHA, res_ps_b, ALU.mult, ALU.add)
        st = tmp["st"]
        P.reduce(st[:, 0:1], r[:, :], ALU.add)
        P.act(sq[:, :], r[:, :], AF.Square)
        P.reduce(st[:, 1:2], sq[:, :], ALU.add)
        P.ts(st[:, 2:3], st[:, 0:1], 1.0 / 1024, ALU.mult)
        P.tt(st[:, 3:4], st[:, 2:3], st[:, 2:3], ALU.mult)
        P.stt(st[:, 4:5], st[:, 1:2], 1.0 / 1024, st[:, 3:4], ALU.mult, ALU.subtract)
        P.ts(st[:, 4:5], st[:, 4:5], 1e-5, ALU.add)
        P.tt(st[:, 6:7], st[:, 4:5], self.cst[:, C_NH:C_NH + 1], ALU.pow, eng="pool")
        P.ts(r[:, :], r[:, :], st[:, 2:3], ALU.subtract, st[:, 6:7], ALU.mult)
        if lng is not None:
            P.tt(r[:, :], r[:, :], lng, ALU.mult)
            P.tt(r[:, :], r[:, :], lnb, ALU.add, eng="pool")
        i = P.dma(dst_dram_rows, r[:, :], chan=chan, q="sp")
        return i

    def phase_a(self):
        P = self.P
        T, NT = self.T, self.NT
        cst, cstb = self.cst, self.cstb
        wbf_d = self.nc.dram_tensor("wbf0_d", [D, EVEN_IN], BF16).ap()
        wbf_cast = []
        for c in range(8):
            wbf_cast.append(P.add("pool", (lambda e, o=wbf_d[c * 128:(c + 1) * 128, :], i=self.w_in0[c * 128:(c + 1) * 128, :]:
                                           e.dma_start(out=o, in_=i)), chan="wl%d" % (c % 2)))
        wbf_dv = wbf_d.rearrange("(c p) n -> p c n", p=128)
        wg = [P.sb("wg%d" % j, [128, 8, 512], BF16) for j in range(2)]
        wob_d = self.nc.dram_tensor("wob0_d", [D, D], BF16).ap()
        wob_cast = []
        for c in range(8):
            wob_cast.append(P.add("pool", (lambda e, o=wob_d[c * 128:(c + 1) * 128, :], i=self.w_out0[c * 128:(c + 1) * 128, :]:
                                           e.dma_start(out=o, in_=i)), chan="wl%d" % (c % 2)))
        wob2 = [P.sb("wob0_%d" % j, [128, 8, 512], BF16) for j in range(2)]
        wob_dv = wob_d.rearrange("(c p) n -> p c n", p=128)
        bc = P.sb("bc0", [128, NBC0], F32)
        P.dma(bc[:, :], self.bcv0[0].partition_broadcast(128), chan="bc0")
        BCV = lambda n: bc[:, BC0[n][0]:BC0[n][0] + BC0[n][1]]
        w2a2 = P.sb("w2a2", [65, 1024], F32)
        P.dma(w2a2[0:64, :], self.w2a2, chan="w2a2")
        P.dma(w2a2[64:65, :], self.bcv0[0:1, BC0["w0"][0]:BC0["w0"][0] + 1024], chan="w2a2")
        wukv = P.sb("wukv", [128, 128], F32)
        P.dma(wukv[:, :], self.wukv, chan="wukv")
        posi = P.sb("posi", [128, NT], I32)
        P.dma(posi[:, :], self.pos, chan="pos")
        posf = P.sb("posf", [128, NT], F32)
        P.copy(posf[:, :], posi[:, :])
        sc = [(P.sb("scr%d" % j, [128, 512], F32) if j != 6 else None) for j in range(9)]
        sinT = P.sb("sinT", [128, NT, 12], F32)
        cosT = P.sb("cosT", [128, NT, 12], F32)
        tph = sc[0][:, 0:NT * 12].re("p (n j) -> p n j", j=12)
        tmpR = sc[1][:, 0:NT * 12].re("p (n j) -> p n j", j=12)
        tmpI = sc[2][:, 0:NT * 12].bitcast(I32).re("p (n j) -> p n j", j=12)
        P.tt(tph[:, :, :], posf[:, :, None].bc([128, NT, 12]),
             cst[:, None, C_INV:C_INV + 12].bc([128, NT, 12]), ALU.mult)

        def frac_sin(dst, shift):
            P.ts(tmpR[:, :, :], tph[:, :, :], shift, ALU.add)
            P.copy(tmpI[:, :, :], tmpR[:, :, :])
            P.copy(dst[:, :, :], tmpI[:, :, :])
            P.tt(tmpR[:, :, :], tmpR[:, :, :], dst[:, :, :], ALU.subtract)
            P.ts(dst[:, :, :], tmpR[:, :, :], 0.5, ALU.is_ge)
            P.tt(tmpR[:, :, :], tmpR[:, :, :], dst[:, :, :], ALU.subtract)
            P.ts(dst[:, :, :], tmpR[:, :, :], -0.5, ALU.is_lt)
            P.tt(tmpR[:, :, :], tmpR[:, :, :], dst[:, :, :], ALU.add)
            P.act(dst[:, :, :], tmpR[:, :, :], AF.Sin, scale=TWO_PI)
        frac_sin(sinT, 0.0)
        frac_sin(cosT, 0.25)
        self.dump("sinT", sinT[:, :, :], [128, NT, 12])
        self.dump("cosT", cosT[:, :, :], [128, NT, 12])
        if self.stop == 0:
            return

        kT = P.sb("kT", [64, T], BF16)
        kiT = P.sb("kiT", [32, T], BF16)
        Vaug = P.sb("Vaug", [128, NT, 65], BF16)
        P.memset(Vaug[:, :, 64:65], 1.0)
        score = P.sb("score", [128, T], F32)
        Hf = P.sb("Hf", [64, 8, 64], F32)
        Hb = P.sb("Hb", [64, 8, 64], BF16)
        P.memset(Hf[:, :, :], 0.0)
        P.memset(Hb[:, :, :], 0.0)
        xb = P.sb("xb", [128, D], BF16)
        xT = P.sb("xT", [128, 8, 128], BF16)
        p = P.sb("p", [128, 1960], F32)
        psh = P.sb("psh", [128, 1664], F32)
        xsT = P.sb("xsT", [128, 8, 128], BF16)
        xlast = P.sb("xlast", [128, 8], BF16)
        sm = P.sb("sm", [128, 64], F32)
        thw = P.sb("thw", [128, 128], F32)
        thwT = P.sb("thwT", [65, 256], F32)
        P.memset(thwT[64:65, :], 1.0)
        rt_b = P.sb("rt_b", [128, 512], BF16)
        at_b = P.sb("at_b", [128, 512], BF16)
        bt_b = P.sb("bt_b", [128, 512], BF16)
        kt_b = P.sb("kt_b", [128, 512], BF16)
        v_b = P.sb("v_b", [128, 512], BF16)
        arT = P.sb("arT", [64, 8, 256], BF16)
        bT = P.sb("bT", [64, 8, 128], BF16)
        kTt = P.sb("kTt", [64, 8, 128], BF16)
        gC = P.sb("gC", [64, 8], F32)
        Mt = [P.sb("Mt%d" % j, [128, 4, 128], F32) for j in range(2)]
        LRb = P.sb("LRb", [128, 4, 128], BF16)
        LRk = P.sb("LRk", [128, 4, 256], BF16)
        rhs_f = P.sb("rhs_f", [128, 256], F32)
        Ub = P.sb("Ub", [128, 8, 64], BF16)
        ycb2 = [P.sb("ycb%d" % j, [128, D], BF16) for j in range(2)]
        yT_d = P.sb("yT_d", [128, 8, 128], BF16)
        xl = P.sb("xl", [128, D], F32)
        lnt_d = {"r": xl[:, :], "st": P.sb("ln_st", [128, 8], F32), "sq": P.sb("ln_sq", [128, D], F32)[:, :]}
        cT = sc[3][:, 0:128]
        kv = sc[3][:, 128:256]
        qb16 = rt_b
        qT2 = [P.sb("qT%d" % j, [64, 8, 128], BF16) for j in range(2)]
        gbs2 = [P.sb("gbs%d" % j, [128, 512], BF16) for j in range(2)]
        qib = at_b
        qiT2 = [P.sb("qiT%d" % j, [32, 8, 128], BF16) for j in range(2)]
        kb16 = P.sb("kb16", [128, 96], BF16)
        wis2 = [P.sb("wis%d" % j, [128, 8], F32) for j in range(2)]
        relu2 = [P.sb("relu%d" % j, [128, 512], F32) for j in range(2)]
        bis = P.sb("bis", [128, 8], F32)
        hw = P.sb("hw", [128, NIT], F32)
        sacc = P.sb("sacc", [128, NIT], F32)
        cacc = P.sb("cacc", [128, NIT], F32)
        junk = P.sb("junk", [128, T], BF16)
        mk = [P.sb("mk%d" % j, [128, 128], BF16) for j in range(2)]
        mkT = [P.sb("mkT%d" % j, [128, 4, 128], BF16) for j in range(2)]
        Eh = [[P.sb("E%d_%d" % (g, j), [128, 4, 128], BF16) for j in range(2)] for g in range(2)]
        rope_t = P.sb("rope_t", [128, 8, 16], F32)
        den = P.sb("den", [128, 8], F32)

        B0, B1, B2, B3, BI, BS, BA, B2d = self.B0, self.B1, self.B2, self.B3, self.BI, self.BS, self.BA, self.B2d
        groups = [(0, 512), (512, 1024), (1024, 1536), (1536, 1664), (1664, 2176), (2176, 2688), (2688, 3112), (3112, 3624)]
        O = 0
        c_ga, c_q, c_ckv, c_qi, c_ki, c_wi, c_gb = O, O + 512, O + 1024, O + 1152, O + 1408, O + 1440, O + 1448

        def rope(dst, src, nh, hd, half, ti, foff):
            cs = cosT[:, ti, None, foff:foff + half].bc([128, nh, half])
            sn = sinT[:, ti, None, foff:foff + half].bc([128, nh, half])
            x1 = src[:, :, 0:half]
            x2 = src[:, :, half:2 * half]
            t = rope_t
            P.tt(t[:, 0:nh, 0:half], x1, cs, ALU.mult)
            P.tt(t[:, 0:nh, 8:8 + half], x2, sn, ALU.mult)
            P.tt(dst[:, :, 0:half], t[:, 0:nh, 0:half], t[:, 0:nh, 8:8 + half], ALU.subtract)
            P.tt(t[:, 0:nh, 0:half], x2, cs, ALU.mult)
            P.tt(t[:, 0:nh, 8:8 + half], x1, sn, ALU.mult)
            P.tt(dst[:, :, half:2 * half], t[:, 0:nh, 0:half], t[:, 0:nh, 8:8 + half], ALU.add)
            P.copy(dst[:, :, 2 * half:hd], src[:, :, 2 * half:hd], eng="pool")

        H8 = lambda v: v.re("p (h d) -> p h d", h=8)

        def emit_A(ti):
            S = 128 * (ti + 1)
            tok = slice(ti * 128, (ti + 1) * 128)
            P.dma(sc[0][:, :], self.x[tok, 0:512], chan="xin0")
            P.dma(sc[1][:, :], self.x[tok, 512:1024], chan="xin1")
            P.copy(xb[:, 0:512], sc[0][:, :], eng="act")
            P.copy(xb[:, 512:1024], sc[1][:, :], eng="act")
            for c in range(8):
                P.tr(B2[:, c * 128:(c + 1) * 128], xb[:, c * 128:(c + 1) * 128], self.ident(128, True))
            P.copy(xT[:, :, :], B2[:, :].re("p (c t) -> p c t", c=8))
            if ti == 0:
                P.memset(xsT[:, :, 0:1], 0.0)
            else:
                P.copy(xsT[:, :, 0:1], xlast[:, :, None], eng="pool")
            P.copy(xsT[:, :, 1:128], xT[:, :, 0:127], eng="dve")
            P.copy(xlast[:, :], xT[:, :, 127], eng="pool")

            def project(gl, off, shifted=False):
                for (c0, c1) in gl:
                    bk = self.rbank()
                    wgb = wg[self.wgi % 2]
                    self.wgi += 1
                    _w = 16 if "wgsmall" in _SKIP else c1 - c0
                    P.add("sp", (lambda e, o=wgb[:, :, 0:_w].ap, i=wbf_dv[:, :, c0:c0 + _w]: e.dma_start(out=o, in_=i)),
                          writes=[wgb[:, :, :]], chan="wg%d" % (self.wgi % 2), extra_deps=wbf_cast)
                    for c in range(8):
                        P.mm(bk[:, 0:c1 - c0], xT[:, c, :], wgb[:, c, 0:c1 - c0], start=(c == 0), stop=(c == 7))
                    P.copy(p[:, c0 - off:c1 - off], bk[:, 0:c1 - c0], eng=("act" if ("evact" in _SKIP and S < 2048) else "dve"))
                    if shifted:
                        bk2 = self.rbank()
                        for c in range(8):
                            P.mm(bk2[:, 0:c1 - c0], xsT[:, c, :], wgb[:, c, 0:c1 - c0], start=(c == 0), stop=(c == 7))
                        P.tt(psh[:, c0:c1], bk2[:, 0:c1 - c0], p[:, c0:c1], ALU.subtract)
            project(groups[0:4], 0, shifted=True)
            if self.stop == 1:
                self.dump("p", p[:, :], [128, EVEN_IN])
                return
            P.tt(psh[:, :], psh[:, :], BCV("mu"), ALU.mult)
            P.tt(psh[:, :], psh[:, :], p[:, 0:1664], ALU.add)
            r_, k_, v_ = psh[:, 0:512], psh[:, 512:1024], psh[:, 1024:1536]
            project(groups[4:8], 1664)
            if ti == self.NT - 1 or self.stop == 2:
                self.dump("ps", psh[:, :], [128, 1664])
            if self.stop == 2:
                return
            P.act(thw[:, 0:64], psh[:, 1536:1600], AF.Tanh)
            P.copy(thw[:, 64:128], psh[:, 1600:1664], eng="pool")
            P.tr(B3[0:64, 0:128], thw[:, 0:64], self.ident(128))
            P.tr(B3[0:64, 128:256], thw[:, 64:128], self.ident(128))
            P.copy(thwT[0:64, :], B3[0:64, 0:256])
            bz = self.rbank()
            P.mm(bz[:, :], thwT[:, 0:128], w2a2[:, 0:512])
            lw = sc[1]
            P.act(lw[:, :], bz[:, :], AF.Tanh, scale=0.5)
            P.ts(lw[:, :], lw[:, :], -0.5 * float(np.exp(-0.5)), ALU.mult, -0.5 * float(np.exp(-0.5)), ALU.add)
            if self.stop == 30:
                self.dump("lw", lw[:, :], [128, 512])
                return
            ba = self.rbank()
            P.mm(ba[:, :], thwT[:, 128:256], w2a2[:, 512:1024])
            av = sc[2]
            P.act(av[:, :], ba[:, :], AF.Tanh, scale=0.5)
            P.ts(av[:, :], av[:, :], 0.5, ALU.mult, 0.5, ALU.add)
            bcs = self.rbank()
            P.mm(bcs[:, :], cst[:, C_TRIU + 128:C_TRIU + 256], lw[:, :])
            cum = sc[3]
            P.copy(cum[:, :], bcs[:, :])
            for h in range(8):
                P.mm(B3[0:64, 256 + h:257 + h], lw[:, h * 64:(h + 1) * 64], cst[:, C_ONES:C_ONES + 1])
            P.act(gC[:, :], B3[0:64, 256:264], AF.Exp)
            gam = sc[4]
            P.act(gam[:, :], cum[:, :], AF.Exp)
            ginv = sc[5]
            P.act(ginv[:, :], cum[:, :], AF.Exp, scale=-1.0)
            gprev = sc[0]
            P.tt(gprev[:, :], cum[:, :], lw[:, :], ALU.subtract)
            P.act(gprev[:, :], gprev[:, :], AF.Exp)
            if self.stop == 31:
                self.dump("cum", cum[:, :], [128, 512])
                self.dump("gC", gC[:, :], [64, 8])
                return
            kk = sc[7]
            P.tt(kk[:, :], k_, BCV("kks"), ALU.mult)
            t8 = sc[8]
            P.act(t8[:, :], kk[:, :], AF.Square)
            P.reduce(sm[:, 0:8], H8(t8[:, :]), ALU.add)
            P.ts(sm[:, 8:16], sm[:, 0:8], 1e-24, ALU.max)
            P.tt(sm[:, 16:24], sm[:, 8:16], cst[:, C_NH:C_NH + 8], ALU.pow, eng="pool")
            P.tt(H8(kk[:, :]), H8(kk[:, :]), sm[:, 16:24, None].bc([128, 8, 64]), ALU.mult)
            k2 = sc[8]
            _ke = "pool" if "k2pool" in _SKIP else "dve"
            P.ts(k2[:, :], av[:, :], -1.0, ALU.add, eng=_ke)
            P.tt(k2[:, :], k2[:, :], BCV("ka"), ALU.mult, eng=_ke)
            P.ts(k2[:, :], k2[:, :], 1.0, ALU.add, eng=_ke)
            P.tt(k2[:, :], k2[:, :], k_, ALU.mult, eng=_ke)
            P.tt(rt_b[:, :], r_, gam[:, :], ALU.mult)
            P.stt(at_b[:, :], kk[:, :], -1.0, gprev[:, :], ALU.mult, ALU.mult)
            P.tt(kk[:, :], kk[:, :], av[:, :], ALU.mult)
            P.tt(bt_b[:, :], kk[:, :], ginv[:, :], ALU.mult)
            P.tt(kt_b[:, :], k2[:, :], ginv[:, :], ALU.mult)
            P.copy(v_b[:, :], v_, eng="act")
            bon = sc[0]
            P.tt(bon[:, :], r_, k2[:, :], ALU.mult)
            P.tt(bon[:, :], bon[:, :], BCV("rk"), ALU.mult, eng="pool")
            P.reduce(sm[:, 24:32], H8(bon[:, :]), ALU.add)
            P.tt(H8(bon[:, :]), H8(v_), sm[:, 24:32, None].bc([128, 8, 64]), ALU.mult)
            for (src, dstf) in ((at_b, lambda h: arT[:, h, 0:128]), (rt_b, lambda h: arT[:, h, 128:256]),
                                (bt_b, lambda h: bT[:, h, :]), (kt_b, lambda h: kTt[:, h, :])):
                for h in range(8):
                    P.tr(B2[0:64, h * 128:(h + 1) * 128], src[:, h * 64:(h + 1) * 64], self.ident(128, True))
                if src is at_b:
                    P.copy(arT[:, :, 0:128], B2[0:64, :].re("p (h t) -> p h t", h=8))
                elif src is rt_b:
                    P.copy(arT[:, :, 128:256], B2[0:64, :].re("p (h t) -> p h t", h=8), eng="act")
                elif src is bt_b:
                    P.copy(bT[:, :, :], B2[0:64, :].re("p (h t) -> p h t", h=8))
                else:
                    P.copy(kTt[:, :, :], B2[0:64, :].re("p (h t) -> p h t", h=8), eng="act")
            if self.stop == 32:
                self.dump("arT", arT[:, :, :], [64, 8, 256])
                self.dump("bon", bon[:, :], [128, 512])
                return
            cv = sc[2]
            P.tt(cv[:, 0:128], p[:, c_ckv:c_ckv + 128], p[:, c_ckv:c_ckv + 128], ALU.mult, eng="pool")
            P.reduce(sm[:, 56:57], cv[:, 0:128], ALU.add)
            P.ts(sm[:, 56:57], sm[:, 56:57], 1.0 / 128, ALU.mult, 1e-6, ALU.add)
            P.tt(sm[:, 57:58], sm[:, 56:57], cst[:, C_NH:C_NH + 1], ALU.pow, eng="pool")
            P.ts(cv[:, 0:128], p[:, c_ckv:c_ckv + 128], sm[:, 57:58], ALU.mult)
            P.tt(cv[:, 0:128], cv[:, 0:128], BCV("kvg"), ALU.mult)
            P.tr(B3[:, 0:128], cv[:, 0:128], self.ident(128))
            P.copy(cT[:, :], B3[:, 0:128])
            bkv = self.rbank()
            P.mm(bkv[:, 0:128], cT[:, :], wukv[:, :])
            P.copy(kv[:, :], bkv[:, 0:128], eng="act")
            kr = sc[7]
            rope(kr[:, 0:64].re("p (h d) -> p h d", h=1), kv[:, 0:64].re("p (h d) -> p h d", h=1), 1, 64, 8, ti, 0)
            qr = sc[8]
            qir = sc[1]
            rope(H8(qr[:, :]), H8(p[:, c_q:c_q + 512]), 8, 64, 8, ti, 0)
            rope(qir[:, 0:256].re("p (h d) -> p h d", h=8), p[:, c_qi:c_qi + 256].re("p (h d) -> p h d", h=8), 8, 32, 4, ti, 8)
            kir = sc[4]
            rope(kir[:, 0:32].re("p (h d) -> p h d", h=1), p[:, c_ki:c_ki + 32].re("p (h d) -> p h d", h=1), 1, 32, 4, ti, 8)
            P.copy(kb16[:, 0:64], kr[:, 0:64])
            P.copy(kb16[:, 64:96], kir[:, 0:32])
            P.tr(B2[0:64, 0:128], kb16[:, 0:64], self.ident(128, True))
            P.tr(B2[0:32, 128:256], kb16[:, 64:96], self.ident(128, True))
            P.copy(kT.k(ti)[:, tok], B2[0:64, 0:128])
            P.copy(kiT.k(ti)[:, tok], B2[0:32, 128:256], eng="act")
            P.copy(Vaug.k(ti)[:, ti, 0:64], kv[:, 64:128], eng="act")
            P.ts(qb16[:, :], qr[:, :], 0.125, ALU.mult)
            for h in range(8):
                P.tr(B2[0:64, h * 128:(h + 1) * 128], qb16[:, h * 64:(h + 1) * 64], self.ident(128, True))
            P.copy(qT2[ti % 2][:, :, :], B2[0:64, :].re("p (h t) -> p h t", h=8))
            P.copy(qib[:, 0:256], qir[:, 0:256], eng="act")
            for h in range(8):
                P.tr(B2[0:32, h * 128:(h + 1) * 128], qib[:, h * 32:(h + 1) * 32], self.ident(128, True))
            P.copy(qiT2[ti % 2][:, :, :], B2[0:32, :].re("p (h t) -> p h t", h=8))
            P.ts(wis2[ti % 2][:, :], p[:, c_wi:c_wi + 8], 1.0 / 16.0, ALU.mult)
            P.act(gbs2[ti % 2][:, :], p[:, c_gb:c_gb + 512], AF.Tanh, scale=0.5)
            P.stt(gbs2[ti % 2][:, :], gbs2[ti % 2][:, :], 1.0, p[:, c_gb:c_gb + 512], ALU.add, ALU.mult)
            y = sc[3]
            G4 = lambda v: v.re("p (h t) -> p h t", h=4)
            triS = cst[:, None, C_TRIU:C_TRIU + 128].bc([128, 4, 128])
            triI = cst[:, None, C_TRIU + 128:C_TRIU + 256].bc([128, 4, 128])
            trilS = cst[:, None, C_TRIL:C_TRIL + 128].bc([128, 4, 128])
            idb = cst[:, None, C_ID:C_ID + 128].bc([128, 4, 128])
            for g2 in range(2):
                hg = [4 * g2 + i for i in range(4)]
                if "invf32" in _SKIP:
                    Pp = [G4(sc[2][:, :]), G4(sc[7][:, :])]
                    Qp = [G4(sc[8][:, :]), G4(sc[4][:, :])]
                    Mp = [Mt[0][:, :, :], Mt[1][:, :, :]]
                    rhs_v = rhs_f[:, :]
                    idm, idb_ = self.ident(128), idb
                else:
                    Pp = [G4(sc[2][:, 0:256].bitcast(BF16)), G4(sc[7][:, 0:256].bitcast(BF16))]
                    Qp = [G4(sc[8][:, 0:256].bitcast(BF16)), G4(sc[4][:, 0:256].bitcast(BF16))]
                    Mp = [G4(Mt[0][:, 0:2, :].re("p h t -> p (h t)").bitcast(BF16)), G4(Mt[1][:, 0:2, :].re("p h t -> p (h t)").bitcast(BF16))]
                    rhs_v = rhs_f[:, 0:128].bitcast(BF16)
                    idb_ = cstb[:, None, C_ID:C_ID + 128].bc([128, 4, 128])
                bk = self.rbank()
                for i, h in enumerate(hg):
                    P.mm(bk[:, i * 128:(i + 1) * 128], bT[:, h, :], arT[:, h, 0:128])
                P.tt(Qp[0], G4(bk[:, :]), triS, ALU.mult)
                bk = self.rbank()
                for i, h in enumerate(hg):
                    P.mm(bk[:, i * 128:(i + 1) * 128], arT[:, h, 0:128], bT[:, h, :])
                P.tt(Pp[0], G4(bk[:, :]), trilS, ALU.mult)
                P.tt(Mp[0], Qp[0], idb_, ALU.add, eng=("dve" if "mp0dve" in _SKIP else "pool"))
                bk = B3
                for i, h in enumerate(hg):
                    P.mm(bk[:, i * 128:(i + 1) * 128], bT[:, h, :], arT[:, h, 128:256])
                P.tt(LRb[:, :, :], G4(bk[:, :]), triI, ALU.mult)
                bk = self.rbank()
                for i, h in enumerate(hg):
                    P.mm(bk[:, i * 128:(i + 1) * 128], kTt[:, h, :], arT[:, h, 0:128])
                P.tt(LRk[:, :, 0:128], G4(bk[:, :]), triS, ALU.mult)
                bk = self.rbank()
                for i, h in enumerate(hg):
                    P.mm(bk[:, i * 128:(i + 1) * 128], kTt[:, h, :], arT[:, h, 128:256])
                P.tt(LRk[:, :, 128:256], G4(bk[:, :]), triI, ALU.mult)
                cp, cq, cm = 0, 0, 0
                for kstep in range(1, 1 if "inv" in _SKIP else 7):
                    bp = self.rbank()
                    for i in range(4):
                        P.mm(bp[:, i * 128:(i + 1) * 128], Qp[cq][:, i, :], Pp[cp][:, i, :])
                    if kstep < 6:
                        bq = self.rbank()
                        for i in range(4):
                            P.mm(bq[:, i * 128:(i + 1) * 128], Pp[cp][:, i, :], Qp[cq][:, i, :])
                    P.copy(Pp[1 - cp], G4(bp[:, :]))
                    if kstep < 6:
                        P.copy(Qp[1 - cq], G4(bq[:, :]), eng="act")
                    cp, cq = 1 - cp, 1 - cq
                    for i in range(4):
                        P.mm(B3[:, i * 128:(i + 1) * 128], Pp[cp][:, i, :], Mp[cm][:, i, :])
                    P.tt(Mp[1 - cm], G4(B3[:, :]), Mp[cm], ALU.add)
                    cm = 1 - cm
                br = self.rbank()
                for i, h in enumerate(hg):
                    P.mm(br[:, i * 64:(i + 1) * 64], arT[:, h, 0:128], Hb[:, h, :], start=True, stop=False)
                    P.mm(br[:, i * 64:(i + 1) * 64], LRk[:, i, 0:128], v_b[:, h * 64:(h + 1) * 64], start=False, stop=True)
                P.copy(rhs_v, br[:, 0:256], eng="act")
                bu = self.rbank()
                for i, h in enumerate(hg):
                    P.mm(bu[:, i * 64:(i + 1) * 64], Mp[cm][:, i, :], rhs_v[:, i * 64:(i + 1) * 64])
                P.copy(Ub[:, 4 * g2:4 * g2 + 4, :], bu[:, 0:256].re("p (h v) -> p h v", h=4))
                by = self.rbank()
                for i, h in enumerate(hg):
                    hs = slice(h * 64, (h + 1) * 64)
                    P.mm(by[:, i * 64:(i + 1) * 64], arT[:, h, 128:256], Hb[:, h, :], start=True, stop=False)
                    P.mm(by[:, i * 64:(i + 1) * 64], LRb[:, i, :], Ub[:, h, :], start=False, stop=False)
                    P.mm(by[:, i * 64:(i + 1) * 64], LRk[:, i, 128:256], v_b[:, hs], start=False, stop=True)
                P.copy(y[:, g2 * 256:(g2 + 1) * 256], by[:, 0:256], eng="act")
                bh = self.rbank()
                for i, h in enumerate(hg):
                    hs = slice(h * 64, (h + 1) * 64)
                    P.mm(bh[0:64, i * 64:(i + 1) * 64], bt_b[:, hs], Ub[:, h, :], start=True, stop=False)
                    P.mm(bh[0:64, i * 64:(i + 1) * 64], kt_b[:, hs], v_b[:, hs], start=False, stop=True)
                Hg = Hf[:, 4 * g2:4 * g2 + 4, :]
                P.tt(Hg, Hg, bh[0:64, 0:256].re("p (h v) -> p h v", h=4), ALU.add)
                P.tt(Hg, Hg, gC[:, 4 * g2:4 * g2 + 4, None].bc([64, 4, 64]), ALU.mult)
                P.copy(Hb[:, 4 * g2:4 * g2 + 4, :], Hg, eng="act")
            y = sc[3]
            if ti == self.NT - 1 or self.stop == 3:
                self.dump("yraw", y[:, :], [128, 512])
            if self.stop == 3:
                return
            P.reduce(sm[:, 32:40], H8(y[:, :]), ALU.add)
            ysq = sc[1]
            P.act(ysq[:, :], y[:, :], AF.Square)
            P.reduce(sm[:, 40:48], H8(ysq[:, :]), ALU.add)
            P.ts(sm[:, 32:40], sm[:, 32:40], 1.0 / 64, ALU.mult)
            P.tt(sm[:, 48:56], sm[:, 32:40], sm[:, 32:40], ALU.mult)
            P.stt(sm[:, 40:48], sm[:, 40:48], 1.0 / 64, sm[:, 48:56], ALU.mult, ALU.subtract)
            P.ts(sm[:, 40:48], sm[:, 40:48], 64e-5, ALU.add)
            P.tt(sm[:, 48:56], sm[:, 40:48], cst[:, C_NH:C_NH + 8], ALU.pow, eng="pool")
            P.tt(H8(y[:, :]), H8(y[:, :]), sm[:, 32:40, None].bc([128, 8, 64]), ALU.subtract)
            P.tt(H8(y[:, :]), H8(y[:, :]), sm[:, 48:56, None].bc([128, 8, 64]), ALU.mult)
            P.tt(y[:, :], y[:, :], BCV("gng"), ALU.mult)
            P.tt(y[:, :], y[:, :], BCV("gnb"), ALU.add, eng="pool")
            P.tt(y[:, :], y[:, :], bon[:, :], ALU.add)
            ga = sc[1]
            P.act(ga[:, :], p[:, c_ga:c_ga + 512], AF.Tanh, scale=0.5)
            P.stt(ga[:, :], ga[:, :], 1.0, p[:, c_ga:c_ga + 512], ALU.add, ALU.mult)
            P.stt(ycb2[ti % 2][:, 0:512], y[:, :], 0.5, ga[:, :], ALU.mult, ALU.mult)


        def emit_D(ti):
            S = 128 * (ti + 1)
            tok = slice(ti * 128, (ti + 1) * 128)
            qT, qiT, wis = qT2[ti % 2], qiT2[ti % 2], wis2[ti % 2]
            for half in range(2):
                P.add("sp", (lambda e, o=wob2[half][:, :, :].ap, i=wob_dv[:, :, half * 512:(half + 1) * 512]: e.dma_start(out=o, in_=i)),
                      writes=[wob2[half][:, :, :]], chan="wobld%d" % half, extra_deps=wob_cast)
            P.dma(xl[:, :], self.x[tok, :], chan="xl")
            nch = (S + 511) // 512
            for n in range(1 if "idx" in _SKIP else nch):
                k0 = n * 512
                k1 = min(S, k0 + 512)
                w = k1 - k0
                kkeys = range(k0 // 128, k1 // 128)
                scn = score.k(n)
                for h in range(8):
                    bi = BI if h % 2 == 0 else BS
                    P.add("pe", (lambda e, o=bi[:, 0:w].ap, l=qiT[:, h, :].ap, r=kiT[:, k0:k1].ap: e.matmul(o, l, r, start=True, stop=True)),
                          reads=[qiT[:, h, :]] + [kiT.k(j)[:, j * 128:(j + 1) * 128] for j in kkeys], writes=[bi[:, 0:w]])
                    rl = relu2[h % 2]
                    P.act(rl[:, 0:w], bi[:, 0:w], AF.Relu)
                    if h == 0:
                        P.ts(scn[:, k0:k1], rl[:, 0:w], wis[:, 0:1], ALU.mult, eng="pool")
                    else:
                        P.stt(scn[:, k0:k1], rl[:, 0:w], wis[:, h:h + 1], scn[:, k0:k1], ALU.mult, ALU.add)
            use_topk = S > self.topk
            if use_topk:
                P.add("dve", (lambda e, o=bis[:, 0:1].ap, i=score[:, 0:S].ap: e.tensor_reduce(o, i, AX.X, ALU.max, apply_absolute_value=True)),
                      reads=[score[:, 0:S]], writes=[bis[:, 0:1]])
            P.memset(score[0:64, S - 64:S], NEG_BIG)
            if use_topk:
                P.ts(bis[:, 1:2], bis[:, 0:1], -1.0, ALU.mult, -1.0, ALU.add)
                P.ts(bis[:, 2:3], bis[:, 0:1], 2.0, ALU.mult, 1.0, ALU.add)
                P.tt(hw[:, :], cst[:, C_POW:C_POW + NIT], bis[:, 2:3].bc([128, NIT]), ALU.mult)
                P.memset(sacc[:, :], 0.0)
                P.tt(bis[:, 3:4], bis[:, 1:2], hw[:, 0:1], ALU.add)
                nit = 0 if "topk" in _SKIP else NIT
                S1 = (int(S * float(_os.environ.get("KS1", "0.42"))) // 128) * 128
                if S < 1536:
                    S1 = 0
                n2 = S - S1
                for k in range(nit):
                    P.act(junk.k("a")[:, S1:S], score[:, S1:S], AF.Sign, bias=bis[:, 3:4], scale=-1.0, accum=sacc[:, k:k + 1])
                    if S1 > 0:
                        P.ts(junk.k("d")[:, 0:S1], score[:, 0:S1], bis[:, 3:4], ALU.is_gt, 0.0, ALU.add, accum=cacc[:, k:k + 1])
                        P.stt(bis[:, 5:6], cacc[:, k:k + 1], -2.0, sacc[:, k:k + 1], ALU.mult, ALU.add)
                        tv = bis[:, 5:6]
                    else:
                        tv = sacc[:, k:k + 1]
                    P.ts(bis[:, 4:5], tv, float(n2 - 2 * self.topk), ALU.is_le, hw[:, k:k + 1], ALU.mult)
                    P.tt(bis[:, 1:2], bis[:, 1:2], bis[:, 4:5], ALU.add)
                    if k + 1 < nit:
                        P.tt(bis[:, 3:4], bis[:, 1:2], hw[:, k + 1:k + 2], ALU.add)
            if use_topk:
                P.ts(junk[:, 0:S], score[:, 0:S], bis[:, 1:2], ALU.is_le, -30000.0, ALU.mult)
            else:
                P.ts(junk[:, 0:S], score[:, 0:S], -1.0e29, ALU.is_le, -30000.0, ALU.mult)
            nj = ti + 1
            for j in range(1 if "attn" in _SKIP else nj):
                ks = slice(j * 128, (j + 1) * 128)
                P.tr(B2d[:, 0:128], junk[:, ks], self.ident(128, True))
                mT = mkT[j % 2]
                P.copy(mT[:, :, :], B2d[:, None, 0:128].bc([128, 4, 128]), eng="act")
                for g in range(2):
                    bs_ = BI if g == 0 else BS
                    P.mm(bs_[:, 0:512], kT.k(j)[:, ks], qT[:, 4 * g:4 * g + 4, :].re("p h t -> p (h t)"), start=True, stop=False)
                    P.mm(bs_[:, 0:512], self.ident(128, True), mT[:, :, :].re("p h t -> p (h t)"), start=False, stop=True)
                    e_ = Eh[g][j % 2]
                    P.act(e_[:, :, :].re("p h t -> p (h t)"), bs_[:, 0:512], AF.Exp)
                    for hh in range(4):
                        o0 = g * 512 + hh * 65
                        P.mm(BA[:, o0:o0 + 65], e_[:, hh, :], Vaug.k(j)[:, j, :], start=(j == 0 and hh == 0),
                             stop=((j == nj - 1 or "attn" in _SKIP) and hh == 3))
            accv = BA[:, :].re("p (g c) -> p g c", g=2)[:, :, 0:260].re("p g (h c) -> p g h c", h=4)
            for g in range(2):
                P.copy(den[:, g * 4:(g + 1) * 4], accv[:, g, :, 64])
            P.recip(den[:, :], den[:, :])
            ycb = ycb2[ti % 2]
            for g in range(2):
                P.tt(ycb[:, 512 + g * 256:512 + (g + 1) * 256].re("p (h d) -> p h d", h=4), accv[:, g, :, 0:64],
                     den[:, g * 4:(g + 1) * 4, None].bc([128, 4, 64]), ALU.mult)
            P.stt(ycb[:, 512:1024], ycb[:, 512:1024], 0.5, gbs2[ti % 2][:, :], ALU.mult, ALU.mult)
            for c in range(8):
                P.tr(B2d[:, c * 128:(c + 1) * 128], ycb[:, c * 128:(c + 1) * 128], self.ident(128, True))
            P.copy(yT_d[:, :, :], B2d[:, :].re("p (c t) -> p c t", c=8))
            for half in range(2):
                bo = BI if half == 0 else BS
                for c in range(8):
                    P.mm(bo[:, :], yT_d[:, c, :], wob2[half][:, c, :], start=(c == 0), stop=(c == 7))
            i = self.layernorm_out(BI[:, :], BS[:, :], xl, None, None, self.h1[tok, :], lnt_d, "h1out")
            self.finals.append(i)

        self.wgi = 0
        for ti in range(NT + 1):
            a_ops = P.begin_stream()
            if ti < NT:
                emit_A(ti)
            if self.w1_d is not None:
                jobs = [(0, c) for c in range(8)] + [(1, c) for c in range(8)]
                if NT >= 20:
                    todo = [jobs[ti - 2]] if 2 <= ti < 18 else []
                else:
                    todo = jobs if ti == 0 else []
                for (w, c) in todo:
                    src = (self.w_in1 if w == 0 else self.w_out1)[c * 128:(c + 1) * 128, :]
                    self.finals.append(P.add("pool", (lambda e, o=self.w1_d[w][c * 128:(c + 1) * 128, :], i=src:
                                                      e.dma_start(out=o, in_=i)), chan="w1c%d" % (c % 2)))
            P.end_stream()
            d_ops = P.begin_stream()
            if ti >= 1:
                emit_D(ti - 1)
            P.end_stream()
            if "nomerge" in _SKIP:
                P.cur.extend(a_ops)
                P.cur.extend(d_ops)
            elif "propmerge" in _SKIP:
                P.merge(a_ops, d_ops)
            else:
                P.prio_delta = float(_os.environ.get("KDELTA", "1.0"))
                P.hop_x = float(_os.environ.get("KHOPX", "0.7"))
                P.hop_same = float(_os.environ.get("KHOPS", "0.15"))
                P.merge_sched(a_ops, d_ops)

    def phase_b(self):
        P = self.P
        T, NT = self.T, self.NT
        cst, cstb = self.cst, self.cstb
        wbf = P.sb("wbf1", [128, 8, ODD_IN], BF16)
        wob = P.sb("wob1", [128, 8, D], BF16)
        if self.w1_d is not None:
            for c in range(8):
                P.dma(wbf[:, c, :], self.w1_d[0][c * 128:(c + 1) * 128, :], chan="wl%d" % (c % 4), q="sp")
            for c in range(8):
                P.dma(wob[:, c, :], self.w1_d[1][c * 128:(c + 1) * 128, :], chan="wl%d" % (c % 4), q="sp")
        else:
            for c in range(8):
                P.dma(wbf[:, c, :], self.w_in1[c * 128:(c + 1) * 128, :], chan="wl%d" % (c % 2), q="pool")
            for c in range(8):
                P.dma(wob[:, c, :], self.w_out1[c * 128:(c + 1) * 128, :], chan="wl%d" % (c % 2), q="pool")
        bc = P.sb("bc1", [128, NBC1], F32)
        P.dma(bc[:, :], self.bcv1[0].partition_broadcast(128), chan="bc1")
        BCV = lambda n: bc[:, BC1[n][0]:BC1[n][0] + BC1[n][1]]
        g2 = P.sb("g2", [16, 512], F32)
        P.dma(g2[:, :], self.g2, chan="g2")
        Sf = P.sb("Sf", [128, 4, 256], F32)
        Sb = P.sb("Sb", [128, 4, 256], BF16)
        P.memset(Sf[:, :, :], 0.0)
        P.memset(Sb[:, :, :], 0.0)
        xt2 = [P.sb("xt%d" % j, [128, D], F32) for j in range(2)]
        xb = P.sb("xb", [128, D], BF16)
        xT = P.sb("xT", [128, 8, 128], BF16)
        p = P.sb("p", [128, ODD_IN], F32)
        sc = [P.sb("scr%d" % j, [128, 512], F32) for j in range(4)]
        sm = P.sb("sm", [128, 32], F32)
        gdT = P.sb("gdT", [16, 128], F32)
        qe_b = P.sb("qe_b", [128, 512], BF16)
        ke_b = P.sb("ke_b", [128, 512], BF16)
        kd_b2 = [P.sb("kd_b%d" % j, [128, 512], BF16) for j in range(2)]
        v_b2 = [P.sb("v_b%d" % j, [128, 1024], BF16) for j in range(2)]
        qeT2 = [P.sb("qeT%d" % j, [128, 4, 128], BF16) for j in range(2)]
        keT2 = [P.sb("keT%d" % j, [128, 4, 128], BF16) for j in range(2)]
        gS2 = [P.sb("gS%d" % j, [128, 4], F32) for j in range(2)]
        gc2 = [P.sb("gc%d" % j, [128, D], BF16) for j in range(2)]
        attT = [P.sb("attT%d" % j, [128, 128], BF16) for j in range(2)]
        o = P.sb("o", [128, D], F32)
        osq = P.sb("osq", [128, D], F32)
        ycb = P.sb("ycb", [128, D], BF16)
        yT = P.sb("yT", [128, 8, 128], BF16)
        ln_st = P.sb("ln_st", [128, 8], F32)
        B0, B1, B2, B3, B56, B78 = self.B0, self.B1, self.B2, self.B3, self.B56, self.B78
        B78b = B78[:, 0:512].bitcast(BF16)
        groups = [(0, 512), (512, 1024), (1024, 1536), (1536, 2048), (2048, 2064), (2064, 2576), (2576, 3088)]
        H4 = lambda v, h=4: v.re("p (h d) -> p h d", h=h)

        def emit_B1(ti):
            tok = slice(ti * 128, (ti + 1) * 128)
            xt = xt2[ti % 2]
            kd_b, v_b, qeT, keT, gS, gc = kd_b2[ti % 2], v_b2[ti % 2], qeT2[ti % 2], keT2[ti % 2], gS2[ti % 2], gc2[ti % 2]
            P.dma(xt[:, :], self.h1[tok, :], chan="xin")
            P.tt(xt[:, :], xt[:, :], BCV("lng0"), ALU.mult)
            P.tt(xt[:, :], xt[:, :], BCV("lnb0"), ALU.add, eng="pool")
            P.copy(xb[:, :], xt[:, :], eng="act")
            for c in range(8):
                P.tr(B2[:, c * 128:(c + 1) * 128], xb[:, c * 128:(c + 1) * 128], self.ident(128, True))
            P.copy(xT[:, :, :], B2[:, :].re("p (c t) -> p c t", c=8))
            for (c0, c1) in groups:
                bk = self.rbank()
                for c in range(8):
                    P.mm(bk[:, 0:c1 - c0], xT[:, c, :], wbf[:, c, c0:c1], start=(c == 0), stop=(c == 7))
                P.copy(p[:, c0:c1], bk[:, 0:c1 - c0], eng="act")
            q_, k_, v_, gd_, gcc = p[:, 0:512], p[:, 512:1024], p[:, 1024:2048], p[:, 2048:2064], p[:, 2064:3088]
            bt_ = self.rbank()
            P.tr(bt_[0:16, 0:128], gd_, self.ident(128))
            P.copy(gdT[:, :], bt_[0:16, 0:128])
            bz = self.rbank()
            P.mm(bz[:, :], gdT[:, :], g2[:, :])
            la = sc[0]
            P.tt(la[:, :], bz[:, :], BCV("gbias"), ALU.add)
            P.act(la[:, :], la[:, :], AF.Tanh, scale=0.5)
            P.ts(la[:, :], la[:, :], 0.5, ALU.mult, 0.5, ALU.add)
            P.act(la[:, :], la[:, :], AF.Ln)
            P.ts(la[:, :], la[:, :], 1.0 / 16.0, ALU.mult)
            bcs = self.rbank()
            P.mm(bcs[:, :], cst[:, C_TRIU + 128:C_TRIU + 256], la[:, :])
            cum = sc[1]
            P.copy(cum[:, :], bcs[:, :])
            bl = self.rbank()
            P.mm(bl[:, :], cst[:, C_ONES:C_ONES + 128], la[:, :])
            dlt = sc[2]
            P.tt(dlt[:, :], bl[:, :], cum[:, :], ALU.subtract)
            P.act(dlt[:, :], dlt[:, :], AF.Exp)
            P.tt(kd_b[:, :], k_, dlt[:, :], ALU.mult)
            bg = self.rbank()
            for h in range(4):
                P.mm(bg[:, 256 + h:257 + h], la[:, h * 128:(h + 1) * 128], cst[:, C_ONES:C_ONES + 1])
            P.act(gS[:, :], bg[:, 256:260], AF.Exp)
            eb = sc[3]
            P.act(eb[:, :], cum[:, :], AF.Exp)
            P.stt(qe_b[:, :], q_, float(128.0 ** -0.5), eb[:, :], ALU.mult, ALU.mult)
            P.act(eb[:, :], cum[:, :], AF.Exp, scale=-1.0)
            P.tt(ke_b[:, :], k_, eb[:, :], ALU.mult)
            P.copy(v_b[:, :], v_, eng="act")
            P.act(gc[:, :], gcc, AF.Tanh, scale=0.5)
            P.stt(gc[:, :], gc[:, :], 1.0, gcc, ALU.add, ALU.mult)
            for h in range(4):
                P.tr(B2[:, h * 128:(h + 1) * 128], qe_b[:, h * 128:(h + 1) * 128], self.ident(128, True))
                P.tr(B2[:, 512 + h * 128:512 + (h + 1) * 128], ke_b[:, h * 128:(h + 1) * 128], self.ident(128, True))
            P.copy(qeT[:, :, :], B2[:, 0:512].re("p (h t) -> p h t", h=4))
            P.copy(keT[:, :, :], B2[:, 512:1024].re("p (h t) -> p h t", h=4), eng="act")

        def emit_B2(ti):
            tok = slice(ti * 128, (ti + 1) * 128)
            xt = xt2[ti % 2]
            kd_b, v_b, qeT, keT, gS, gc = kd_b2[ti % 2], v_b2[ti % 2], qeT2[ti % 2], keT2[ti % 2], gS2[ti % 2], gc2[ti % 2]
            for h in range(4):
                P.mm(B3[:, 0:128], keT[:, h, :], qeT[:, h, :])
                at = attT[h % 2]
                P.tt(at[:, :], B3[:, 0:128], cst[:, C_TRIU + 128:C_TRIU + 256], ALU.mult)
                vs = slice(h * 256, (h + 1) * 256)
                P.mm(B78[:, vs], at[:, :], v_b[:, vs], start=True, stop=False)
                P.mm(B78[:, vs], qeT[:, h, :], Sb[:, h, :], start=False, stop=True)
                P.mm(B56[:, vs], kd_b[:, h * 128:(h + 1) * 128], v_b[:, vs], start=True, stop=True)
            P.copy(o[:, :], B78[:, :], eng="act")
            P.tt(Sf[:, :, :], Sf[:, :, :], gS[:, :, None].bc([128, 4, 256]), ALU.mult)
            P.tt(Sf[:, :, :], Sf[:, :, :], B56[:, :].re("p (h v) -> p h v", h=4), ALU.add)
            P.copy(Sb[:, :, :], Sf[:, :, :], eng="act")
            if ti == self.NT - 1:
                self.dump("gla_o", o[:, :], [128, 1024])
            P.act(osq[:, :], o[:, :], AF.Square)
            P.reduce(sm[:, 0:4], H4(osq[:, :]), ALU.add)
            P.ts(sm[:, 0:4], sm[:, 0:4], 1.0 / 256, ALU.mult, 1e-6, ALU.add)
            P.tt(sm[:, 8:12], sm[:, 0:4], cst[:, C_NH:C_NH + 4], ALU.pow, eng="pool")
            P.tt(H4(o[:, :]), H4(o[:, :]), sm[:, 8:12, None].bc([128, 4, 256]), ALU.mult)
            P.tt(H4(o[:, :]), H4(o[:, :]), BCV("normg")[:, None, :].bc([128, 4, 256]), ALU.mult, eng="pool")
            P.stt(ycb[:, :], o[:, :], 0.5, gc[:, :], ALU.mult, ALU.mult)
            for c in range(8):
                P.tr(B78b[:, c * 128:(c + 1) * 128], ycb[:, c * 128:(c + 1) * 128], self.ident(128, True))
            P.copy(yT[:, :, :], B78b[:, :].re("p (c t) -> p c t", c=8))
            for half in range(2):
                for c in range(8):
                    P.mm(B56[:, half * 512:(half + 1) * 512], yT[:, c, :], wob[:, c, half * 512:(half + 1) * 512],
                         start=(c == 0), stop=(c == 7))
            lnt = {"r": xt[:, :], "st": ln_st, "sq": osq[:, :]}
            i = self.layernorm_out(B56[:, 0:512], B56[:, 512:1024], xt, BCV("lng"), BCV("lnb"), self.out[tok, :], lnt, "yout")
            self.finals.append(i)

        for ti in range(NT + 1):
            a_ops = P.begin_stream()
            if ti < NT:
                emit_B1(ti)
            P.end_stream()
            d_ops = P.begin_stream()
            if ti >= 1:
                emit_B2(ti - 1)
            P.end_stream()
            if "nomergeb" in _SKIP:
                P.cur.extend(a_ops)
                P.cur.extend(d_ops)
            else:
                P.prio_delta = float(_os.environ.get("KDELTAB", "-1.0"))
                P.merge_sched(a_ops, d_ops)


BC1 = {}
_o = 0
for _n, _l in (("gbias", 512), ("normg", 256), ("lng", 1024), ("lnb", 1024), ("lng0", 1024), ("lnb0", 1024)):
    BC1[_n] = (_o, _l)
    _o += _l
NBC1 = _o


def prep_inputs(inp, b, T):
    f = lambda a: np.ascontiguousarray(np.asarray(a, dtype=np.float32))
    bcv0 = np.concatenate([f(inp["a_mu"][0]), f(inp["a_w0"][0]), f(inp["a_a0"][0]), f(inp["a_kk_scale"][0]),
                           f(inp["a_ka"][0]), f(inp["a_rk"][0]).reshape(-1), f(inp["a_gn_g"][0]), f(inp["a_gn_b"][0]),
                           f(inp["b_kv_norm_g"][0])])[None, :]
    bcv1 = np.concatenate([f(inp["c_g_bias"][0]), f(inp["c_norm_g"][0]), f(inp["ln_g"][1]), f(inp["ln_b"][1]),
                           f(inp["ln_g"][0]), f(inp["ln_b"][0])])[None, :]
    pos = np.ascontiguousarray(np.asarray(inp["positions"][b], dtype=np.int32).reshape(T // 128, 128).T)
    return {
        "x": f(inp["x"][b]),
        "pos": pos,
        "consts": make_consts(),
        "w_in0": f(inp["e_w_in"][0]),
        "w_out0": f(inp["e_w_out"][0]),
        "bcv0": f(bcv0),
        "w2a2": f(np.concatenate([inp["a_w2"][0], inp["a_a2"][0]], axis=1)),
        "wukv": f(np.concatenate([inp["b_wuk"][0], inp["b_wuv"][0]], axis=1)),
        "w_in1": f(inp["o_w_in"][0]),
        "w_out1": f(inp["o_w_out"][0]),
        "bcv1": f(bcv1),
        "g2": f(inp["c_g2"][0]),
    }


_T = 4096
_NB = 8


def kernel(**inputs):
    T = _T
    kb = K(T, min(256, T // 4))
    nc = kb.build()
    in_maps = [prep_inputs(inputs, b, T) for b in range(_NB)]
    res = run_bass_kernel_spmd(nc, in_maps, core_ids=list(range(_NB)))
    out = np.stack([np.asarray(r["out"], dtype=np.float32) for r in res.results], axis=0)
    return out
```
